# Optimizing a Trainium2 kernel written in Bass

```python
import math
import jax, jax.numpy as jnp
from jax import lax
import numpy as np

D_MODEL = 2048
BATCH = 16
SEQ = 2048
DEPTH = 1
DEC_BATCH = 16
DEC_SEQ = 16
PAST_LEN = 2048

CHUNK = 64
MIX_WIDTH = D_MODEL
SSD_WIDTH = MIX_WIDTH // 2
SSD_HEADDIM = 64
SSD_HEADS = SSD_WIDTH // SSD_HEADDIM
SSD_GROUPS = 4
SSD_STATE = 128
CONV_W = 4
CONV_DIM = SSD_WIDTH + 2 * SSD_GROUPS * SSD_STATE
S5_WIDTH = MIX_WIDTH - SSD_WIDTH
S5_GROUP_CH = 16
S5_GROUPS = S5_WIDTH // S5_GROUP_CH
S5_STATE = 64
D_FF = 5504
IN_PROJ_DIM = SSD_WIDTH + CONV_DIM + SSD_HEADS + S5_WIDTH
EPS = 1e-6

kernel_name = 'hybrid_ssd_s5_streaming_encoder'


def _rmsnorm(x, g):
    xf = x.astype(jnp.float32)
    xf = xf * lax.rsqrt(jnp.mean(xf * xf, axis=-1, keepdims=True) + EPS)
    return (xf * g.astype(jnp.float32)).astype(x.dtype)


def _swiglu(x, w_gate, w_up, w_down):
    return (jax.nn.silu(x @ w_gate) * (x @ w_up)) @ w_down


def _causal_conv(xbc, buf, w, b):
    seq = xbc.shape[1]
    padded = jnp.concatenate([buf.astype(xbc.dtype), xbc], axis=1)
    out = b + padded[:, 0:seq] * w[0]
    for k in range(1, CONV_W):
        out = out + padded[:, k:k + seq] * w[k]
    return jax.nn.silu(out), padded[:, seq:]


def _ssd(x, dt, a, bmat, cmat, h0):
    f32 = jnp.float32
    bsz, seq = x.shape[0], x.shape[1]
    q = min(CHUNK, seq)
    nc = seq // q
    r = SSD_HEADS // SSD_GROUPS
    xc = x.astype(f32).reshape(bsz, nc, q, SSD_GROUPS, r, SSD_HEADDIM)
    dtc = dt.astype(f32).reshape(bsz, nc, q, SSD_GROUPS, r)
    bc = bmat.astype(f32).reshape(bsz, nc, q, SSD_GROUPS, SSD_STATE)
    cc = cmat.astype(f32).reshape(bsz, nc, q, SSD_GROUPS, SSD_STATE)
    acs = jnp.cumsum(dtc * a.astype(f32).reshape(SSD_GROUPS, r), axis=2)
    seg = acs[:, :, :, None] - acs[:, :, None, :]
    causal = jnp.tril(jnp.ones((q, q), dtype=bool))[:, :, None, None]
    lmat = jnp.exp(jnp.where(causal, seg, -jnp.inf))
    xdt = xc * dtc[..., None]
    cb = jnp.einsum('bcign,bcjgn->bcijg', cc, bc)
    y_diag = jnp.einsum('bcijg,bcijgr,bcjgrp->bcigrp', cb, lmat, xdt)
    decay_end = jnp.exp(acs[:, :, -1:] - acs)
    chunk_states = jnp.einsum('bcjgn,bcjgr,bcjgrp->bcgrpn', bc, decay_end, xdt)
    chunk_decay = jnp.exp(acs[:, :, -1])

    def step(h, inp):
        dec, st = inp
        return dec[..., None, None] * h + st, h

    h_init = h0.astype(f32).reshape(bsz, SSD_GROUPS, r, SSD_HEADDIM, SSD_STATE)
    h_last, h_prev = lax.scan(step, h_init,
                              (jnp.moveaxis(chunk_decay, 1, 0), jnp.moveaxis(chunk_states, 1, 0)))
    h_prev = jnp.moveaxis(h_prev, 0, 1)
    y_off = jnp.einsum('bcign,bcgrpn,bcigr->bcigrp', cc, h_prev, jnp.exp(acs))
    y = (y_diag + y_off).reshape(bsz, seq, SSD_HEADS, SSD_HEADDIM)
    return y.astype(x.dtype), h_last.reshape(h0.shape).astype(h0.dtype)


def _s5(u, h0_re, h0_im, lam_re, lam_im, log_step, b_re, b_im, c_re, c_im, d):
    f32 = jnp.float32
    bsz, seq = u.shape[0], u.shape[1]
    ug = u.astype(f32).reshape(bsz, seq, S5_GROUPS, S5_GROUP_CH)
    lam_re = lam_re.astype(f32)
    lam_im = lam_im.astype(f32)
    step = jnp.exp(log_step.astype(f32))[:, None]
    mag = jnp.exp(lam_re * step)
    ang = lam_im * step
    lb_re = mag * jnp.cos(ang)
    lb_im = mag * jnp.sin(ang)
    den = lam_re * lam_re + lam_im * lam_im
    nr = lb_re - 1.0
    q_re = (nr * lam_re + lb_im * lam_im) / den
    q_im = (lb_im * lam_re - nr * lam_im) / den
    b_re = b_re.astype(f32)
    b_im = b_im.astype(f32)
    bb_re = q_re[..., None] * b_re - q_im[..., None] * b_im
    bb_im = q_re[..., None] * b_im + q_im[..., None] * b_re
    bu_re = jnp.einsum('blgh,gnh->blgn', ug, bb_re)
    bu_im = jnp.einsum('blgh,gnh->blgn', ug, bb_im)
    h0_re = h0_re.astype(f32)
    h0_im = h0_im.astype(f32)
    bu_re = bu_re.at[:, 0].add(lb_re * h0_re - lb_im * h0_im)
    bu_im = bu_im.at[:, 0].add(lb_re * h0_im + lb_im * h0_re)
    a_re = jnp.broadcast_to(lb_re[None, None], (1, seq, S5_GROUPS, S5_STATE))
    a_im = jnp.broadcast_to(lb_im[None, None], (1, seq, S5_GROUPS, S5_STATE))

    def combine(e1, e2):
        a1r, a1i, b1r, b1i = e1
        a2r, a2i, b2r, b2i = e2
        return (a1r * a2r - a1i * a2i, a1r * a2i + a1i * a2r,
                a2r * b1r - a2i * b1i + b2r, a2r * b1i + a2i * b1r + b2i)

    _, _, hr, hi = lax.associative_scan(combine, (a_re, a_im, bu_re, bu_im), axis=1)
    y = (jnp.einsum('blgn,ghn->blgh', hr, c_re.astype(f32))
         - jnp.einsum('blgn,ghn->blgh', hi, c_im.astype(f32))
         + d.astype(f32) * ug)
    return (y.reshape(bsz, seq, S5_WIDTH).astype(u.dtype),
            hr[:, -1].astype(h0_re.dtype), hi[:, -1].astype(h0_im.dtype))


def _mixer(u, conv_buf, h_ssd, h5_re, h5_im, p):
    bsz, seq = u.shape[0], u.shape[1]
    proj = u @ p['w_in']
    z, xbc, dt_raw, u5 = jnp.split(
        proj, [SSD_WIDTH, SSD_WIDTH + CONV_DIM, SSD_WIDTH + CONV_DIM + SSD_HEADS], axis=-1)
    xbc, new_conv = _causal_conv(xbc, conv_buf, p['conv_w'], p['conv_b'])
    xs, bm, cm = jnp.split(xbc, [SSD_WIDTH, SSD_WIDTH + SSD_GROUPS * SSD_STATE], axis=-1)
    xs = xs.reshape(bsz, seq, SSD_HEADS, SSD_HEADDIM)
    dt = jax.nn.softplus(dt_raw + p['dt_bias'])
    a = -jnp.exp(p['a_log'])
    y_ssd, new_h = _ssd(xs, dt, a,
                        bm.reshape(bsz, seq, SSD_GROUPS, SSD_STATE),
                        cm.reshape(bsz, seq, SSD_GROUPS, SSD_STATE), h_ssd)
    y_ssd = (y_ssd + p['d_ssd'][:, None] * xs).reshape(bsz, seq, SSD_WIDTH)
    y_ssd = _rmsnorm(y_ssd * jax.nn.silu(z), p['norm_ssd'])
    y5, new_re, new_im = _s5(u5, h5_re, h5_im, p['s5_lambda_re'], p['s5_lambda_im'],
                             p['s5_log_step'], p['s5_b_re'], p['s5_b_im'],
                             p['s5_c_re'], p['s5_c_im'], p['s5_d'])
    g = jax.nn.gelu(y5, approximate=False)
    y5 = _rmsnorm(g * jax.nn.sigmoid(g @ p['w_glu'] + p['b_glu']), p['norm_s5'])
    out = jnp.concatenate([y_ssd, y5], axis=-1) @ p['w_out']
    return out, new_conv, new_h, new_re, new_im


def _layer(x, conv_buf, h_ssd, h5_re, h5_im, p):
    x = x + 0.5 * _swiglu(_rmsnorm(x, p['norm_ffn1']), p['w_ffn1_gate'], p['w_ffn1_up'], p['w_ffn1_down'])
    m, new_conv, new_h, new_re, new_im = _mixer(_rmsnorm(x, p['norm_mix']), conv_buf, h_ssd, h5_re, h5_im, p)
    x = x + m
    x = x + 0.5 * _swiglu(_rmsnorm(x, p['norm_ffn2']), p['w_ffn2_gate'], p['w_ffn2_up'], p['w_ffn2_down'])
    return x, new_conv, new_h, new_re, new_im


def _trunk(x, conv_buf, h_ssd, h5_re, h5_im, params, norm_final):
    convs, hs, res, ims = [], [], [], []
    for l in range(DEPTH):
        p = {k: v[l] for k, v in params.items()}
        x, c, h, r, i = _layer(x, conv_buf[l], h_ssd[l], h5_re[l], h5_im[l], p)
        convs.append(c)
        hs.append(h)
        res.append(r)
        ims.append(i)
    return (_rmsnorm(x, norm_final), jnp.stack(convs), jnp.stack(hs),
            jnp.stack(res), jnp.stack(ims))


def setup_inputs(seed: int = 0) -> dict:
    key = jax.random.key(seed)
    ks = jax.random.split(key, 40)
    f32 = jnp.float32

    def nrm(k, shape, scale):
        return scale * jax.random.normal(k, shape, f32)

    def gain(k, shape):
        return 1.0 + 0.01 * jax.random.normal(k, shape, f32)

    L = DEPTH
    dt0 = jnp.exp(jax.random.uniform(ks[10], (L, SSD_HEADS), f32, math.log(1e-3), math.log(1e-1)))
    lam_im = math.pi * jnp.arange(S5_STATE, dtype=f32)
    return {
        'x_prompt': nrm(ks[0], (BATCH, SEQ, D_MODEL), 1.0),
        'x_sample': nrm(ks[1], (DEC_BATCH, DEC_SEQ, D_MODEL), 1.0),
        'cache_conv': nrm(ks[2], (L, DEC_BATCH, CONV_W - 1, CONV_DIM), 1.0),
        'state_ssd': nrm(ks[3], (L, DEC_BATCH, SSD_HEADS, SSD_HEADDIM, SSD_STATE), 0.1),
        'state_s5_re': nrm(ks[4], (L, DEC_BATCH, S5_GROUPS, S5_STATE), 0.5),
        'state_s5_im': nrm(ks[5], (L, DEC_BATCH, S5_GROUPS, S5_STATE), 0.5),
        'norm_ffn1': gain(ks[6], (L, D_MODEL)),
        'w_ffn1_gate': nrm(ks[7], (L, D_MODEL, D_FF), D_MODEL ** -0.5),
        'w_ffn1_up': nrm(ks[8], (L, D_MODEL, D_FF), D_MODEL ** -0.5),
        'w_ffn1_down': nrm(ks[9], (L, D_FF, D_MODEL), D_FF ** -0.5),
        'norm_mix': gain(ks[11], (L, D_MODEL)),
        'w_in': nrm(ks[12], (L, D_MODEL, IN_PROJ_DIM), D_MODEL ** -0.5),
        'conv_w': nrm(ks[13], (L, CONV_W, CONV_DIM), CONV_W ** -0.5),
        'conv_b': nrm(ks[14], (L, CONV_DIM), 0.01),
        'dt_bias': dt0 + jnp.log(-jnp.expm1(-dt0)),
        'a_log': jnp.log(jax.random.uniform(ks[15], (L, SSD_HEADS), f32, 1.0, 16.0)),
        'd_ssd': gain(ks[16], (L, SSD_HEADS)),
        'norm_ssd': gain(ks[17], (L, SSD_WIDTH)),
        's5_lambda_re': -0.5 + nrm(ks[18], (L, S5_GROUPS, S5_STATE), 0.01),
        's5_lambda_im': lam_im + nrm(ks[19], (L, S5_GROUPS, S5_STATE), 0.01),
        's5_log_step': jax.random.uniform(ks[20], (L, S5_GROUPS), f32, math.log(1e-3), math.log(1e-1)),
        's5_b_re': nrm(ks[21], (L, S5_GROUPS, S5_STATE, S5_GROUP_CH), (2 * S5_GROUP_CH) ** -0.5),
        's5_b_im': nrm(ks[22], (L, S5_GROUPS, S5_STATE, S5_GROUP_CH), (2 * S5_GROUP_CH) ** -0.5),
        's5_c_re': nrm(ks[23], (L, S5_GROUPS, S5_GROUP_CH, S5_STATE), S5_STATE ** -0.5),
        's5_c_im': nrm(ks[24], (L, S5_GROUPS, S5_GROUP_CH, S5_STATE), S5_STATE ** -0.5),
        's5_d': nrm(ks[25], (L, S5_GROUPS, S5_GROUP_CH), 1.0),
        'w_glu': nrm(ks[26], (L, S5_WIDTH, S5_WIDTH), S5_WIDTH ** -0.5),
        'b_glu': nrm(ks[27], (L, S5_WIDTH), 0.01),
        'norm_s5': gain(ks[28], (L, S5_WIDTH)),
        'w_out': nrm(ks[29], (L, MIX_WIDTH, D_MODEL), MIX_WIDTH ** -0.5),
        'norm_ffn2': gain(ks[30], (L, D_MODEL)),
        'w_ffn2_gate': nrm(ks[31], (L, D_MODEL, D_FF), D_MODEL ** -0.5),
        'w_ffn2_up': nrm(ks[32], (L, D_MODEL, D_FF), D_MODEL ** -0.5),
        'w_ffn2_down': nrm(ks[33], (L, D_FF, D_MODEL), D_FF ** -0.5),
        'norm_final': gain(ks[34], (D_MODEL,)),
    }


def reference(x_prompt, x_sample, cache_conv, state_ssd, state_s5_re, state_s5_im,
              norm_ffn1, w_ffn1_gate, w_ffn1_up, w_ffn1_down, norm_mix, w_in,
              conv_w, conv_b, dt_bias, a_log, d_ssd, norm_ssd,
              s5_lambda_re, s5_lambda_im, s5_log_step, s5_b_re, s5_b_im,
              s5_c_re, s5_c_im, s5_d, w_glu, b_glu, norm_s5, w_out,
              norm_ffn2, w_ffn2_gate, w_ffn2_up, w_ffn2_down, norm_final):
    params = {
        'norm_ffn1': norm_ffn1, 'w_ffn1_gate': w_ffn1_gate, 'w_ffn1_up': w_ffn1_up,
        'w_ffn1_down': w_ffn1_down, 'norm_mix': norm_mix, 'w_in': w_in,
        'conv_w': conv_w, 'conv_b': conv_b, 'dt_bias': dt_bias, 'a_log': a_log,
        'd_ssd': d_ssd, 'norm_ssd': norm_ssd, 's5_lambda_re': s5_lambda_re,
        's5_lambda_im': s5_lambda_im, 's5_log_step': s5_log_step, 's5_b_re': s5_b_re,
        's5_b_im': s5_b_im, 's5_c_re': s5_c_re, 's5_c_im': s5_c_im, 's5_d': s5_d,
        'w_glu': w_glu, 'b_glu': b_glu, 'norm_s5': norm_s5, 'w_out': w_out,
        'norm_ffn2': norm_ffn2, 'w_ffn2_gate': w_ffn2_gate, 'w_ffn2_up': w_ffn2_up,
        'w_ffn2_down': w_ffn2_down,
    }
    bsz = x_prompt.shape[0]
    zero_conv = jnp.zeros((DEPTH, bsz, CONV_W - 1, CONV_DIM), x_prompt.dtype)
    zero_ssd = jnp.zeros((DEPTH, bsz, SSD_HEADS, SSD_HEADDIM, SSD_STATE), state_ssd.dtype)
    zero_s5_re = jnp.zeros((DEPTH, bsz, S5_GROUPS, S5_STATE), state_s5_re.dtype)
    zero_s5_im = jnp.zeros((DEPTH, bsz, S5_GROUPS, S5_STATE), state_s5_im.dtype)
    y_prompt, conv_p, ssd_p, s5re_p, s5im_p = _trunk(
        x_prompt, zero_conv, zero_ssd, zero_s5_re, zero_s5_im, params, norm_final)
    y_sample, conv_s, ssd_s, s5re_s, s5im_s = _trunk(
        x_sample, cache_conv, state_ssd, state_s5_re, state_s5_im, params, norm_final)
    return (y_prompt, y_sample, conv_p, ssd_p, s5re_p, s5im_p, conv_s, ssd_s, s5re_s, s5im_s)
```

```python
import numpy as np
import concourse.bass as bass
import concourse.mybir as mybir
from concourse.bass_utils import run_bass_kernel_spmd

F32 = mybir.dt.float32
BF16 = mybir.dt.bfloat16
I32 = mybir.dt.int32
ALU = mybir.AluOpType
AF = mybir.ActivationFunctionType
ENGS = ['pe', 'act', 'dve', 'pool', 'sp']
NDS = 12
BLK = 512
EPS = 1e-6
TWO_PI = 6.283185307179586


def _dtsize(dt):
    return 4 if dt in (F32, I32) else 2


def _prod(s):
    r = 1
    for v in s:
        r *= v
    return r


class Buf:
    def __init__(s, root, space, off, nbytes, dt, shape):
        s.root, s.space, s.off, s.nbytes, s.dt, s.shape = root, space, off, nbytes, dt, tuple(shape)
        blk = 2048 if space == 'ps' else BLK
        s.blocks = [(space, b) for b in range(off // blk, (off + nbytes - 1) // blk + 1)]
        s._ap = None

    @property
    def ap(s):
        if s._ap is None:
            a = s.root[:, s.off // 4:(s.off + s.nbytes) // 4]
            if s.dt != F32:
                a = a.bitcast(s.dt)
            if len(s.shape) > 1:
                names = ' '.join('d%d' % i for i in range(len(s.shape)))
                a = a.rearrange('p (%s) -> p %s' % (names, names),
                                **{'d%d' % i: s.shape[i] for i in range(len(s.shape))})
            s._ap = a
        return s._ap

    def __getitem__(s, i):
        inner = _prod(s.shape[1:]) * _dtsize(s.dt)
        return Buf(s.root, s.space, s.off + i * inner, inner, s.dt, s.shape[1:] or (1,))

    def view(s, dt, shape):
        assert _prod(shape) * _dtsize(dt) <= s.nbytes
        return Buf(s.root, s.space, s.off, _prod(shape) * _dtsize(dt), dt, shape)


class Arena:
    def __init__(s, root, space, size):
        s.root, s.space, s.size, s.top = root, space, size, 0

    def alloc(s, shape, dt, align=BLK):
        off = (s.top + align - 1) // align * align
        nb = _prod(shape) * _dtsize(dt)
        nb4 = (nb + 3) // 4 * 4
        s.top = off + nb4
        assert s.top <= s.size, "arena %s overflow: %d > %d" % (s.space, s.top, s.size)
        return Buf(s.root, s.space, off, nb4, dt, shape)

    def at(s, off, shape, dt):
        nb = _prod(shape) * _dtsize(dt)
        assert off + nb <= s.size
        return Buf(s.root, s.space, off, nb, dt, shape)


class _Stop(Exception):
    pass


class DRes:
    def __init__(s, name, i=0):
        s.blocks = [('dr', name, i)]


class Op:
    __slots__ = ('fn', 'waits', 'signal', 'cnt', 'dma')

    def __init__(s, fn, dma):
        s.fn, s.waits, s.signal, s.cnt, s.dma = fn, [], False, 0, dma


class Sched:
    def __init__(s):
        s.ops = {e: [] for e in ENGS}
        s.blocks = {}
        s.seen = {e: {} for e in ENGS}
        s.ndma = {'sp': 0, 'pool': 0}
        s.dval = {}

    def op(s, eng, fn, r=(), w=(), dma=False):
        idx = len(s.ops[eng])
        o = Op(fn, None)
        deps = {}

        def add(tok):
            k = tok[:-1]
            if deps.get(k, -1) < tok[-1]:
                deps[k] = tok[-1]
        for x in r:
            for b in x.blocks:
                st = s.blocks.get(b)
                if st and st[0]:
                    add(st[0])
        for x in w:
            for b in x.blocks:
                st = s.blocks.get(b)
                if st:
                    if st[0]:
                        add(st[0])
                    for t in st[1].values():
                        add(t)
        seen = s.seen[eng]
        if dma:
            i = s.ndma[eng] % NDS
            s.ndma[eng] += 1
            val = s.dval.get((eng, i), 0) + 16
            s.dval[(eng, i)] = val
            if val > 16:
                add(('d', eng, i, val - 16))
            tok = ('d', eng, i, val)
            o.dma = (eng, i)
        else:
            tok = ('e', eng, idx)
        for k, v in deps.items():
            if k[0] == 'e' and k[1] == eng:
                if eng in ('pe', 'sp'):
                    continue
                if v < idx - 4:
                    continue
            if seen.get(k, -1) >= v:
                continue
            seen[k] = v
            if k[0] == 'e':
                s.ops[k[1]][v].signal = True
            o.waits.append((k, v))
        s.ops[eng].append(o)
        key = tok[:-1]
        for x in r:
            for b in x.blocks:
                st = s.blocks.setdefault(b, [None, {}])
                st[1][key] = tok
        for x in w:
            for b in x.blocks:
                s.blocks[b] = [tok, {}]
        return tok

    def emit(s, nc, block, esem, dsem):
        for e in ENGS:
            c = 0
            for o in s.ops[e]:
                if o.signal:
                    c += 1
                o.cnt = c

        def run(eng_name):
            def body(e):
                for o in s.ops[eng_name]:
                    for k, v in o.waits:
                        if k[0] == 'e':
                            e.wait_ge(esem[k[1]], s.ops[k[1]][v].cnt)
                        else:
                            e.wait_ge(dsem[(k[1], k[2])], v)
                    ins = o.fn(e)
                    if o.dma:
                        ins.then_inc(dsem[o.dma], 16)
                    elif o.signal:
                        ins.then_inc(esem[eng_name], 1)
                if eng_name == 'sp':
                    for (q, i), v in s.dval.items():
                        e.wait_ge(dsem[(q, i)], v)
            return body
        block.tensor(run('pe'))
        block.scalar(run('act'))
        block.vector(run('dve'))
        block.gpsimd(run('pool'))
        block.sync(run('sp'))


NFC = 43
WD_GROUPS = [(0, 8), (8, 8), (16, 8), (24, 8), (32, 8), (40, 3)]
NSLOT = 8

INPUT_SHAPES = {
    'wgu1': [43, 128, 2, 16, 128], 'wd1': [4, 43, 128, 512],
    'wgu2': [43, 128, 2, 16, 128], 'wd2': [4, 43, 128, 512],
    'win': [32, 128, 16, 128], 'wdt': [128, 16, 16], 'wout': [16, 128, 16, 128], 'wglu': [2, 128, 4, 1024],
    'g1c': [128, 16], 'gmc': [128, 16], 'g2c': [128, 16], 'gfc': [128, 16],
    'gssd': [128, 8], 'gs5': [128, 8], 'convw': [128, 16, 4], 'convb': [128, 16],
    'dssd': [128, 8], 'bglu': [128, 8], 'd5': [128, 8],
    'dtb': [1, 16], 'alog': [1, 16], 'tau': [1, 128],
    'lrc': [128, 32], 'lic': [128, 32], 'lsc': [128, 32],
    'wbre': [128, 32, 128], 'wbim': [128, 32, 128], 'wcre': [128, 32, 128], 'wcim': [128, 32, 128],
}


def build(cfg):
    NSEQ, LP, NS = cfg['nseq'], cfg['lp'], cfg['ns']
    LS = 16
    nc = bass.Bass("TRN2", target_bir_lowering=False)

    def din(name, shape):
        return nc.dram_tensor(name, list(shape), F32, kind="ExternalInput").ap()

    def dout(name, shape):
        return nc.dram_tensor(name, list(shape), F32, kind="ExternalOutput").ap()

    def dscr(name, shape, dt):
        return nc.dram_tensor(name, list(shape), dt, kind="Internal").ap()

    I = {k: din(k, v) for k, v in INPUT_SHAPES.items()}
    I['xp'] = din('xp', [NSEQ * LP, 2048])
    O = {'yp': dout('yp', [NSEQ * LP, 2048]), 'convp': dout('convp', [NSEQ, 3, 2048]),
         'ssdp': dout('ssdp', [NSEQ, 1024, 128]), 's5rep': dout('s5rep', [NSEQ, 32, 128]),
         's5imp': dout('s5imp', [NSEQ, 32, 128])}
    if NS:
        I['xs'] = din('xs', [NS * LS, 2048])
        I['cconv'] = din('cconv', [NS, 3, 2048])
        I['sssd'] = din('sssd', [NS, 1024, 128])
        I['s5re'] = din('s5re', [NS, 32, 128])
        I['s5im'] = din('s5im', [NS, 32, 128])
        O.update({'ys': dout('ys', [NS * LS, 2048]), 'convs': dout('convs', [NS, 3, 2048]),
                  'ssds': dout('ssds', [NS, 1024, 128]), 's5res': dout('s5res', [NS, 32, 128]),
                  's5ims': dout('s5ims', [NS, 32, 128])})
    Wb = {k: dscr(k + '_b', INPUT_SHAPES[k], BF16) for k in ['wgu1', 'wd1', 'wgu2', 'wd2', 'win', 'wout', 'wglu']}
    Wb['wbbre'] = dscr('wbbre_b', [128, 32, 128], BF16)
    Wb['wbbim'] = dscr('wbbim_b', [128, 32, 128], BF16)
    Wb['wcre'] = dscr('wcre_b', [128, 32, 128], BF16)
    Wb['wcimn'] = dscr('wcimn_b', [128, 32, 128], BF16)
    Wb['wcren'] = dscr('wcren_b', [128, 32, 128], BF16)
    Wb['tcos'] = dscr('tcos', [128, 32, 128], BF16)
    Wb['tsin'] = dscr('tsin', [128, 32, 128], BF16)

    S = Sched()
    SB_SIZE = 207 * 1024
    STG = 16 * 1024
    ctx = {}

    def ckpt(stage):
        if cfg.get('upto') == stage:
            raise _Stop()

    def program(sb_root, ps_root):
        sb = Arena(sb_root, 'sb', SB_SIZE)
        ps = Arena(ps_root, 'ps', 16384)

        def pbank(bank, shape, dt=F32, coloff=0):
            return ps.at(bank * 2048 + coloff, shape, dt)

        def tt(eng, out, in0, in1, op, r, w):
            S.op(eng, lambda e: e.tensor_tensor(out=out, in0=in0, in1=in1, op=op), r, w)

        def stt(eng, out, in0, scalar, in1, op0, op1, r, w):
            S.op(eng, lambda e: e.scalar_tensor_tensor(out=out, in0=in0, scalar=scalar, in1=in1, op0=op0, op1=op1), r, w)

        def ts(eng, out, in0, s1, s2, op0, op1, r, w):
            if s2 is None:
                S.op(eng, lambda e: e.tensor_scalar(out=out, in0=in0, scalar1=s1, scalar2=None, op0=op0), r, w)
            else:
                S.op(eng, lambda e: e.tensor_scalar(out=out, in0=in0, scalar1=s1, scalar2=s2, op0=op0, op1=op1), r, w)

        def act(out, in_, func, r, w, bias=None, scale=None):
            kw = {}
            if bias is not None:
                kw['bias'] = bias
            if scale is not None:
                kw['scale'] = scale
            S.op('act', lambda e: e.activation(out=out, in_=in_, func=func, **kw), r, w)

        def cp(eng, out, in_, r, w):
            if eng == 'act':
                S.op('act', lambda e: e.activation(out=out, in_=in_, func=AF.Copy), r, w)
            else:
                S.op(eng, lambda e: e.tensor_copy(out=out, in_=in_), r, w)

        def mm(out, lhsT, rhs, start, stop, r, w):
            S.op('pe', lambda e: e.matmul(out, lhsT=lhsT, rhs=rhs, start=start, stop=stop), r, w)

        def tr(out, in_, ident, r, w):
            S.op('pe', lambda e: e.transpose(out, in_, ident), r, w)

        def dma(q, out, in_, r, w):
            S.op(q, lambda e: e.dma_start(out=out, in_=in_), r, w, dma=True)

        def scan(out, d0, d1, init, r, w):
            S.op('dve', lambda e: e.tensor_tensor_scan(out=out, data0=d0, data1=d1, initial=init,
                                                       op0=ALU.mult, op1=ALU.add), r, w)

        def memset(eng, ap, val, w):
            S.op(eng, lambda e: e.memset(ap, val), (), w)

        xf = sb.alloc([16, 512], F32)
        xn = sb.alloc([16, 512], BF16)
        slots = [sb.alloc([4096], BF16) for _ in range(NSLOT)]
        slot_ctr = [0]

        ring_n = [NSLOT - 1]

        def next_slot():
            b = slots[slot_ctr[0] % ring_n[0]]
            slot_ctr[0] += 1
            return b

        ident_f = sb.alloc([128], F32)
        ident_b = sb.alloc([128], BF16)
        UL = sb.alloc([128], F32)
        SL = sb.alloc([128], F32)
        ones_f = sb.alloc([128], F32)
        onesD = sb.alloc([128], BF16)
        ones1k = sb.alloc([128], BF16)
        cvec = sb.alloc([16 * 4 + 8 * 5 + 16 * 4 + 16], F32)
        co = [0]

        def cslice(n):
            a = cvec.ap[:, co[0]:co[0] + n]
            co[0] += n
            return a
        g1c, gmc, g2c, gfc = cslice(16), cslice(16), cslice(16), cslice(16)
        gssd, gs5, dssd, bglu, d5 = cslice(8), cslice(8), cslice(8), cslice(8), cslice(8)
        convw_ap = cslice(64)
        convb = cslice(16)
        convw = convw_ap.rearrange('p (c k) -> p c k', k=4)
        rowc = sb.alloc([16 + 16 + 128], F32)
        dtb_bc, A_bc, tau_bc = rowc.ap[:, 0:16], rowc.ap[:, 16:32], rowc.ap[:, 32:160]
        s5c = sb.alloc([32 * 9], F32)
        mag_c = s5c.ap[:, 0:32]
        cFs = {('tcos', 128): s5c.ap[:, 32:64], ('tsin', 128): s5c.ap[:, 64:96],
               ('tcos', 16): s5c.ap[:, 96:128], ('tsin', 16): s5c.ap[:, 128:160],
               ('ss', 128): s5c.ap[:, 160:224].rearrange('p (s c) -> p s c', c=2),
               ('ss', 16): s5c.ap[:, 224:288].rearrange('p (s c) -> p s c', c=2)}
        identn_b = sb.alloc([128], BF16)
        Ddiag5 = sb.alloc([8, 128], BF16)
        wdt_b = sb.alloc([16, 16], BF16)
        nstate = max(1, NS)
        st_halo = [sb.alloc([16, 3], F32) for _ in range(nstate)]
        st_S = [sb.alloc([1024], F32) for _ in range(nstate)]
        st_Sb = [sb.alloc([1024], BF16) for _ in range(nstate)]
        st_h5 = [sb.alloc([2, 32], F32) for _ in range(nstate)]
        arena0 = sb.top

        wres = {}
        units = []
        upos = {}

        def cast_unit(name, idx, dst, src, fshape):
            wres[(name, idx)] = DRes(name, idx)
            upos[(name, idx)] = len(units)
            units.append((name, idx, dst, src, list(fshape)))

        def cast_ffn(tag):
            for c in range(NFC):
                cast_unit('wgu' + tag, c, Wb['wgu' + tag][c], I['wgu' + tag][c], [2, 16, 128])
            for fg in range(4):
                for gi, (c0, n) in enumerate(WD_GROUPS):
                    cast_unit('wd' + tag, fg * 6 + gi,
                              Wb['wd' + tag][fg, c0:c0 + n].rearrange('c p j -> p c j'),
                              I['wd' + tag][fg, c0:c0 + n].rearrange('c p j -> p c j'), [n, 512])
        cast_ffn('1')
        for pr in range(16):
            cast_unit('win', pr, Wb['win'][2 * pr:2 * pr + 2].rearrange('c p k j -> p c k j'),
                      I['win'][2 * pr:2 * pr + 2].rearrange('c p k j -> p c k j'), [2, 16, 128])
        for hh in range(2):
            cast_unit('wglu', hh, Wb['wglu'][hh], I['wglu'][hh], [4, 1024])
        for pr in range(8):
            cast_unit('wout', pr, Wb['wout'][2 * pr:2 * pr + 2].rearrange('c p k j -> p c k j'),
                      I['wout'][2 * pr:2 * pr + 2].rearrange('c p k j -> p c k j'), [2, 16, 128])
        cast_ffn('2')
        ckpt('cast')

        cres = [cvec]
        o = 0
        for nm, n in [('g1c', 16), ('gmc', 16), ('g2c', 16), ('gfc', 16), ('gssd', 8), ('gs5', 8), ('dssd', 8),
                      ('bglu', 8), ('d5', 8)]:
            dma('sp', cvec.ap[:, o:o + n], I[nm], (), cres)
            o += n
        dma('sp', cvec.ap[:, o:o + 64], I['convw'].rearrange('p c k -> p (c k)'), (), cres)
        o += 64
        dma('sp', cvec.ap[:, o:o + 16], I['convb'], (), cres)
        dma('sp', rowc.ap[:, 0:16], I['dtb'].partition_broadcast(128), (), [rowc])
        dma('sp', rowc.ap[:, 16:32], I['alog'].partition_broadcast(128), (), [rowc])
        dma('sp', rowc.ap[:, 32:160], I['tau'].partition_broadcast(128), (), [rowc])
        act(A_bc, A_bc, AF.Exp, [rowc], [rowc])
        ts('dve', A_bc, A_bc, -1.0, None, ALU.mult, None, [rowc], [rowc])
        memset('pool', ones_f.ap, 1.0, [ones_f])
        memset('pool', ident_f.ap, 0.0, [ident_f])
        S.op('pool', lambda e: e.affine_select(out=ident_f.ap, in_=ident_f.ap, pattern=[[-1, 128]],
                                               compare_op=ALU.not_equal, fill=1.0, base=0, channel_multiplier=1),
             [ident_f], [ident_f])
        S.op('pool', lambda e: e.affine_select(out=UL.ap, in_=ones_f.ap, pattern=[[1, 128]],
                                               compare_op=ALU.is_ge, fill=0.0, base=0, channel_multiplier=-1),
             [ones_f], [UL])
        S.op('pool', lambda e: e.affine_select(out=SL.ap, in_=ones_f.ap, pattern=[[-1, 128]],
                                               compare_op=ALU.is_gt, fill=0.0, base=0, channel_multiplier=1),
             [ones_f], [SL])
        cp('dve', ident_b.ap, ident_f.ap, [ident_f], [ident_b])
        ts('dve', identn_b.ap, ident_f.ap, -1.0, None, ALU.mult, None, [ident_f], [identn_b])
        memset('dve', onesD.ap, 1.0 / 2048.0, [onesD])
        memset('dve', ones1k.ap, 1.0 / 1024.0, [ones1k])
        tmpA = Arena(sb_root, 'sb', SB_SIZE)
        tmpA.top = arena0
        wdt_f = tmpA.alloc([16, 16], F32)
        dma('sp', wdt_f.ap, I['wdt'], (), [wdt_f])
        cp('dve', wdt_b.ap, wdt_f.ap, [wdt_f], [wdt_b])
        for kc in range(8):
            ts('dve', Ddiag5[kc].ap, ident_f.ap, d5[:, kc:kc + 1], None, ALU.mult, None, [ident_f, cvec], [Ddiag5[kc]])

        ckpt('consts')
        c5 = tmpA.alloc([16, 32], F32)
        ci5 = tmpA.alloc([32], I32)

        def C5(i):
            return c5.ap[:, i, :]
        LR, LI, LS_, STEP, ANG, LBR, LBI, T0, T1, QRE, QIM, T2, T3 = range(13)
        dma('sp', C5(LR), I['lrc'], (), [c5])
        dma('sp', C5(LI), I['lic'], (), [c5])
        dma('sp', C5(LS_), I['lsc'], (), [c5])
        R5 = [c5, ci5]
        act(C5(STEP), C5(LS_), AF.Exp, R5, R5)
        tt('dve', C5(T0), C5(LR), C5(STEP), ALU.mult, R5, R5)
        act(mag_c, C5(T0), AF.Exp, R5, [s5c] + R5)
        tt('dve', C5(ANG), C5(LI), C5(STEP), ALU.mult, R5, R5)

        def sincos(dst, src, shift, ki, tmp, R):
            ts('dve', tmp, src, shift, 1.0 / TWO_PI, ALU.add, ALU.mult, R, R)
            cp('dve', ki, tmp, R, R)
            cp('dve', tmp, ki, R, R)
            stt('dve', tmp, tmp, -TWO_PI, src, ALU.mult, ALU.add, R, R)
            ts('dve', tmp, tmp, shift, None, ALU.add, None, R, R)
            ts('dve', tmp, tmp, -3.141592, 3.141592, ALU.max, ALU.min, R, R)
            act(dst, tmp, AF.Sin, R, R)
        sincos(C5(LBI), C5(ANG), 0.0, ci5.ap, C5(T1), R5)
        sincos(C5(LBR), C5(ANG), 1.5707963267948966, ci5.ap, C5(T1), R5)
        tt('dve', C5(LBR), C5(LBR), mag_c, ALU.mult, R5 + [s5c], R5)
        tt('dve', C5(LBI), C5(LBI), mag_c, ALU.mult, R5 + [s5c], R5)
        tt('dve', C5(T0), C5(LR), C5(LR), ALU.mult, R5, R5)
        tt('dve', C5(T1), C5(LI), C5(LI), ALU.mult, R5, R5)
        tt('dve', C5(T0), C5(T0), C5(T1), ALU.add, R5, R5)
        S.op('dve', lambda e: e.reciprocal(out=C5(T0), in_=C5(T0)), R5, R5)
        ts('dve', C5(T1), C5(LBR), -1.0, None, ALU.add, None, R5, R5)
        tt('dve', C5(T2), C5(T1), C5(LR), ALU.mult, R5, R5)
        tt('dve', C5(T3), C5(LBI), C5(LI), ALU.mult, R5, R5)
        tt('dve', C5(T2), C5(T2), C5(T3), ALU.add, R5, R5)
        tt('dve', C5(QRE), C5(T2), C5(T0), ALU.mult, R5, R5)
        tt('dve', C5(T2), C5(LBI), C5(LR), ALU.mult, R5, R5)
        tt('dve', C5(T3), C5(T1), C5(LI), ALU.mult, R5, R5)
        tt('dve', C5(T2), C5(T2), C5(T3), ALU.subtract, R5, R5)
        tt('dve', C5(QIM), C5(T2), C5(T0), ALU.mult, R5, R5)

        ckpt('s5a')
        wbre_f = tmpA.alloc([32, 128], F32)
        wbim_f = tmpA.alloc([32, 128], F32)
        wbbre_o = tmpA.alloc([32, 128], BF16)
        wbbim_o = tmpA.alloc([32, 128], BF16)
        dma('sp', wbre_f.ap, I['wbre'], (), [wbre_f])
        dma('sp', wbim_f.ap, I['wbim'], (), [wbim_f])
        dg = [tmpA.alloc([2, 128], F32) for _ in range(2)]
        tq = [tmpA.alloc([4, 128], F32) for _ in range(2)]
        for sc in range(32):
            d_ = dg[sc % 2]
            t_ = tq[sc % 2]
            qps = pbank(sc % 4, [2, 128])
            ts('dve', d_[0].ap, ident_f.ap, C5(QRE)[:, sc:sc + 1], None, ALU.mult, None, [ident_f, c5], [d_[0]])
            ts('dve', d_[1].ap, ident_f.ap, C5(QIM)[:, sc:sc + 1], None, ALU.mult, None, [ident_f, c5], [d_[1]])
            mm(qps[0].ap, ones_f.ap, d_[0].ap, True, True, [ones_f, d_[0]], [qps[0]])
            mm(qps[1].ap, ones_f.ap, d_[1].ap, True, True, [ones_f, d_[1]], [qps[1]])
            tt('dve', t_[0].ap, wbre_f[sc].ap, qps[0].ap, ALU.mult, [wbre_f[sc], qps[0]], [t_[0]])
            tt('dve', t_[1].ap, wbim_f[sc].ap, qps[1].ap, ALU.mult, [wbim_f[sc], qps[1]], [t_[1]])
            tt('dve', t_[2].ap, wbre_f[sc].ap, qps[1].ap, ALU.mult, [wbre_f[sc], qps[1]], [t_[2]])
            tt('dve', t_[3].ap, wbim_f[sc].ap, qps[0].ap, ALU.mult, [wbim_f[sc], qps[0]], [t_[3]])
            tt('pool', wbbre_o[sc].ap, t_[0].ap, t_[1].ap, ALU.subtract, [t_[0], t_[1]], [wbbre_o[sc]])
            tt('pool', wbbim_o[sc].ap, t_[2].ap, t_[3].ap, ALU.add, [t_[2], t_[3]], [wbbim_o[sc]])
        for nm, bf in [('wbbre', wbbre_o), ('wbbim', wbbim_o)]:
            rs = DRes(nm)
            wres[(nm, 0)] = rs
            dma('sp', Wb[nm], bf.ap, [bf], [rs])
        ckpt('s5b')
        tmpA.top = wbre_f.off
        for nm, src, scale in [('wcre', 'wcre', 1.0), ('wcren', 'wcre', -1.0), ('wcimn', 'wcim', -1.0)]:
            wf_ = tmpA.alloc([32, 128], F32)
            wo_ = tmpA.alloc([32, 128], BF16)
            dma('sp', wf_.ap, I[src], (), [wf_])
            act(wo_.ap, wf_.ap, AF.Copy, [wf_], [wo_], scale=scale)
            rs = DRes(nm)
            wres[(nm, 0)] = rs
            dma('sp', Wb[nm], wo_.ap, [wo_], [rs])
            tmpA.top = wbre_f.off
        ckpt('s5c')
        tmpA.top = wbre_f.off
        trs = {nm: DRes(nm) for nm in ('tsin', 'tcos')}
        for nm in trs:
            wres[(nm, 0)] = trs[nm]
        for half in range(2):
            av = tmpA.alloc([16, 128], F32)
            tv = tmpA.alloc([16, 128], F32)
            ov = tmpA.alloc([16, 128], F32)
            kv = tmpA.alloc([16, 128], I32)
            ob = [tmpA.alloc([16, 128], BF16) for _ in range(2)]
            Rt = [av, tv, ov, kv]
            hs = slice(half * 16, (half + 1) * 16)
            tt('dve', av.ap, C5(ANG)[:, hs].unsqueeze(2).to_broadcast([128, 16, 128]),
               tau_bc.unsqueeze(1).to_broadcast([128, 16, 128]), ALU.mult, [c5, rowc], Rt)
            for ti_, (nm, shift) in enumerate([('tsin', 0.0), ('tcos', 1.5707963267948966)]):
                sincos(ov.ap, av.ap, shift, kv.ap, tv.ap, Rt)
                for frl in (128, 16):
                    cp('dve', cFs[(nm, frl)][:, hs], ov.ap[:, :, frl - 1], [ov], [s5c])
                    if nm == 'tsin':
                        ts('dve', cFs[('ss', frl)][:, hs, 0], ov.ap[:, :, frl - 1], -1.0, None, ALU.mult, None, [ov], [s5c])
                        cp('dve', cFs[('ss', frl)][:, hs, 1], ov.ap[:, :, frl - 1], [ov], [s5c])
                cp('act', ob[ti_].ap, ov.ap, [ov], [ob[ti_]])
                dma('sp', Wb[nm][:, hs, :], ob[ti_].ap, [ob[ti_]], [trs[nm]])
            tmpA.top = wbre_f.off

        ckpt('setup')
        stgs = [Buf(sb_root, 'sb', SB_SIZE - STG + i * (STG // 2), STG // 2, F32, [STG // 8]) for i in range(2)]
        stgs3 = stgs + [slots[NSLOT - 1].view(F32, [STG // 8])]
        t0 = {'active': True, 'emitted': 0, 'slot': {}, 'pending': [], 'k': 0}
        LOOKAHEAD = 2

        def fpat(shape):
            names = ' '.join('d%d' % k for k in range(len(shape)))
            return 'p (%s) -> p %s' % (names, names), {'d%d' % k: shape[k] for k in range(len(shape))}

        def t0_emit(pos):
            name, idx, dst, src, fshape = units[pos]
            b = next_slot()
            parts = []
            if len(fshape) == 3:
                a_, k_, j_ = fshape
                kh = k_ if k_ * j_ <= STG // 8 else k_ // 2
                for ai in range(a_):
                    for k0 in range(0, k_, kh):
                        parts.append((src[:, ai, k0:k0 + kh, :], (ai * k_ + k0) * j_, [kh, j_]))
            else:
                n_, j_ = fshape
                rp = max(1, (STG // 8) // j_)
                for r0 in range(0, n_, rp):
                    r1 = min(n_, r0 + rp)
                    parts.append((src[:, r0:r1, :], r0 * j_, [r1 - r0, j_]))
            for (sap, off, shp) in parts:
                stg = stgs3[t0['k'] % 3]
                eng = 'act' if t0['k'] % 2 == 0 else 'dve'
                t0['k'] += 1
                cnt = _prod(shp)
                pat, kw = fpat(shp)
                dma('sp', stg.ap[:, 0:cnt].rearrange(pat, **kw), sap, (), [stg])
                cp(eng, b.ap[:, off:off + cnt], stg.ap[:, 0:cnt], [stg], [b])
            t0['slot'][pos] = b
            t0['pending'].append(pos)

        def t0_flush(keep):
            while len(t0['pending']) > keep:
                pos = t0['pending'].pop(0)
                name, idx, dst, src, fshape = units[pos]
                b = t0['slot'][pos]
                pat, kw = fpat(fshape)
                dma('sp', dst, b.ap[:, 0:_prod(fshape)].rearrange(pat, **kw), [b], [wres[(name, idx)]])

        def t0_finish():
            assert t0['emitted'] == len(units), (t0['emitted'], len(units))
            t0_flush(0)
            t0['active'] = False
            ring_n[0] = NSLOT

        def pool_load(name, idx, src_ap, view):
            if t0['active'] and (name, idx) in upos:
                pos = upos[(name, idx)]
                gend = upos[('wglu', 0)]
                lim = gend if pos < gend else len(units)
                while t0['emitted'] < min(pos + 1 + LOOKAHEAD, lim):
                    t0_emit(t0['emitted'])
                    t0['emitted'] += 1
                    t0_flush(LOOKAHEAD + 1)
                b = t0['slot'][pos]
                return b, view(b)
            if t0['active']:
                t0_flush(0)
            b = next_slot()
            dst = view(b)
            dma('sp', dst, src_ap, [wres[(name, idx)]], [b])
            return b, dst

        def norm_rstd(sqsrc, nchunks, ones_b, Tt, ssb, tmp_sq, rstd):
            for kc in range(nchunks):
                a, rr = sqsrc(kc)
                sq = tmp_sq[kc % 2]
                act(sq.ap[:, :Tt], a, AF.Square, rr, [sq])
                mm(ssb.ap[:, :Tt], ones_b.ap, sq.ap[:, :Tt], kc == 0, kc == nchunks - 1, [ones_b, sq], [ssb])
            finish_rstd(ssb, Tt, rstd)

        def finish_rstd(ssb, Tt, rstd):
            act(rstd.ap[:, :Tt], ssb.ap[:, :Tt], AF.Sqrt, [ssb], [rstd], bias=EPS)
            S.op('dve', lambda e: e.reciprocal(out=rstd.ap[:, :Tt], in_=rstd.ap[:, :Tt]), [rstd], [rstd])

        def ffn(Tt, tag, gcol, A):
            h = A.alloc([NFC, 512], BF16)
            sgt = [A.alloc([512], F32) for _ in range(2)]
            sqt = [A.alloc([512], BF16) for _ in range(2)]
            rstd = A.alloc([512], F32)
            norm_rstd(lambda kc: (xf[kc].ap[:, :Tt], [xf[kc]]), 16, onesD, Tt, pbank(7, [512]), sqt, rstd)
            for kc in range(16):
                stt('dve', xn[kc].ap[:, :Tt], xf[kc].ap[:, :Tt], gcol[:, kc:kc + 1], rstd.ap[:, :Tt],
                    ALU.mult, ALU.mult, [xf[kc], cvec, rstd], [xn[kc]])
            for c in range(NFC):
                b, w = pool_load('wgu' + tag, c, Wb['wgu' + tag][c],
                                 lambda b: b.ap.rearrange('p (g k j) -> p g k j', g=2, k=16))
                gp = pbank(2 * (c % 2), [512])
                up = pbank(2 * (c % 2) + 1, [512])
                for gi, pp in enumerate((gp, up)):
                    for ko in range(16):
                        mm(pp.ap[:, :Tt], w[:, gi, ko, :], xn[ko].ap[:, :Tt], ko == 0, ko == 15, [b, xn[ko]], [pp])
                sg = sgt[c % 2]
                act(sg.ap[:, :Tt], gp.ap[:, :Tt], AF.Silu, [gp], [sg])
                tt('dve', h[c].ap[:, :Tt], up.ap[:, :Tt], sg.ap[:, :Tt], ALU.mult, [up, sg], [h[c]])
            for fg in range(4):
                base = 4 if fg % 2 == 0 else 0
                ops_ = [pbank(base + j, [512]) for j in range(4)]
                for gi, (c0, n) in enumerate(WD_GROUPS):
                    b, w = pool_load('wd' + tag, fg * 6 + gi,
                                     Wb['wd' + tag][fg, c0:c0 + n].rearrange('c p j -> p c j'),
                                     lambda b: b.ap[:, 0:n * 512].rearrange('p (c j) -> p c j', c=n))
                    for ci in range(n):
                        c = c0 + ci
                        for j in range(4):
                            mm(ops_[j].ap[:, :Tt], w[:, ci, j * 128:(j + 1) * 128], h[c].ap[:, :Tt],
                               c == 0, c == NFC - 1, [b, h[c]], [ops_[j]])
                for j in range(4):
                    kc = 4 * fg + j
                    stt('dve', xf[kc].ap[:, :Tt], ops_[j].ap[:, :Tt], 0.5, xf[kc].ap[:, :Tt], ALU.mult, ALU.add,
                        [ops_[j], xf[kc]], [xf[kc]])

        def load_tile(src_rows, Tt, A):
            nb = (Tt + 127) // 128
            stg = [A.alloc([2048], F32) for _ in range(2)]
            for tb in range(nb):
                n = min(128, Tt - tb * 128)
                sg = stg[tb % 2]
                dma('sp', sg.ap[0:n, :], src_rows[tb * 128:tb * 128 + n, :], (), [sg])
                for q4 in range(4):
                    pb_ = pbank((tb * 4 + q4) % 4, [4, 128])
                    for j in range(4):
                        kc = q4 * 4 + j
                        tr(pb_.ap[:, j, 0:n], sg.ap[0:n, kc * 128:(kc + 1) * 128], ident_f.ap[0:n, 0:n], [sg, ident_f], [pb_])
                    dst = xf.ap[:, q4 * 4:q4 * 4 + 4, tb * 128:tb * 128 + n]
                    cp('act' if q4 % 2 == 0 else 'dve', dst, pb_.ap[:, :, 0:n], [pb_], [xf[q4 * 4 + j] for j in range(4)])

        def store_tile(dst_rows, Tt, A):
            nb = (Tt + 127) // 128
            sqt = [A.alloc([512], BF16) for _ in range(2)]
            rstd = A.alloc([512], F32)
            stg = [A.alloc([2048], F32) for _ in range(2)]
            norm_rstd(lambda kc: (xf[kc].ap[:, :Tt], [xf[kc]]), 16, onesD, Tt, pbank(7, [512]), sqt, rstd)
            for kc in range(16):
                stt('dve', xf[kc].ap[:, :Tt], xf[kc].ap[:, :Tt], gfc[:, kc:kc + 1], rstd.ap[:, :Tt],
                    ALU.mult, ALU.mult, [xf[kc], cvec, rstd], [xf[kc]])
            for tb in range(nb):
                n = min(128, Tt - tb * 128)
                sg = stg[tb % 2]
                for q4 in range(4):
                    pb_ = pbank((tb * 4 + q4) % 4, [4, 128])
                    for j in range(4):
                        kc = q4 * 4 + j
                        tr(pb_.ap[0:n, j, :], xf[kc].ap[:, tb * 128:tb * 128 + n], ident_f.ap, [xf[kc], ident_f], [pb_])
                    cp('act' if q4 % 2 == 0 else 'dve', sg.ap[0:n, q4 * 512:(q4 + 1) * 512],
                       pb_.ap[0:n].rearrange('p a b -> p (a b)'), [pb_], [sg])
                dma('sp', dst_rows[tb * 128:tb * 128 + n, :], sg.ap[0:n, :], [sg], ())

        def mixer(Tt, segs, A):
            nseg = len(segs)
            L = segs[0][2]
            Q = min(128, L)
            Fr = min(128, L)
            sz = A.alloc([8, 512], BF16)
            u5b = A.alloc([8, 512], BF16)
            rstd1 = A.alloc([512], F32)
            xsb = A.alloc([8, 512], BF16)
            Bb = A.alloc([4, 512], BF16)
            Cb = A.alloc([4, 512], BF16)
            sub0 = A.top
            rawc = [A.alloc([nseg * (L + 3)], F32) for _ in range(2)]
            acc = [A.alloc([512], F32) for _ in range(2)]
            mix_in = xn
            sqt = [acc[0].view(BF16, [512]), acc[1].view(BF16, [512])]
            norm_rstd(lambda kc: (xf[kc].ap[:, :Tt], [xf[kc]]), 16, onesD, Tt, pbank(7, [512]), sqt, rstd1)
            for kc in range(16):
                stt('dve', xn[kc].ap[:, :Tt], xf[kc].ap[:, :Tt], gmc[:, kc:kc + 1], rstd1.ap[:, :Tt],
                    ALU.mult, ALU.mult, [xf[kc], cvec, rstd1], [xn[kc]])
            for pr in range(16):
                b, w = pool_load('win', pr, Wb['win'][2 * pr:2 * pr + 2].rearrange('c p k j -> p c k j'),
                                 lambda b: b.ap.rearrange('p (c k j) -> p c k j', c=2, k=16))
                for cc in range(2):
                    c = 2 * pr + cc
                    pp = pbank(c % 4, [512])
                    for ko in range(16):
                        mm(pp.ap[:, :Tt], w[:, cc, ko, :], xn[ko].ap[:, :Tt], ko == 0, ko == 15, [b, xn[ko]], [pp])
                    if c < 8:
                        act(sz[c].ap[:, :Tt], pp.ap[:, :Tt], AF.Silu, [pp], [sz[c]])
                    elif c < 24:
                        j = c - 8
                        rw = rawc[j % 2]
                        rw3 = rw.ap.rearrange('p (s l) -> p s l', s=nseg)
                        ac = acc[j % 2]
                        ac3 = ac.ap[:, :Tt].rearrange('p (s l) -> p s l', s=nseg)
                        for si, (sidx, c0, _) in enumerate(segs):
                            cp('pool', rw3[:, si, 0:3], st_halo[sidx].ap[:, j, :], [st_halo[sidx]], [rw])
                        cp('act', rw3[:, :, 3:3 + L], pp.ap[:, :Tt].rearrange('p (s l) -> p s l', s=nseg), [pp], [rw])
                        for si, (sidx, c0, _) in enumerate(segs):
                            cp('pool', st_halo[sidx].ap[:, j, :], rw3[:, si, L:L + 3], [rw], [st_halo[sidx]])
                        ts('dve', ac3, rw3[:, :, 3:3 + L], convw[:, j, 3:4], convb[:, j:j + 1], ALU.mult, ALU.add,
                           [rw, cvec], [ac])
                        for k in (2, 1, 0):
                            stt('dve', ac3, rw3[:, :, k:k + L], convw[:, j, k:k + 1], ac3, ALU.mult, ALU.add,
                                [rw, cvec, ac], [ac])
                        dst = xsb[j] if j < 8 else (Bb[j - 8] if j < 12 else Cb[j - 12])
                        act(dst.ap[:, :Tt], ac.ap[:, :Tt], AF.Silu, [ac], [dst])
                    else:
                        cp('dve', u5b[c - 24].ap[:, :Tt], pp.ap[:, :Tt], [pp], [u5b[c - 24]])
            ckpt('inproj')
            A.top = sub0
            dtt = A.alloc([6, 16], F32)
            V2 = [A.alloc([4, Q], F32)] * 2
            Lexp2 = [A.alloc([4, Q], F32)] * 2
            Ebc2 = [A.alloc([4, Q], F32)] * 2
            Mp2 = [A.alloc([4, Q], BF16) for _ in range(2)]
            Cexp2 = [A.alloc([4, Q], BF16) for _ in range(2)]
            CBm = A.alloc([4, Q], F32)
            xdt = A.alloc([16, 64], BF16)
            xdd = A.alloc([16, 64], BF16)
            xdd_flat = xdd.view(BF16, [1024])
            Btm = A.alloc([4, 128], BF16)
            ygf = [A.alloc([Q], F32) for _ in range(2)]
            ytmp = [A.alloc([Q], F32) for _ in range(2)]
            ysq = [A.alloc([Q], BF16) for _ in range(2)]
            ss1 = pbank(7, [512])
            for (sidx, c0, Ls) in segs:
                Sst, Sb = st_S[sidx], st_Sb[sidx]
                for ck in range(Ls // Q):
                    cs = c0 + ck * Q
                    dtp = pbank(3, [16])
                    remp = pbank(3, [16], coloff=512)
                    totp = pbank(3, [16], coloff=1024)
                    for ko in range(16):
                        mm(dtp.ap[0:Q, :], xn[ko].ap[:, cs:cs + Q], wdt_b.ap[:, ko, :], ko == 0, ko == 15, [xn[ko], wdt_b], [dtp])
                    t1, dt_, dtA, dec_end, decay_bc, e1 = [dtt[i] for i in range(6)]
                    tt('dve', t1.ap[0:Q], dtp.ap[0:Q, :], dtb_bc[0:Q], ALU.add, [dtp, rowc], [t1])
                    act(e1.ap[0:Q], t1.ap[0:Q], AF.Exp, [t1], [e1])
                    act(dt_.ap[0:Q], e1.ap[0:Q], AF.Ln, [e1], [dt_], bias=1.0)
                    tt('dve', dtA.ap[0:Q], dt_.ap[0:Q], A_bc[0:Q], ALU.mult, [dt_, rowc], [dtA])
                    mm(remp.ap[0:Q, :], SL.ap[0:Q, 0:Q], dtA.ap[0:Q], True, True, [SL, dtA], [remp])
                    mm(totp.ap[:, :], ones_f.ap[0:Q, :], dtA.ap[0:Q], True, True, [ones_f, dtA], [totp])
                    act(dec_end.ap[0:Q], remp.ap[0:Q, :], AF.Exp, [remp], [dec_end])
                    act(decay_bc.ap, totp.ap, AF.Exp, [totp], [decay_bc])
                    cbp = pbank(2, [4, 128])
                    for g in range(4):
                        mm(cbp.ap[0:Q, g, 0:Q], Bb[g].ap[:, cs:cs + Q], Cb[g].ap[:, cs:cs + Q], True, True, [Bb[g], Cb[g]], [cbp])
                    tt('dve', CBm.ap[0:Q], cbp.ap[0:Q, :, 0:Q], UL.ap[0:Q, 0:Q].unsqueeze(1).to_broadcast([Q, 4, Q]),
                       ALU.mult, [cbp, UL], [CBm])
                    xtp = ps.at(0, [8, 128], BF16)
                    xtp16 = ps.at(0, [16, 64], BF16)
                    for kc in range(8):
                        tr(xtp.ap[0:Q, kc, :], xsb[kc].ap[:, cs:cs + Q], ident_b.ap, [xsb[kc], ident_b], [xtp])
                    tt('dve', xdt.ap[0:Q], xtp16.ap[0:Q], dt_.ap[0:Q].unsqueeze(2).to_broadcast([Q, 16, 64]), ALU.mult,
                       [xtp, dt_], [xdt])
                    tt('pool', xdd.ap[0:Q], xdt.ap[0:Q], dec_end.ap[0:Q].unsqueeze(2).to_broadcast([Q, 16, 64]), ALU.mult,
                       [xdt, dec_end], [xdd])
                    btp = ps.at(2048, [4, 128], BF16)
                    for g in range(4):
                        tr(btp.ap[0:Q, g, :], Bb[g].ap[:, cs:cs + Q], ident_b.ap, [Bb[g], ident_b], [btp])
                    cp('act', Btm.ap[0:Q], btp.ap[0:Q], [btp], [Btm])
                    def ssd_a(g):
                        V, Lexp, Ebc, Mp, Cexp = V2[g % 2], Lexp2[g % 2], Ebc2[g % 2], Mp2[g % 2], Cexp2[g % 2]
                        segp = pbank(4 + (g % 2), [4, Q])
                        Rp = pbank(6 if g % 2 == 0 else 3, [4, Q])
                        tt('dve', V.ap[0:Q], UL.ap[0:Q, 0:Q].unsqueeze(1).to_broadcast([Q, 4, Q]),
                           dtA.ap[0:Q, 4 * g:4 * g + 4].unsqueeze(2).to_broadcast([Q, 4, Q]), ALU.mult, [UL, dtA], [V])
                        mm(segp.ap[0:Q], SL.ap[0:Q, 0:Q], V.ap[0:Q], True, True, [SL, V], [segp])
                        mm(Rp.ap, ones_f.ap[0:Q, :], V.ap[0:Q], True, True, [ones_f, V], [Rp])
                        act(Lexp.ap[0:Q], segp.ap[0:Q], AF.Exp, [segp], [Lexp])
                        act(Ebc.ap, Rp.ap, AF.Exp, [Rp], [Ebc])
                        tt('dve', Mp.ap[0:Q], Lexp.ap[0:Q], CBm.ap[0:Q, g:g + 1, :].to_broadcast([Q, 4, Q]), ALU.mult,
                           [Lexp, CBm], [Mp])
                        tt('pool', Cexp.ap, Ebc.ap, Cb.ap[:, g:g + 1, cs:cs + Q].to_broadcast([128, 4, Q]), ALU.mult,
                           [Ebc, Cb[g]], [Cexp])

                    def ssd_b(g):
                        Mp, Cexp = Mp2[g % 2], Cexp2[g % 2]
                        for kk in range(2):
                            kc = 2 * g + kk
                            ybase = (2048 + 1024) if g % 2 == 0 else 0
                            ypk = ps.at(ybase + kk * 512, [Q], F32)
                            for hh in range(2):
                                h_ = 2 * kc + hh
                                hq = h_ - 4 * g
                                o_ = ypk.ap[hh * 64:(hh + 1) * 64, :]
                                mm(o_, xdt.ap[0:Q, h_, :], Mp.ap[0:Q, hq, :], True, False, [xdt, Mp], [ypk])
                                mm(o_, Sb.ap[:, h_ * 64:(h_ + 1) * 64], Cexp.ap[:, hq, :], False, True, [Sb, Cexp], [ypk])
                            yt = ytmp[kc % 2]
                            yg = ygf[kc % 2]
                            sq = ysq[kc % 2]
                            stt('dve', yt.ap, xsb[kc].ap[:, cs:cs + Q], dssd[:, kc:kc + 1], ypk.ap, ALU.mult, ALU.add,
                                [xsb[kc], cvec, ypk], [yt])
                            tt('dve', yg.ap, yt.ap, sz[kc].ap[:, cs:cs + Q], ALU.mult, [yt, sz[kc]], [yg])
                            act(sq.ap, yg.ap, AF.Square, [yg], [sq])
                            mm(ss1.ap[:, cs:cs + Q], ones1k.ap, sq.ap, kc == 0, kc == 7, [ones1k, sq], [ss1])
                            act(mix_in[kc].ap[:, cs:cs + Q], yg.ap, AF.Copy, [yg, cvec], [mix_in[kc]], scale=gssd[:, kc:kc + 1])

                    ssd_a(0)
                    for g in range(4):
                        if g + 1 < 4:
                            ssd_a(g + 1)
                        ssd_b(g)
                    Sp = [pbank(0, [512]), pbank(2, [512])]
                    for g in range(4):
                        mm(Sp[g // 2].ap[:, (g % 2) * 256:(g % 2) * 256 + 256], Btm.ap[0:Q, g, :],
                           xdd_flat.ap[0:Q, g * 256:(g + 1) * 256], True, True, [Btm, xdd], [Sp[g // 2]])
                    tt('dve', Sst.ap.rearrange('p (h j) -> p h j', j=64), Sst.ap.rearrange('p (h j) -> p h j', j=64),
                       decay_bc.ap.unsqueeze(2).to_broadcast([128, 16, 64]), ALU.mult, [Sst, decay_bc], [Sst])
                    for j in range(2):
                        tt('dve', Sst.ap[:, j * 512:(j + 1) * 512], Sst.ap[:, j * 512:(j + 1) * 512], Sp[j].ap, ALU.add,
                           [Sst, Sp[j]], [Sst])
                    cp('act', Sb.ap, Sst.ap, [Sst], [Sb])
            finish_rstd(ss1, Tt, rstd1)

            ckpt('ssd')
            A.top = xsb.off
            nfr = Tt // Fr
            s5base = A.top
            sets = []
            for _ in range(4):
                d = {k: A.alloc([512], BF16) for k in ['brb', 'bib', 'p1', 'p2']}
                d['p3'], d['p4'] = d['bib'], d['brb']
                d['u1'], d['u2'], d['u3'], d['u4'] = d['p1'], d['p2'], d['brb'], d['bib']
                d['G'] = A.alloc([2, 512], F32)
                d['gre'] = d['G'][0]
                d['gim'] = d['G'][1]
                d['greb'] = d['brb']
                d['gimb'] = d['bib']
                d['tiny'] = A.alloc([8], F32)
                sets.append(d)
            glb = sz
            wsl = {}
            for nm in ['wbbre', 'wbbim', 'tcos', 'tsin', 'wcre', 'wcren', 'wcimn']:
                wsl[nm] = pool_load(nm, 0, Wb[nm], lambda b: b.ap.rearrange('p (s j) -> p s j', s=32))
            cb_, ctab_all = wsl['tcos']
            sb_, stab_all = wsl['tsin']
            cFc, sFc, ssc = cFs[('tcos', Fr)], cFs[('tsin', Fr)], cFs[('ss', Fr)]
            y5ps = {}

            def v3(bf):
                return bf.ap[:, :Tt].rearrange('p (f r) -> p f r', f=nfr)

            def tabs(sc):
                ctab = ctab_all[:, sc, 0:Fr].unsqueeze(1).to_broadcast([128, nfr, Fr])
                stab = stab_all[:, sc, 0:Fr].unsqueeze(1).to_broadcast([128, nfr, Fr])
                return ctab, stab

            def st_front(sc):
                kc = sc // 4
                d = sets[sc % 4]
                brp = pbank(0, [512])
                bip = pbank(1, [512])
                mm(brp.ap[:, :Tt], wsl['wbbre'][1][:, sc, :], u5b[kc].ap[:, :Tt], True, True, [wsl['wbbre'][0], u5b[kc]], [brp])
                mm(bip.ap[:, :Tt], wsl['wbbim'][1][:, sc, :], u5b[kc].ap[:, :Tt], True, True, [wsl['wbbim'][0], u5b[kc]], [bip])
                cp('act', d['brb'].ap[:, :Tt], brp.ap[:, :Tt], [brp], [d['brb']])
                cp('act', d['bib'].ap[:, :Tt], bip.ap[:, :Tt], [bip], [d['bib']])

            def st_mid(sc):
                d = sets[sc % 4]
                ctab, stab = tabs(sc)
                tt('dve', v3(d['p1']), v3(d['brb']), ctab, ALU.mult, [d['brb'], cb_], [d['p1']])
                tt('dve', v3(d['p2']), v3(d['bib']), stab, ALU.mult, [d['bib'], sb_], [d['p2']])
                tt('dve', v3(d['p3']), v3(d['bib']), ctab, ALU.mult, [d['bib'], cb_], [d['p3']])
                tt('dve', v3(d['p4']), v3(d['brb']), stab, ALU.mult, [d['brb'], sb_], [d['p4']])
                bprp = pbank(2 + 2 * (sc % 2), [512])
                bpip = pbank(3 + 2 * (sc % 2), [512])
                mm(bprp.ap[:, :Tt], ident_b.ap, d['p1'].ap[:, :Tt], True, False, [ident_b, d['p1']], [bprp])
                mm(bprp.ap[:, :Tt], ident_b.ap, d['p2'].ap[:, :Tt], False, True, [ident_b, d['p2']], [bprp])
                mm(bpip.ap[:, :Tt], ident_b.ap, d['p3'].ap[:, :Tt], True, False, [ident_b, d['p3']], [bpip])
                mm(bpip.ap[:, :Tt], identn_b.ap, d['p4'].ap[:, :Tt], False, True, [identn_b, d['p4']], [bpip])

            def st_scan_pair(scs):
                frames = [(sidx, c0 + f * Fr) for (sidx, c0, Ls) in segs for f in range(Ls // Fr)]
                for (sidx, a) in frames:
                    h5 = st_h5[sidx]
                    col = a + Fr - 1
                    for sc in scs:
                        d = sets[sc % 4]
                        bprp = pbank(2 + 2 * (sc % 2), [512])
                        bpip = pbank(3 + 2 * (sc % 2), [512])
                        magb = mag_c[:, sc:sc + 1].to_broadcast([128, Fr])
                        for comp, srcp, dst in ((0, bprp, 'gre'), (1, bpip, 'gim')):
                            scan(d[dst].ap[:, a:a + Fr], magb, srcp.ap[:, a:a + Fr], h5.ap[:, comp, sc:sc + 1],
                                 [s5c, srcp, h5], [d[dst]])
                    for sc in scs:
                        d = sets[sc % 4]
                        tt('dve', d['tiny'].ap[:, 0:2], d['G'].ap[:, ::-1, col], ssc[:, sc, :], ALU.mult, [d['G'], s5c], [d['tiny']])
                    for sc in scs:
                        d = sets[sc % 4]
                        stt('dve', h5.ap[:, :, sc], d['G'].ap[:, :, col], cFc[:, sc:sc + 1], d['tiny'].ap[:, 0:2], ALU.mult, ALU.add,
                            [d['G'], s5c, d['tiny']], [h5])
                for sc in scs:
                    d = sets[sc % 4]
                    cp('act', d['greb'].ap[:, :Tt], d['gre'].ap[:, :Tt], [d['gre']], [d['greb']])
                    cp('act', d['gimb'].ap[:, :Tt], d['gim'].ap[:, :Tt], [d['gim']], [d['gimb']])

            def st_out(sc):
                kc = sc // 4
                d = sets[sc % 4]
                ctab, stab = tabs(sc)
                tt('dve', v3(d['u1']), v3(d['greb']), ctab, ALU.mult, [d['greb'], cb_], [d['u1']])
                tt('dve', v3(d['u2']), v3(d['gimb']), stab, ALU.mult, [d['gimb'], sb_], [d['u2']])
                tt('dve', v3(d['u3']), v3(d['greb']), stab, ALU.mult, [d['greb'], sb_], [d['u3']])
                tt('dve', v3(d['u4']), v3(d['gimb']), ctab, ALU.mult, [d['gimb'], cb_], [d['u4']])
                if sc % 4 == 0:
                    y5ps[kc] = pbank(6 + (kc % 2), [512])
                y5p = y5ps[kc]
                for wi, (wn, un) in enumerate([('wcre', 'u1'), ('wcren', 'u2'), ('wcimn', 'u3'), ('wcimn', 'u4')]):
                    mm(y5p.ap[:, :Tt], wsl[wn][1][:, sc, :], d[un].ap[:, :Tt], sc % 4 == 0 and wi == 0, False,
                       [wsl[wn][0], d[un]], [y5p])
                if sc % 4 == 3:
                    mm(y5p.ap[:, :Tt], Ddiag5[kc].ap, u5b[kc].ap[:, :Tt], False, True, [Ddiag5[kc], u5b[kc]], [y5p])
                    act(glb[kc].ap[:, :Tt], y5p.ap[:, :Tt], AF.Gelu, [y5p], [glb[kc]])

            st_front(0)
            st_front(1)
            st_mid(0)
            st_mid(1)
            for j in range(16):
                a_, b_ = 2 * j, 2 * j + 1
                if j + 1 < 16:
                    st_front(a_ + 2)
                    st_front(b_ + 2)
                st_scan_pair((a_, b_))
                if j + 1 < 16:
                    st_mid(a_ + 2)
                    st_mid(b_ + 2)
                st_out(a_)
                st_out(b_)
            A.top = s5base
            rstd2 = A.alloc([512], F32)
            gtmp = [{'sig': A.alloc([512], F32), 'yg': A.alloc([512], F32), 'sq': A.alloc([512], BF16),
                     'tA': A.alloc([512], F32), 'tB': A.alloc([512], F32)} for _ in range(2)]
            ckpt('s5')
            wg = [pool_load('wglu', hh, Wb['wglu'][hh], lambda b: b.ap.rearrange('p (k j) -> p k j', k=4)) for hh in range(2)]
            ss2 = pbank(5, [512])
            for fc in range(8):
                gp_ = pbank(fc % 4, [512])
                for kc in range(8):
                    b, w = wg[kc // 4]
                    mm(gp_.ap[:, :Tt], w[:, kc % 4, fc * 128:(fc + 1) * 128], glb[kc].ap[:, :Tt], kc == 0, kc == 7, [b, glb[kc]], [gp_])
                sig = gtmp[fc % 2]['sig']
                yg = gtmp[fc % 2]['yg']
                sq = gtmp[fc % 2]['sq']
                act(sig.ap[:, :Tt], gp_.ap[:, :Tt], AF.Sigmoid, [gp_, cvec], [sig], bias=bglu[:, fc:fc + 1])
                tt('dve', yg.ap[:, :Tt], glb[fc].ap[:, :Tt], sig.ap[:, :Tt], ALU.mult, [glb[fc], sig], [yg])
                act(sq.ap[:, :Tt], yg.ap[:, :Tt], AF.Square, [yg], [sq])
                mm(ss2.ap[:, :Tt], ones1k.ap, sq.ap[:, :Tt], fc == 0, fc == 7, [ones1k, sq], [ss2])
                ts('pool', mix_in[8 + fc].ap[:, :Tt], yg.ap[:, :Tt], gs5[:, fc:fc + 1], None, ALU.mult, None, [yg, cvec], [mix_in[8 + fc]])
            finish_rstd(ss2, Tt, rstd2)
            for pr in range(8):
                b, w = pool_load('wout', pr, Wb['wout'][2 * pr:2 * pr + 2].rearrange('c p k j -> p c k j'),
                                 lambda b: b.ap.rearrange('p (c k j) -> p c k j', c=2, k=16))
                for cc in range(2):
                    fc = 2 * pr + cc
                    m1 = pbank(2 * (fc % 2), [512])
                    m2 = pbank(2 * (fc % 2) + 1, [512])
                    for ko in range(8):
                        mm(m1.ap[:, :Tt], w[:, cc, ko, :], mix_in[ko].ap[:, :Tt], ko == 0, ko == 7, [b, mix_in[ko]], [m1])
                    for ko in range(8, 16):
                        mm(m2.ap[:, :Tt], w[:, cc, ko, :], mix_in[ko].ap[:, :Tt], ko == 8, ko == 15, [b, mix_in[ko]], [m2])
                    tA = gtmp[fc % 2]['tA']
                    tB = gtmp[fc % 2]['tB']
                    tt('dve', tA.ap[:, :Tt], m1.ap[:, :Tt], rstd1.ap[:, :Tt], ALU.mult, [m1, rstd1], [tA])
                    tt('dve', tB.ap[:, :Tt], m2.ap[:, :Tt], rstd2.ap[:, :Tt], ALU.mult, [m2, rstd2], [tB])
                    tt('pool', tA.ap[:, :Tt], tA.ap[:, :Tt], tB.ap[:, :Tt], ALU.add, [tA, tB], [tA])
                    tt('pool', xf[fc].ap[:, :Tt], xf[fc].ap[:, :Tt], tA.ap[:, :Tt], ALU.add, [xf[fc], tA], [xf[fc]])

        def init_state_zero(sidx):
            memset('pool', st_halo[sidx].ap, 0.0, [st_halo[sidx]])
            memset('pool', st_S[sidx].ap, 0.0, [st_S[sidx]])
            memset('pool', st_Sb[sidx].ap, 0.0, [st_Sb[sidx]])
            memset('pool', st_h5[sidx].ap, 0.0, [st_h5[sidx]])

        def init_state_from(sidx, b, A):
            t0 = A.top
            cst = A.alloc([3, 128], F32)
            dma('sp', cst.ap[0:16], I['cconv'][b].rearrange('k (c p) -> c k p', p=128), (), [cst])
            pb_ = pbank(0, [3, 16])
            for k in range(3):
                tr(pb_.ap[:, k, :], cst.ap[0:16, k, :], ident_f.ap[0:16, 0:16], [cst, ident_f], [pb_])
            cp('dve', st_halo[sidx].ap.rearrange('p c k -> p k c'), pb_.ap, [pb_], [st_halo[sidx]])
            sst = A.alloc([8, 128], F32)
            dma('sp', sst.ap, I['sssd'][b].rearrange('(c p) n -> p c n', p=128), (), [sst])
            for half in range(2):
                pb2 = pbank(1 + half, [4, 128])
                for j in range(4):
                    tr(pb2.ap[:, j, :], sst.ap[:, half * 4 + j, :], ident_f.ap, [sst, ident_f], [pb2])
                cp('dve', st_S[sidx].ap[:, half * 512:(half + 1) * 512], pb2.ap.rearrange('p a b -> p (a b)'), [pb2], [st_S[sidx]])
            cp('act', st_Sb[sidx].ap, st_S[sidx].ap, [st_S[sidx]], [st_Sb[sidx]])
            h5s = A.alloc([2, 128], F32)
            dma('sp', h5s.ap[0:32, 0, :], I['s5re'][b], (), [h5s])
            dma('sp', h5s.ap[0:32, 1, :], I['s5im'][b], (), [h5s])
            pb3 = pbank(3, [2, 32])
            for k in range(2):
                tr(pb3.ap[:, k, :], h5s.ap[0:32, k, :], ident_f.ap[0:32, 0:32], [h5s, ident_f], [pb3])
            cp('dve', st_h5[sidx].ap, pb3.ap, [pb3], [st_h5[sidx]])
            A.top = t0

        def out_state(sidx, b, okeys, A):
            t0 = A.top
            oc, os_, ore, oim = okeys
            pb_ = pbank(0, [3, 128])
            for k in range(3):
                tr(pb_.ap[0:16, k, :], st_halo[sidx].ap[:, :, k], ident_f.ap, [st_halo[sidx], ident_f], [pb_])
            cst = A.alloc([3, 128], F32)
            cp('dve', cst.ap[0:16], pb_.ap[0:16], [pb_], [cst])
            dma('sp', O[oc][b].rearrange('k (c p) -> c k p', p=128), cst.ap[0:16], [cst], ())
            sst = A.alloc([8, 128], F32)
            for half in range(2):
                pb2 = pbank(1 + half, [4, 128])
                for j in range(4):
                    kc = half * 4 + j
                    tr(pb2.ap[:, j, :], st_S[sidx].ap[:, kc * 128:(kc + 1) * 128], ident_f.ap, [st_S[sidx], ident_f], [pb2])
                cp('dve', sst.ap[:, half * 4:half * 4 + 4, :], pb2.ap, [pb2], [sst])
            dma('sp', O[os_][b].rearrange('(c p) n -> p c n', p=128), sst.ap, [sst], ())
            pb3 = pbank(3, [2, 128])
            for k in range(2):
                tr(pb3.ap[0:32, k, :], st_h5[sidx].ap[:, k, :], ident_f.ap, [st_h5[sidx], ident_f], [pb3])
            h5s = A.alloc([2, 128], F32)
            cp('dve', h5s.ap[0:32], pb3.ap[0:32], [pb3], [h5s])
            dma('sp', O[ore][b], h5s.ap[0:32, 0, :], [h5s], ())
            dma('sp', O[oim][b], h5s.ap[0:32, 1, :], [h5s], ())
            A.top = t0

        A = Arena(sb_root, 'sb', SB_SIZE - STG)

        def run_tile(src_rows, dst_rows, Tt, segs):
            A.top = arena0
            load_tile(src_rows, Tt, A)
            ckpt('load')
            A.top = arena0
            ffn(Tt, '1', g1c, A)
            ckpt('ffn1')
            A.top = arena0
            mixer(Tt, segs, A)
            ckpt('mixer')
            A.top = arena0
            ffn(Tt, '2', g2c, A)
            if t0['active']:
                t0_finish()
            ckpt('ffn2')
            A.top = arena0
            store_tile(dst_rows, Tt, A)

        for b in range(NSEQ):
            init_state_zero(0)
            for ti in range(LP // 512):
                r0 = b * LP + ti * 512
                run_tile(I['xp'][r0:r0 + 512], O['yp'][r0:r0 + 512], 512, [(0, 0, 512)])
            A.top = arena0
            out_state(0, b, ('convp', 'ssdp', 's5rep', 's5imp'), A)
        if NS:
            A.top = arena0
            for s_ in range(NS):
                init_state_from(s_, s_, A)
            run_tile(I['xs'], O['ys'], NS * LS, [(s_, s_ * LS, LS) for s_ in range(NS)])
            A.top = arena0
            for s_ in range(NS):
                out_state(s_, s_, ('convs', 'ssds', 's5res', 's5ims'), A)

    import contextlib
    with contextlib.ExitStack() as es:
        sb_root = es.enter_context(nc.sbuf_tensor("arena", [128, SB_SIZE // 4], F32))
        ps_root = es.enter_context(nc.psum_tensor("psum", [128, 4096], F32))
        esem = {e: es.enter_context(nc.semaphore("e_" + e)) for e in ENGS}
        dsem = {(q, i): es.enter_context(nc.semaphore("d_%s%d" % (q, i))) for q in ('sp', 'pool') for i in range(NDS)}
        try:
            program(sb_root, ps_root)
        except _Stop:
            pass
        block = es.enter_context(nc.Block())
        S.emit(nc, block, esem, dsem)
    ctx['nops'] = {e: len(S.ops[e]) for e in ENGS}
    return nc, ctx


def prep_weights(p):
    f = np.float32
    W = {}

    def gu(wg, wu):
        g = wg.reshape(16, 128, 43, 128).transpose(2, 1, 0, 3)
        u = wu.reshape(16, 128, 43, 128).transpose(2, 1, 0, 3)
        return np.ascontiguousarray(np.stack([g, u], axis=2))

    def dn(wd):
        return np.ascontiguousarray(wd.reshape(43, 128, 4, 512).transpose(2, 0, 1, 3))

    def col(v, n):
        return np.ascontiguousarray(np.asarray(v, f).reshape(n, 128).T)
    W['wgu1'] = gu(p['w_ffn1_gate'][0], p['w_ffn1_up'][0])
    W['wd1'] = dn(p['w_ffn1_down'][0])
    W['wgu2'] = gu(p['w_ffn2_gate'][0], p['w_ffn2_up'][0])
    W['wd2'] = dn(p['w_ffn2_down'][0])
    win = p['w_in'][0]
    main = np.concatenate([win[:, 0:3072], win[:, 3088:4112]], axis=1)
    W['win'] = np.ascontiguousarray(main.reshape(16, 128, 32, 128).transpose(2, 1, 0, 3))
    W['wdt'] = np.ascontiguousarray(win[:, 3072:3088].reshape(16, 128, 16).transpose(1, 0, 2))
    W['wout'] = np.ascontiguousarray(p['w_out'][0].reshape(16, 128, 16, 128).transpose(2, 1, 0, 3))
    W['wglu'] = np.ascontiguousarray(p['w_glu'][0].reshape(2, 4, 128, 1024).transpose(0, 2, 1, 3))
    W['g1c'] = col(p['norm_ffn1'][0], 16)
    W['gmc'] = col(p['norm_mix'][0], 16)
    W['g2c'] = col(p['norm_ffn2'][0], 16)
    W['gfc'] = col(p['norm_final'], 16)
    W['gssd'] = col(p['norm_ssd'][0], 8)
    W['gs5'] = col(p['norm_s5'][0], 8)
    W['convw'] = np.ascontiguousarray(p['conv_w'][0].reshape(4, 16, 128).transpose(2, 1, 0))
    W['convb'] = col(p['conv_b'][0], 16)
    W['dssd'] = col(np.repeat(p['d_ssd'][0], 64), 8)
    W['bglu'] = col(p['b_glu'][0], 8)
    W['d5'] = col(p['s5_d'][0].reshape(-1), 8)
    W['dtb'] = np.ascontiguousarray(p['dt_bias'][0].reshape(1, 16))
    W['alog'] = np.ascontiguousarray(p['a_log'][0].reshape(1, 16))
    W['tau'] = np.arange(1, 129, dtype=f).reshape(1, 128)
    W['lrc'] = col(p['s5_lambda_re'][0].reshape(-1), 32)
    W['lic'] = col(p['s5_lambda_im'][0].reshape(-1), 32)
    W['lsc'] = col(np.repeat(p['s5_log_step'][0], 64), 32)
    bre, bim = p['s5_b_re'][0], p['s5_b_im'][0]
    cre, cim = p['s5_c_re'][0], p['s5_c_im'][0]
    wbre = np.zeros((128, 32, 128), f)
    wbim = np.zeros((128, 32, 128), f)
    wcre = np.zeros((128, 32, 128), f)
    wcim = np.zeros((128, 32, 128), f)
    for sc in range(32):
        for gl in range(2):
            g = 2 * sc + gl
            r0 = (sc % 4) * 32 + gl * 16
            wbre[r0:r0 + 16, sc, gl * 64:(gl + 1) * 64] = bre[g].T
            wbim[r0:r0 + 16, sc, gl * 64:(gl + 1) * 64] = bim[g].T
            wcre[gl * 64:(gl + 1) * 64, sc, r0:r0 + 16] = cre[g].T
            wcim[gl * 64:(gl + 1) * 64, sc, r0:r0 + 16] = cim[g].T
    W['wbre'], W['wbim'], W['wcre'], W['wcim'] = wbre, wbim, wcre, wcim
    for k in W:
        W[k] = np.ascontiguousarray(W[k], dtype=f)
        assert list(W[k].shape) == INPUT_SHAPES[k], (k, W[k].shape)
    return W


_CACHE = {}


def kernel(**inputs):
    p = {k: np.asarray(v) for k, v in inputs.items()}
    NCORES = 8
    cfg = {'nseq': 2, 'lp': 2048, 'ns': 2}
    W = prep_weights(p)
    nc, _ = build(cfg)
    in_maps = []
    for c in range(NCORES):
        m = dict(W)
        m['xp'] = np.ascontiguousarray(p['x_prompt'][2 * c:2 * c + 2].reshape(4096, 2048))
        m['xs'] = np.ascontiguousarray(p['x_sample'][2 * c:2 * c + 2].reshape(32, 2048))
        m['cconv'] = np.ascontiguousarray(p['cache_conv'][0, 2 * c:2 * c + 2])
        m['sssd'] = np.ascontiguousarray(p['state_ssd'][0, 2 * c:2 * c + 2].reshape(2, 1024, 128))
        m['s5re'] = np.ascontiguousarray(p['state_s5_re'][0, 2 * c:2 * c + 2].reshape(2, 32, 128))
        m['s5im'] = np.ascontiguousarray(p['state_s5_im'][0, 2 * c:2 * c + 2].reshape(2, 32, 128))
        in_maps.append(m)
    res = run_bass_kernel_spmd(nc, in_maps, core_ids=list(range(NCORES)))
    R = res.results

    def cat(key, shape):
        return np.concatenate([np.asarray(R[c][key], np.float32).reshape(shape) for c in range(NCORES)], axis=0)
    y_prompt = cat('yp', (2, 2048, 2048))
    y_sample = cat('ys', (2, 16, 2048))
    conv_p = cat('convp', (2, 3, 2048))[None]
    ssd_p = cat('ssdp', (2, 16, 64, 128))[None]
    s5re_p = cat('s5rep', (2, 64, 64))[None]
    s5im_p = cat('s5imp', (2, 64, 64))[None]
    conv_s = cat('convs', (2, 3, 2048))[None]
    ssd_s = cat('ssds', (2, 16, 64, 128))[None]
    s5re_s = cat('s5res', (2, 64, 64))[None]
    s5im_s = cat('s5ims', (2, 64, 64))[None]
    return (y_prompt, y_sample, conv_p, ssd_p, s5re_p, s5im_p, conv_s, ssd_s, s5re_s, s5im_s)
```

```python
import numpy as np
import concourse.bass as bass
import concourse.mybir as mybir
from concourse.bass_utils import run_bass_kernel_spmd

F32 = mybir.dt.float32
BF16 = mybir.dt.bfloat16
I32 = mybir.dt.int32
ALU = mybir.AluOpType
AF = mybir.ActivationFunctionType
ENGS = ['pe', 'act', 'dve', 'pool', 'sp']
NDS = 12
BLK = 512
EPS = 1e-6
TWO_PI = 6.283185307179586


def _dtsize(dt):
    return 4 if dt in (F32, I32) else 2


def _prod(s):
    r = 1
    for v in s:
        r *= v
    return r


class Buf:
    def __init__(s, root, space, off, nbytes, dt, shape):
        s.root, s.space, s.off, s.nbytes, s.dt, s.shape = root, space, off, nbytes, dt, tuple(shape)
        blk = 2048 if space == 'ps' else BLK
        s.blocks = [(space, b) for b in range(off // blk, (off + nbytes - 1) // blk + 1)]
        s._ap = None

    @property
    def ap(s):
        if s._ap is None:
            a = s.root[:, s.off // 4:(s.off + s.nbytes) // 4]
            if s.dt != F32:
                a = a.bitcast(s.dt)
            if len(s.shape) > 1:
                names = ' '.join('d%d' % i for i in range(len(s.shape)))
                a = a.rearrange('p (%s) -> p %s' % (names, names),
                                **{'d%d' % i: s.shape[i] for i in range(len(s.shape))})
            s._ap = a
        return s._ap

    def __getitem__(s, i):
        inner = _prod(s.shape[1:]) * _dtsize(s.dt)
        return Buf(s.root, s.space, s.off + i * inner, inner, s.dt, s.shape[1:] or (1,))

    def view(s, dt, shape):
        assert _prod(shape) * _dtsize(dt) <= s.nbytes
        return Buf(s.root, s.space, s.off, _prod(shape) * _dtsize(dt), dt, shape)


class Arena:
    def __init__(s, root, space, size):
        s.root, s.space, s.size, s.top = root, space, size, 0

    def alloc(s, shape, dt, align=BLK):
        off = (s.top + align - 1) // align * align
        nb = _prod(shape) * _dtsize(dt)
        nb4 = (nb + 3) // 4 * 4
        s.top = off + nb4
        assert s.top <= s.size, "arena %s overflow: %d > %d" % (s.space, s.top, s.size)
        return Buf(s.root, s.space, off, nb4, dt, shape)

    def at(s, off, shape, dt):
        nb = _prod(shape) * _dtsize(dt)
        assert off + nb <= s.size
        return Buf(s.root, s.space, off, nb, dt, shape)


class _Stop(Exception):
    pass


class DRes:
    def __init__(s, name, i=0):
        s.blocks = [('dr', name, i)]


class Op:
    __slots__ = ('fn', 'waits', 'signal', 'cnt', 'dma')

    def __init__(s, fn, dma):
        s.fn, s.waits, s.signal, s.cnt, s.dma = fn, [], False, 0, dma


class Sched:
    def __init__(s):
        s.ops = {e: [] for e in ENGS}
        s.blocks = {}
        s.seen = {e: {} for e in ENGS}
        s.ndma = {'sp': 0, 'pool': 0, 'act': 0}
        s.dval = {}

    def op(s, eng, fn, r=(), w=(), dma=False):
        idx = len(s.ops[eng])
        o = Op(fn, None)
        deps = {}

        def add(tok):
            k = tok[:-1]
            if deps.get(k, -1) < tok[-1]:
                deps[k] = tok[-1]
        for x in r:
            for b in x.blocks:
                st = s.blocks.get(b)
                if st and st[0]:
                    add(st[0])
        for x in w:
            for b in x.blocks:
                st = s.blocks.get(b)
                if st:
                    if st[0]:
                        add(st[0])
                    for t in st[1].values():
                        add(t)
        seen = s.seen[eng]
        if dma:
            i = s.ndma[eng] % NDS
            s.ndma[eng] += 1
            val = s.dval.get((eng, i), 0) + 16
            s.dval[(eng, i)] = val
            if val > 16:
                add(('d', eng, i, val - 16))
            tok = ('d', eng, i, val)
            o.dma = (eng, i)
        else:
            tok = ('e', eng, idx)
        for k, v in deps.items():
            if k[0] == 'e' and k[1] == eng:
                if eng in ('pe', 'sp'):
                    continue
                if v < idx - 4:
                    continue
            if seen.get(k, -1) >= v:
                continue
            seen[k] = v
            if k[0] == 'e':
                s.ops[k[1]][v].signal = True
            o.waits.append((k, v))
        s.ops[eng].append(o)
        key = tok[:-1]
        for x in r:
            for b in x.blocks:
                st = s.blocks.setdefault(b, [None, {}])
                st[1][key] = tok
        for x in w:
            for b in x.blocks:
                s.blocks[b] = [tok, {}]
        return tok

    def emit(s, nc, block, esem, dsem):
        for e in ENGS:
            c = 0
            for o in s.ops[e]:
                if o.signal:
                    c += 1
                o.cnt = c

        def run(eng_name):
            def body(e):
                for o in s.ops[eng_name]:
                    for k, v in o.waits:
                        if k[0] == 'e':
                            e.wait_ge(esem[k[1]], s.ops[k[1]][v].cnt)
                        else:
                            e.wait_ge(dsem[(k[1], k[2])], v)
                    ins = o.fn(e)
                    if o.dma:
                        ins.then_inc(dsem[o.dma], 16)
                    elif o.signal:
                        ins.then_inc(esem[eng_name], 1)
                if eng_name == 'sp':
                    for (q, i), v in s.dval.items():
                        e.wait_ge(dsem[(q, i)], v)
            return body
        block.tensor(run('pe'))
        block.scalar(run('act'))
        block.vector(run('dve'))
        block.gpsimd(run('pool'))
        block.sync(run('sp'))


NFC = 43
WD_GROUPS = [(0, 8), (8, 8), (16, 8), (24, 8), (32, 8), (40, 3)]
NSLOT = 8

INPUT_SHAPES = {
    'wgu1': [43, 128, 2, 16, 128], 'wd1': [4, 43, 128, 512],
    'wgu2': [43, 128, 2, 16, 128], 'wd2': [4, 43, 128, 512],
    'win': [32, 128, 16, 128], 'wdt': [128, 16, 16], 'wout': [16, 128, 16, 128], 'wglu': [2, 128, 4, 1024],
    'g1c': [128, 16], 'gmc': [128, 16], 'g2c': [128, 16], 'gfc': [128, 16],
    'gssd': [128, 8], 'gs5': [128, 8], 'convw': [128, 16, 4], 'convb': [128, 16],
    'dssd': [128, 8], 'bglu': [128, 8], 'd5': [128, 8],
    'dtb': [1, 16], 'alog': [1, 16], 'tau': [1, 128],
    'lrc': [128, 32], 'lic': [128, 32], 'lsc': [128, 32],
    'wbre': [128, 32, 128], 'wbim': [128, 32, 128], 'wcre': [128, 32, 128], 'wcim': [128, 32, 128],
}


def build(cfg):
    NSEQ, LP, NS = cfg['nseq'], cfg['lp'], cfg['ns']
    LS = 16
    nc = bass.Bass("TRN2", target_bir_lowering=False)

    def din(name, shape):
        return nc.dram_tensor(name, list(shape), F32, kind="ExternalInput").ap()

    def dout(name, shape):
        return nc.dram_tensor(name, list(shape), F32, kind="ExternalOutput").ap()

    def dscr(name, shape, dt):
        return nc.dram_tensor(name, list(shape), dt, kind="Internal").ap()

    I = {k: din(k, v) for k, v in INPUT_SHAPES.items()}
    I['xp'] = din('xp', [NSEQ * LP, 2048])
    O = {'yp': dout('yp', [NSEQ * LP, 2048]), 'convp': dout('convp', [NSEQ, 3, 2048]),
         'ssdp': dout('ssdp', [NSEQ, 1024, 128]), 's5rep': dout('s5rep', [NSEQ, 32, 128]),
         's5imp': dout('s5imp', [NSEQ, 32, 128])}
    if NS:
        I['xs'] = din('xs', [NS * LS, 2048])
        I['cconv'] = din('cconv', [NS, 3, 2048])
        I['sssd'] = din('sssd', [NS, 1024, 128])
        I['s5re'] = din('s5re', [NS, 32, 128])
        I['s5im'] = din('s5im', [NS, 32, 128])
        O.update({'ys': dout('ys', [NS * LS, 2048]), 'convs': dout('convs', [NS, 3, 2048]),
                  'ssds': dout('ssds', [NS, 1024, 128]), 's5res': dout('s5res', [NS, 32, 128]),
                  's5ims': dout('s5ims', [NS, 32, 128])})
    Wb = {k: dscr(k + '_b', INPUT_SHAPES[k], BF16) for k in ['wgu1', 'wd1', 'wgu2', 'wd2', 'win', 'wout', 'wglu']}
    Wb['wbbre'] = dscr('wbbre_b', [128, 32, 128], BF16)
    Wb['wbbim'] = dscr('wbbim_b', [128, 32, 128], BF16)
    Wb['wcre'] = dscr('wcre_b', [128, 32, 128], BF16)
    Wb['wcimn'] = dscr('wcimn_b', [128, 32, 128], BF16)
    Wb['wcren'] = dscr('wcren_b', [128, 32, 128], BF16)
    Wb['tcos'] = dscr('tcos', [128, 32, 128], BF16)
    Wb['tsin'] = dscr('tsin', [128, 32, 128], BF16)

    S = Sched()
    SB_SIZE = 207 * 1024
    STG = 16 * 1024
    ctx = {}

    def ckpt(stage):
        if cfg.get('upto') == stage:
            raise _Stop()

    def program(sb_root, ps_root):
        sb = Arena(sb_root, 'sb', SB_SIZE)
        ps = Arena(ps_root, 'ps', 16384)

        def pbank(bank, shape, dt=F32, coloff=0):
            return ps.at(bank * 2048 + coloff, shape, dt)

        def tt(eng, out, in0, in1, op, r, w):
            S.op(eng, lambda e: e.tensor_tensor(out=out, in0=in0, in1=in1, op=op), r, w)

        def stt(eng, out, in0, scalar, in1, op0, op1, r, w):
            S.op(eng, lambda e: e.scalar_tensor_tensor(out=out, in0=in0, scalar=scalar, in1=in1, op0=op0, op1=op1), r, w)

        def ts(eng, out, in0, s1, s2, op0, op1, r, w):
            if s2 is None:
                S.op(eng, lambda e: e.tensor_scalar(out=out, in0=in0, scalar1=s1, scalar2=None, op0=op0), r, w)
            else:
                S.op(eng, lambda e: e.tensor_scalar(out=out, in0=in0, scalar1=s1, scalar2=s2, op0=op0, op1=op1), r, w)

        def act(out, in_, func, r, w, bias=None, scale=None):
            kw = {}
            if bias is not None:
                kw['bias'] = bias
            if scale is not None:
                kw['scale'] = scale
            S.op('act', lambda e: e.activation(out=out, in_=in_, func=func, **kw), r, w)

        def cp(eng, out, in_, r, w):
            if eng == 'act':
                S.op('act', lambda e: e.activation(out=out, in_=in_, func=AF.Copy), r, w)
            else:
                S.op(eng, lambda e: e.tensor_copy(out=out, in_=in_), r, w)

        def mm(out, lhsT, rhs, start, stop, r, w):
            S.op('pe', lambda e: e.matmul(out, lhsT=lhsT, rhs=rhs, start=start, stop=stop), r, w)

        def tr(out, in_, ident, r, w):
            S.op('pe', lambda e: e.transpose(out, in_, ident), r, w)

        def dma(q, out, in_, r, w):
            S.op(q, lambda e: e.dma_start(out=out, in_=in_), r, w, dma=True)

        def scan(out, d0, d1, init, r, w):
            S.op('dve', lambda e: e.tensor_tensor_scan(out=out, data0=d0, data1=d1, initial=init,
                                                       op0=ALU.mult, op1=ALU.add), r, w)

        def memset(eng, ap, val, w):
            S.op(eng, lambda e: e.memset(ap, val), (), w)

        xf = sb.alloc([16, 512], F32)
        xn = sb.alloc([16, 512], BF16)
        slots = [sb.alloc([4096], BF16) for _ in range(NSLOT)]
        slot_ctr = [0]

        def next_slot():
            b = slots[slot_ctr[0] % NSLOT]
            slot_ctr[0] += 1
            return b

        ident_f = sb.alloc([128], F32)
        ident_b = sb.alloc([128], BF16)
        UL = sb.alloc([128], F32)
        SL = sb.alloc([128], F32)
        ones_f = sb.alloc([128], F32)
        onesD = sb.alloc([128], BF16)
        ones1k = sb.alloc([128], BF16)
        cvec = sb.alloc([16 * 4 + 8 * 5 + 16 * 4 + 16], F32)
        co = [0]

        def cslice(n):
            a = cvec.ap[:, co[0]:co[0] + n]
            co[0] += n
            return a
        g1c, gmc, g2c, gfc = cslice(16), cslice(16), cslice(16), cslice(16)
        gssd, gs5, dssd, bglu, d5 = cslice(8), cslice(8), cslice(8), cslice(8), cslice(8)
        convw_ap = cslice(64)
        convb = cslice(16)
        convw = convw_ap.rearrange('p (c k) -> p c k', k=4)
        rowc = sb.alloc([16 + 16 + 128], F32)
        dtb_bc, A_bc, tau_bc = rowc.ap[:, 0:16], rowc.ap[:, 16:32], rowc.ap[:, 32:160]
        s5c = sb.alloc([32 * 9], F32)
        mag_c = s5c.ap[:, 0:32]
        cFs = {('tcos', 128): s5c.ap[:, 32:64], ('tsin', 128): s5c.ap[:, 64:96],
               ('tcos', 16): s5c.ap[:, 96:128], ('tsin', 16): s5c.ap[:, 128:160],
               ('ss', 128): s5c.ap[:, 160:224].rearrange('p (s c) -> p s c', c=2),
               ('ss', 16): s5c.ap[:, 224:288].rearrange('p (s c) -> p s c', c=2)}
        identn_b = sb.alloc([128], BF16)
        Ddiag5 = sb.alloc([8, 128], BF16)
        wdt_b = sb.alloc([16, 16], BF16)
        nstate = max(1, NS)
        st_halo = [sb.alloc([16, 3], F32) for _ in range(nstate)]
        st_S = [sb.alloc([1024], F32) for _ in range(nstate)]
        st_Sb = [sb.alloc([1024], BF16) for _ in range(nstate)]
        st_h5 = [sb.alloc([2, 32], F32) for _ in range(nstate)]
        arena0 = sb.top

        wres = {}
        units = []
        upos = {}

        def cast_unit(name, idx, dst, src, fshape):
            wres[(name, idx)] = DRes(name, idx)
            upos[(name, idx)] = len(units)
            units.append((name, idx, dst, src, list(fshape)))

        def cast_ffn(tag):
            for c in range(NFC):
                cast_unit('wgu' + tag, c, Wb['wgu' + tag][c], I['wgu' + tag][c], [2, 16, 128])
            for fg in range(4):
                for gi, (c0, n) in enumerate(WD_GROUPS):
                    cast_unit('wd' + tag, fg * 6 + gi,
                              Wb['wd' + tag][fg, c0:c0 + n].rearrange('c p j -> p c j'),
                              I['wd' + tag][fg, c0:c0 + n].rearrange('c p j -> p c j'), [n, 512])
        cast_ffn('1')
        for pr in range(16):
            cast_unit('win', pr, Wb['win'][2 * pr:2 * pr + 2].rearrange('c p k j -> p c k j'),
                      I['win'][2 * pr:2 * pr + 2].rearrange('c p k j -> p c k j'), [2, 16, 128])
        for hh in range(2):
            cast_unit('wglu', hh, Wb['wglu'][hh], I['wglu'][hh], [4, 1024])
        for pr in range(8):
            cast_unit('wout', pr, Wb['wout'][2 * pr:2 * pr + 2].rearrange('c p k j -> p c k j'),
                      I['wout'][2 * pr:2 * pr + 2].rearrange('c p k j -> p c k j'), [2, 16, 128])
        cast_ffn('2')
        ckpt('cast')

        cres = [cvec]
        o = 0
        for nm, n in [('g1c', 16), ('gmc', 16), ('g2c', 16), ('gfc', 16), ('gssd', 8), ('gs5', 8), ('dssd', 8),
                      ('bglu', 8), ('d5', 8)]:
            dma('sp', cvec.ap[:, o:o + n], I[nm], (), cres)
            o += n
        dma('sp', cvec.ap[:, o:o + 64], I['convw'].rearrange('p c k -> p (c k)'), (), cres)
        o += 64
        dma('sp', cvec.ap[:, o:o + 16], I['convb'], (), cres)
        dma('sp', rowc.ap[:, 0:16], I['dtb'].partition_broadcast(128), (), [rowc])
        dma('sp', rowc.ap[:, 16:32], I['alog'].partition_broadcast(128), (), [rowc])
        dma('sp', rowc.ap[:, 32:160], I['tau'].partition_broadcast(128), (), [rowc])
        act(A_bc, A_bc, AF.Exp, [rowc], [rowc])
        ts('dve', A_bc, A_bc, -1.0, None, ALU.mult, None, [rowc], [rowc])
        memset('pool', ones_f.ap, 1.0, [ones_f])
        memset('pool', ident_f.ap, 0.0, [ident_f])
        S.op('pool', lambda e: e.affine_select(out=ident_f.ap, in_=ident_f.ap, pattern=[[-1, 128]],
                                               compare_op=ALU.not_equal, fill=1.0, base=0, channel_multiplier=1),
             [ident_f], [ident_f])
        S.op('pool', lambda e: e.affine_select(out=UL.ap, in_=ones_f.ap, pattern=[[1, 128]],
                                               compare_op=ALU.is_ge, fill=0.0, base=0, channel_multiplier=-1),
             [ones_f], [UL])
        S.op('pool', lambda e: e.affine_select(out=SL.ap, in_=ones_f.ap, pattern=[[-1, 128]],
                                               compare_op=ALU.is_gt, fill=0.0, base=0, channel_multiplier=1),
             [ones_f], [SL])
        cp('dve', ident_b.ap, ident_f.ap, [ident_f], [ident_b])
        ts('dve', identn_b.ap, ident_f.ap, -1.0, None, ALU.mult, None, [ident_f], [identn_b])
        memset('dve', onesD.ap, 1.0 / 2048.0, [onesD])
        memset('dve', ones1k.ap, 1.0 / 1024.0, [ones1k])
        tmpA = Arena(sb_root, 'sb', SB_SIZE)
        tmpA.top = arena0
        wdt_f = tmpA.alloc([16, 16], F32)
        dma('sp', wdt_f.ap, I['wdt'], (), [wdt_f])
        cp('dve', wdt_b.ap, wdt_f.ap, [wdt_f], [wdt_b])
        for kc in range(8):
            ts('dve', Ddiag5[kc].ap, ident_f.ap, d5[:, kc:kc + 1], None, ALU.mult, None, [ident_f, cvec], [Ddiag5[kc]])

        ckpt('consts')
        c5 = tmpA.alloc([16, 32], F32)
        ci5 = tmpA.alloc([32], I32)

        def C5(i):
            return c5.ap[:, i, :]
        LR, LI, LS_, STEP, ANG, LBR, LBI, T0, T1, QRE, QIM, T2, T3 = range(13)
        dma('sp', C5(LR), I['lrc'], (), [c5])
        dma('sp', C5(LI), I['lic'], (), [c5])
        dma('sp', C5(LS_), I['lsc'], (), [c5])
        R5 = [c5, ci5]
        act(C5(STEP), C5(LS_), AF.Exp, R5, R5)
        tt('dve', C5(T0), C5(LR), C5(STEP), ALU.mult, R5, R5)
        act(mag_c, C5(T0), AF.Exp, R5, [s5c] + R5)
        tt('dve', C5(ANG), C5(LI), C5(STEP), ALU.mult, R5, R5)

        def sincos(dst, src, shift, ki, tmp, R):
            ts('dve', tmp, src, shift, 1.0 / TWO_PI, ALU.add, ALU.mult, R, R)
            cp('dve', ki, tmp, R, R)
            cp('dve', tmp, ki, R, R)
            stt('dve', tmp, tmp, -TWO_PI, src, ALU.mult, ALU.add, R, R)
            ts('dve', tmp, tmp, shift, None, ALU.add, None, R, R)
            ts('dve', tmp, tmp, -3.141592, 3.141592, ALU.max, ALU.min, R, R)
            act(dst, tmp, AF.Sin, R, R)
        sincos(C5(LBI), C5(ANG), 0.0, ci5.ap, C5(T1), R5)
        sincos(C5(LBR), C5(ANG), 1.5707963267948966, ci5.ap, C5(T1), R5)
        tt('dve', C5(LBR), C5(LBR), mag_c, ALU.mult, R5 + [s5c], R5)
        tt('dve', C5(LBI), C5(LBI), mag_c, ALU.mult, R5 + [s5c], R5)
        tt('dve', C5(T0), C5(LR), C5(LR), ALU.mult, R5, R5)
        tt('dve', C5(T1), C5(LI), C5(LI), ALU.mult, R5, R5)
        tt('dve', C5(T0), C5(T0), C5(T1), ALU.add, R5, R5)
        S.op('dve', lambda e: e.reciprocal(out=C5(T0), in_=C5(T0)), R5, R5)
        ts('dve', C5(T1), C5(LBR), -1.0, None, ALU.add, None, R5, R5)
        tt('dve', C5(T2), C5(T1), C5(LR), ALU.mult, R5, R5)
        tt('dve', C5(T3), C5(LBI), C5(LI), ALU.mult, R5, R5)
        tt('dve', C5(T2), C5(T2), C5(T3), ALU.add, R5, R5)
        tt('dve', C5(QRE), C5(T2), C5(T0), ALU.mult, R5, R5)
        tt('dve', C5(T2), C5(LBI), C5(LR), ALU.mult, R5, R5)
        tt('dve', C5(T3), C5(T1), C5(LI), ALU.mult, R5, R5)
        tt('dve', C5(T2), C5(T2), C5(T3), ALU.subtract, R5, R5)
        tt('dve', C5(QIM), C5(T2), C5(T0), ALU.mult, R5, R5)

        ckpt('s5a')
        wbre_f = tmpA.alloc([32, 128], F32)
        wbim_f = tmpA.alloc([32, 128], F32)
        wbbre_o = tmpA.alloc([32, 128], BF16)
        wbbim_o = tmpA.alloc([32, 128], BF16)
        dma('sp', wbre_f.ap, I['wbre'], (), [wbre_f])
        dma('sp', wbim_f.ap, I['wbim'], (), [wbim_f])
        dg = [tmpA.alloc([2, 128], F32) for _ in range(2)]
        tq = [tmpA.alloc([4, 128], F32) for _ in range(2)]
        for sc in range(32):
            d_ = dg[sc % 2]
            t_ = tq[sc % 2]
            qps = pbank(sc % 4, [2, 128])
            ts('dve', d_[0].ap, ident_f.ap, C5(QRE)[:, sc:sc + 1], None, ALU.mult, None, [ident_f, c5], [d_[0]])
            ts('dve', d_[1].ap, ident_f.ap, C5(QIM)[:, sc:sc + 1], None, ALU.mult, None, [ident_f, c5], [d_[1]])
            mm(qps[0].ap, ones_f.ap, d_[0].ap, True, True, [ones_f, d_[0]], [qps[0]])
            mm(qps[1].ap, ones_f.ap, d_[1].ap, True, True, [ones_f, d_[1]], [qps[1]])
            tt('dve', t_[0].ap, wbre_f[sc].ap, qps[0].ap, ALU.mult, [wbre_f[sc], qps[0]], [t_[0]])
            tt('dve', t_[1].ap, wbim_f[sc].ap, qps[1].ap, ALU.mult, [wbim_f[sc], qps[1]], [t_[1]])
            tt('dve', t_[2].ap, wbre_f[sc].ap, qps[1].ap, ALU.mult, [wbre_f[sc], qps[1]], [t_[2]])
            tt('dve', t_[3].ap, wbim_f[sc].ap, qps[0].ap, ALU.mult, [wbim_f[sc], qps[0]], [t_[3]])
            tt('pool', wbbre_o[sc].ap, t_[0].ap, t_[1].ap, ALU.subtract, [t_[0], t_[1]], [wbbre_o[sc]])
            tt('pool', wbbim_o[sc].ap, t_[2].ap, t_[3].ap, ALU.add, [t_[2], t_[3]], [wbbim_o[sc]])
        for nm, bf in [('wbbre', wbbre_o), ('wbbim', wbbim_o)]:
            rs = DRes(nm)
            wres[(nm, 0)] = rs
            dma('sp', Wb[nm], bf.ap, [bf], [rs])
        ckpt('s5b')
        tmpA.top = wbre_f.off
        for nm, src, scale in [('wcre', 'wcre', 1.0), ('wcren', 'wcre', -1.0), ('wcimn', 'wcim', -1.0)]:
            wf_ = tmpA.alloc([32, 128], F32)
            wo_ = tmpA.alloc([32, 128], BF16)
            dma('sp', wf_.ap, I[src], (), [wf_])
            act(wo_.ap, wf_.ap, AF.Copy, [wf_], [wo_], scale=scale)
            rs = DRes(nm)
            wres[(nm, 0)] = rs
            dma('sp', Wb[nm], wo_.ap, [wo_], [rs])
            tmpA.top = wbre_f.off
        ckpt('s5c')
        tmpA.top = wbre_f.off
        trs = {nm: DRes(nm) for nm in ('tsin', 'tcos')}
        for nm in trs:
            wres[(nm, 0)] = trs[nm]
        for half in range(2):
            av = tmpA.alloc([16, 128], F32)
            tv = tmpA.alloc([16, 128], F32)
            ov = tmpA.alloc([16, 128], F32)
            kv = tmpA.alloc([16, 128], I32)
            ob = [tmpA.alloc([16, 128], BF16) for _ in range(2)]
            Rt = [av, tv, ov, kv]
            hs = slice(half * 16, (half + 1) * 16)
            tt('dve', av.ap, C5(ANG)[:, hs].unsqueeze(2).to_broadcast([128, 16, 128]),
               tau_bc.unsqueeze(1).to_broadcast([128, 16, 128]), ALU.mult, [c5, rowc], Rt)
            for ti_, (nm, shift) in enumerate([('tsin', 0.0), ('tcos', 1.5707963267948966)]):
                sincos(ov.ap, av.ap, shift, kv.ap, tv.ap, Rt)
                for frl in (128, 16):
                    cp('dve', cFs[(nm, frl)][:, hs], ov.ap[:, :, frl - 1], [ov], [s5c])
                    if nm == 'tsin':
                        ts('dve', cFs[('ss', frl)][:, hs, 0], ov.ap[:, :, frl - 1], -1.0, None, ALU.mult, None, [ov], [s5c])
                        cp('dve', cFs[('ss', frl)][:, hs, 1], ov.ap[:, :, frl - 1], [ov], [s5c])
                cp('act', ob[ti_].ap, ov.ap, [ov], [ob[ti_]])
                dma('sp', Wb[nm][:, hs, :], ob[ti_].ap, [ob[ti_]], [trs[nm]])
            tmpA.top = wbre_f.off

        ckpt('setup')
        stgs = [Buf(sb_root, 'sb', SB_SIZE - STG + i * (STG // 2), STG // 2, F32, [STG // 8]) for i in range(2)]
        t0 = {'active': True, 'emitted': 0, 'slot': {}, 'pending': [], 'k': 0}
        LOOKAHEAD = 2

        def fpat(shape):
            names = ' '.join('d%d' % k for k in range(len(shape)))
            return 'p (%s) -> p %s' % (names, names), {'d%d' % k: shape[k] for k in range(len(shape))}

        def t0_emit(pos):
            name, idx, dst, src, fshape = units[pos]
            b = next_slot()
            parts = []
            if len(fshape) == 3:
                a_, k_, j_ = fshape
                kh = k_ if k_ * j_ <= STG // 8 else k_ // 2
                for ai in range(a_):
                    for k0 in range(0, k_, kh):
                        parts.append((src[:, ai, k0:k0 + kh, :], (ai * k_ + k0) * j_, [kh, j_]))
            else:
                n_, j_ = fshape
                rp = max(1, (STG // 8) // j_)
                for r0 in range(0, n_, rp):
                    r1 = min(n_, r0 + rp)
                    parts.append((src[:, r0:r1, :], r0 * j_, [r1 - r0, j_]))
            for (sap, off, shp) in parts:
                stg = stgs[t0['k'] % 2]
                eng = 'act' if t0['k'] % 2 == 0 else 'dve'
                t0['k'] += 1
                cnt = _prod(shp)
                pat, kw = fpat(shp)
                dma('sp', stg.ap[:, 0:cnt].rearrange(pat, **kw), sap, (), [stg])
                cp(eng, b.ap[:, off:off + cnt], stg.ap[:, 0:cnt], [stg], [b])
            t0['slot'][pos] = b
            t0['pending'].append(pos)

        def t0_flush(keep):
            while len(t0['pending']) > keep:
                pos = t0['pending'].pop(0)
                name, idx, dst, src, fshape = units[pos]
                b = t0['slot'][pos]
                pat, kw = fpat(fshape)
                dma('sp', dst, b.ap[:, 0:_prod(fshape)].rearrange(pat, **kw), [b], [wres[(name, idx)]])

        def t0_finish():
            assert t0['emitted'] == len(units), (t0['emitted'], len(units))
            t0_flush(0)
            t0['active'] = False

        def pool_load(name, idx, src_ap, view):
            if t0['active'] and (name, idx) in upos:
                pos = upos[(name, idx)]
                gend = upos[('wglu', 0)]
                lim = gend if pos < gend else len(units)
                while t0['emitted'] < min(pos + 1 + LOOKAHEAD, lim):
                    t0_emit(t0['emitted'])
                    t0['emitted'] += 1
                    t0_flush(LOOKAHEAD + 1)
                b = t0['slot'][pos]
                return b, view(b)
            if t0['active']:
                t0_flush(0)
            b = next_slot()
            dst = view(b)
            dma('sp', dst, src_ap, [wres[(name, idx)]], [b])
            return b, dst

        def norm_rstd(sqsrc, nchunks, ones_b, Tt, ssb, tmp_sq, rstd):
            for kc in range(nchunks):
                a, rr = sqsrc(kc)
                sq = tmp_sq[kc % 2]
                act(sq.ap[:, :Tt], a, AF.Square, rr, [sq])
                mm(ssb.ap[:, :Tt], ones_b.ap, sq.ap[:, :Tt], kc == 0, kc == nchunks - 1, [ones_b, sq], [ssb])
            finish_rstd(ssb, Tt, rstd)

        def finish_rstd(ssb, Tt, rstd):
            act(rstd.ap[:, :Tt], ssb.ap[:, :Tt], AF.Sqrt, [ssb], [rstd], bias=EPS)
            S.op('dve', lambda e: e.reciprocal(out=rstd.ap[:, :Tt], in_=rstd.ap[:, :Tt]), [rstd], [rstd])

        def ffn(Tt, tag, gcol, A):
            h = A.alloc([NFC, 512], BF16)
            sgt = [A.alloc([512], F32) for _ in range(2)]
            sqt = [A.alloc([512], BF16) for _ in range(2)]
            rstd = A.alloc([512], F32)
            norm_rstd(lambda kc: (xf[kc].ap[:, :Tt], [xf[kc]]), 16, onesD, Tt, pbank(7, [512]), sqt, rstd)
            for kc in range(16):
                stt('dve', xn[kc].ap[:, :Tt], xf[kc].ap[:, :Tt], gcol[:, kc:kc + 1], rstd.ap[:, :Tt],
                    ALU.mult, ALU.mult, [xf[kc], cvec, rstd], [xn[kc]])
            for c in range(NFC):
                b, w = pool_load('wgu' + tag, c, Wb['wgu' + tag][c],
                                 lambda b: b.ap.rearrange('p (g k j) -> p g k j', g=2, k=16))
                gp = pbank(2 * (c % 2), [512])
                up = pbank(2 * (c % 2) + 1, [512])
                for gi, pp in enumerate((gp, up)):
                    for ko in range(16):
                        mm(pp.ap[:, :Tt], w[:, gi, ko, :], xn[ko].ap[:, :Tt], ko == 0, ko == 15, [b, xn[ko]], [pp])
                sg = sgt[c % 2]
                act(sg.ap[:, :Tt], gp.ap[:, :Tt], AF.Silu, [gp], [sg])
                tt('dve', h[c].ap[:, :Tt], up.ap[:, :Tt], sg.ap[:, :Tt], ALU.mult, [up, sg], [h[c]])
            for fg in range(4):
                base = 4 if fg % 2 == 0 else 0
                ops_ = [pbank(base + j, [512]) for j in range(4)]
                for gi, (c0, n) in enumerate(WD_GROUPS):
                    b, w = pool_load('wd' + tag, fg * 6 + gi,
                                     Wb['wd' + tag][fg, c0:c0 + n].rearrange('c p j -> p c j'),
                                     lambda b: b.ap[:, 0:n * 512].rearrange('p (c j) -> p c j', c=n))
                    for ci in range(n):
                        c = c0 + ci
                        for j in range(4):
                            mm(ops_[j].ap[:, :Tt], w[:, ci, j * 128:(j + 1) * 128], h[c].ap[:, :Tt],
                               c == 0, c == NFC - 1, [b, h[c]], [ops_[j]])
                for j in range(4):
                    kc = 4 * fg + j
                    stt('dve', xf[kc].ap[:, :Tt], ops_[j].ap[:, :Tt], 0.5, xf[kc].ap[:, :Tt], ALU.mult, ALU.add,
                        [ops_[j], xf[kc]], [xf[kc]])

        def load_tile(src_rows, Tt, A):
            nb = (Tt + 127) // 128
            stg = [A.alloc([2048], F32) for _ in range(2)]
            for tb in range(nb):
                n = min(128, Tt - tb * 128)
                sg = stg[tb % 2]
                dma('sp', sg.ap[0:n, :], src_rows[tb * 128:tb * 128 + n, :], (), [sg])
                for q4 in range(4):
                    pb_ = pbank((tb * 4 + q4) % 4, [4, 128])
                    for j in range(4):
                        kc = q4 * 4 + j
                        tr(pb_.ap[:, j, 0:n], sg.ap[0:n, kc * 128:(kc + 1) * 128], ident_f.ap[0:n, 0:n], [sg, ident_f], [pb_])
                    dst = xf.ap[:, q4 * 4:q4 * 4 + 4, tb * 128:tb * 128 + n]
                    cp('act' if q4 % 2 == 0 else 'dve', dst, pb_.ap[:, :, 0:n], [pb_], [xf[q4 * 4 + j] for j in range(4)])

        def store_tile(dst_rows, Tt, A):
            nb = (Tt + 127) // 128
            sqt = [A.alloc([512], BF16) for _ in range(2)]
            rstd = A.alloc([512], F32)
            stg = [A.alloc([2048], F32) for _ in range(2)]
            norm_rstd(lambda kc: (xf[kc].ap[:, :Tt], [xf[kc]]), 16, onesD, Tt, pbank(7, [512]), sqt, rstd)
            for kc in range(16):
                stt('dve', xf[kc].ap[:, :Tt], xf[kc].ap[:, :Tt], gfc[:, kc:kc + 1], rstd.ap[:, :Tt],
                    ALU.mult, ALU.mult, [xf[kc], cvec, rstd], [xf[kc]])
            for tb in range(nb):
                n = min(128, Tt - tb * 128)
                sg = stg[tb % 2]
                for q4 in range(4):
                    pb_ = pbank((tb * 4 + q4) % 4, [4, 128])
                    for j in range(4):
                        kc = q4 * 4 + j
                        tr(pb_.ap[0:n, j, :], xf[kc].ap[:, tb * 128:tb * 128 + n], ident_f.ap, [xf[kc], ident_f], [pb_])
                    cp('act' if q4 % 2 == 0 else 'dve', sg.ap[0:n, q4 * 512:(q4 + 1) * 512],
                       pb_.ap[0:n].rearrange('p a b -> p (a b)'), [pb_], [sg])
                dma('act', dst_rows[tb * 128:tb * 128 + n, :], sg.ap[0:n, :], [sg], ())

        def mixer(Tt, segs, A):
            nseg = len(segs)
            L = segs[0][2]
            Q = min(128, L)
            Fr = min(128, L)
            sz = A.alloc([8, 512], BF16)
            u5b = A.alloc([8, 512], BF16)
            rstd1 = A.alloc([512], F32)
            xsb = A.alloc([8, 512], BF16)
            Bb = A.alloc([4, 512], BF16)
            Cb = A.alloc([4, 512], BF16)
            sub0 = A.top
            rawc = [A.alloc([nseg * (L + 3)], F32) for _ in range(2)]
            acc = [A.alloc([512], F32) for _ in range(2)]
            mix_in = xn
            sqt = [acc[0].view(BF16, [512]), acc[1].view(BF16, [512])]
            norm_rstd(lambda kc: (xf[kc].ap[:, :Tt], [xf[kc]]), 16, onesD, Tt, pbank(7, [512]), sqt, rstd1)
            for kc in range(16):
                stt('dve', xn[kc].ap[:, :Tt], xf[kc].ap[:, :Tt], gmc[:, kc:kc + 1], rstd1.ap[:, :Tt],
                    ALU.mult, ALU.mult, [xf[kc], cvec, rstd1], [xn[kc]])
            for pr in range(16):
                b, w = pool_load('win', pr, Wb['win'][2 * pr:2 * pr + 2].rearrange('c p k j -> p c k j'),
                                 lambda b: b.ap.rearrange('p (c k j) -> p c k j', c=2, k=16))
                for cc in range(2):
                    c = 2 * pr + cc
                    pp = pbank(c % 4, [512])
                    for ko in range(16):
                        mm(pp.ap[:, :Tt], w[:, cc, ko, :], xn[ko].ap[:, :Tt], ko == 0, ko == 15, [b, xn[ko]], [pp])
                    if c < 8:
                        act(sz[c].ap[:, :Tt], pp.ap[:, :Tt], AF.Silu, [pp], [sz[c]])
                    elif c < 24:
                        j = c - 8
                        rw = rawc[j % 2]
                        rw3 = rw.ap.rearrange('p (s l) -> p s l', s=nseg)
                        ac = acc[j % 2]
                        ac3 = ac.ap[:, :Tt].rearrange('p (s l) -> p s l', s=nseg)
                        for si, (sidx, c0, _) in enumerate(segs):
                            cp('pool', rw3[:, si, 0:3], st_halo[sidx].ap[:, j, :], [st_halo[sidx]], [rw])
                        cp('act', rw3[:, :, 3:3 + L], pp.ap[:, :Tt].rearrange('p (s l) -> p s l', s=nseg), [pp], [rw])
                        for si, (sidx, c0, _) in enumerate(segs):
                            cp('pool', st_halo[sidx].ap[:, j, :], rw3[:, si, L:L + 3], [rw], [st_halo[sidx]])
                        ts('dve', ac3, rw3[:, :, 3:3 + L], convw[:, j, 3:4], convb[:, j:j + 1], ALU.mult, ALU.add,
                           [rw, cvec], [ac])
                        for k in (2, 1, 0):
                            stt('dve', ac3, rw3[:, :, k:k + L], convw[:, j, k:k + 1], ac3, ALU.mult, ALU.add,
                                [rw, cvec, ac], [ac])
                        dst = xsb[j] if j < 8 else (Bb[j - 8] if j < 12 else Cb[j - 12])
                        act(dst.ap[:, :Tt], ac.ap[:, :Tt], AF.Silu, [ac], [dst])
                    else:
                        cp('dve', u5b[c - 24].ap[:, :Tt], pp.ap[:, :Tt], [pp], [u5b[c - 24]])
            ckpt('inproj')
            A.top = sub0
            dtt = A.alloc([6, 16], F32)
            V2 = [A.alloc([4, Q], F32)] * 2
            Lexp2 = [A.alloc([4, Q], F32)] * 2
            Ebc2 = [A.alloc([4, Q], F32)] * 2
            Mp2 = [A.alloc([4, Q], BF16) for _ in range(2)]
            Cexp2 = [A.alloc([4, Q], BF16) for _ in range(2)]
            CBm = A.alloc([4, Q], F32)
            xdt = A.alloc([16, 64], BF16)
            xdd = A.alloc([16, 64], BF16)
            xdd_flat = xdd.view(BF16, [1024])
            Btm = A.alloc([4, 128], BF16)
            ygf = [A.alloc([Q], F32) for _ in range(2)]
            ytmp = [A.alloc([Q], F32) for _ in range(2)]
            ysq = [A.alloc([Q], BF16) for _ in range(2)]
            ss1 = pbank(7, [512])
            for (sidx, c0, Ls) in segs:
                Sst, Sb = st_S[sidx], st_Sb[sidx]
                for ck in range(Ls // Q):
                    cs = c0 + ck * Q
                    dtp = pbank(3, [16])
                    remp = pbank(3, [16], coloff=512)
                    totp = pbank(3, [16], coloff=1024)
                    for ko in range(16):
                        mm(dtp.ap[0:Q, :], xn[ko].ap[:, cs:cs + Q], wdt_b.ap[:, ko, :], ko == 0, ko == 15, [xn[ko], wdt_b], [dtp])
                    t1, dt_, dtA, dec_end, decay_bc, e1 = [dtt[i] for i in range(6)]
                    tt('dve', t1.ap[0:Q], dtp.ap[0:Q, :], dtb_bc[0:Q], ALU.add, [dtp, rowc], [t1])
                    act(e1.ap[0:Q], t1.ap[0:Q], AF.Exp, [t1], [e1])
                    act(dt_.ap[0:Q], e1.ap[0:Q], AF.Ln, [e1], [dt_], bias=1.0)
                    tt('dve', dtA.ap[0:Q], dt_.ap[0:Q], A_bc[0:Q], ALU.mult, [dt_, rowc], [dtA])
                    mm(remp.ap[0:Q, :], SL.ap[0:Q, 0:Q], dtA.ap[0:Q], True, True, [SL, dtA], [remp])
                    mm(totp.ap[:, :], ones_f.ap[0:Q, :], dtA.ap[0:Q], True, True, [ones_f, dtA], [totp])
                    act(dec_end.ap[0:Q], remp.ap[0:Q, :], AF.Exp, [remp], [dec_end])
                    act(decay_bc.ap, totp.ap, AF.Exp, [totp], [decay_bc])
                    cbp = pbank(2, [4, 128])
                    for g in range(4):
                        mm(cbp.ap[0:Q, g, 0:Q], Bb[g].ap[:, cs:cs + Q], Cb[g].ap[:, cs:cs + Q], True, True, [Bb[g], Cb[g]], [cbp])
                    tt('dve', CBm.ap[0:Q], cbp.ap[0:Q, :, 0:Q], UL.ap[0:Q, 0:Q].unsqueeze(1).to_broadcast([Q, 4, Q]),
                       ALU.mult, [cbp, UL], [CBm])
                    xtp = ps.at(0, [8, 128], BF16)
                    xtp16 = ps.at(0, [16, 64], BF16)
                    for kc in range(8):
                        tr(xtp.ap[0:Q, kc, :], xsb[kc].ap[:, cs:cs + Q], ident_b.ap, [xsb[kc], ident_b], [xtp])
                    tt('dve', xdt.ap[0:Q], xtp16.ap[0:Q], dt_.ap[0:Q].unsqueeze(2).to_broadcast([Q, 16, 64]), ALU.mult,
                       [xtp, dt_], [xdt])
                    tt('pool', xdd.ap[0:Q], xdt.ap[0:Q], dec_end.ap[0:Q].unsqueeze(2).to_broadcast([Q, 16, 64]), ALU.mult,
                       [xdt, dec_end], [xdd])
                    btp = ps.at(2048, [4, 128], BF16)
                    for g in range(4):
                        tr(btp.ap[0:Q, g, :], Bb[g].ap[:, cs:cs + Q], ident_b.ap, [Bb[g], ident_b], [btp])
                    cp('act', Btm.ap[0:Q], btp.ap[0:Q], [btp], [Btm])
                    def ssd_a(g):
                        V, Lexp, Ebc, Mp, Cexp = V2[g % 2], Lexp2[g % 2], Ebc2[g % 2], Mp2[g % 2], Cexp2[g % 2]
                        segp = pbank(4 + (g % 2), [4, Q])
                        Rp = pbank(6 if g % 2 == 0 else 3, [4, Q])
                        tt('dve', V.ap[0:Q], UL.ap[0:Q, 0:Q].unsqueeze(1).to_broadcast([Q, 4, Q]),
                           dtA.ap[0:Q, 4 * g:4 * g + 4].unsqueeze(2).to_broadcast([Q, 4, Q]), ALU.mult, [UL, dtA], [V])
                        mm(segp.ap[0:Q], SL.ap[0:Q, 0:Q], V.ap[0:Q], True, True, [SL, V], [segp])
                        mm(Rp.ap, ones_f.ap[0:Q, :], V.ap[0:Q], True, True, [ones_f, V], [Rp])
                        act(Lexp.ap[0:Q], segp.ap[0:Q], AF.Exp, [segp], [Lexp])
                        act(Ebc.ap, Rp.ap, AF.Exp, [Rp], [Ebc])
                        tt('dve', Mp.ap[0:Q], Lexp.ap[0:Q], CBm.ap[0:Q, g:g + 1, :].to_broadcast([Q, 4, Q]), ALU.mult,
                           [Lexp, CBm], [Mp])
                        tt('pool', Cexp.ap, Ebc.ap, Cb.ap[:, g:g + 1, cs:cs + Q].to_broadcast([128, 4, Q]), ALU.mult,
                           [Ebc, Cb[g]], [Cexp])

                    def ssd_b(g):
                        Mp, Cexp = Mp2[g % 2], Cexp2[g % 2]
                        for kk in range(2):
                            kc = 2 * g + kk
                            ybase = (2048 + 1024) if g % 2 == 0 else 0
                            ypk = ps.at(ybase + kk * 512, [Q], F32)
                            for hh in range(2):
                                h_ = 2 * kc + hh
                                hq = h_ - 4 * g
                                o_ = ypk.ap[hh * 64:(hh + 1) * 64, :]
                                mm(o_, xdt.ap[0:Q, h_, :], Mp.ap[0:Q, hq, :], True, False, [xdt, Mp], [ypk])
                                mm(o_, Sb.ap[:, h_ * 64:(h_ + 1) * 64], Cexp.ap[:, hq, :], False, True, [Sb, Cexp], [ypk])
                            yt = ytmp[kc % 2]
                            yg = ygf[kc % 2]
                            sq = ysq[kc % 2]
                            stt('dve', yt.ap, xsb[kc].ap[:, cs:cs + Q], dssd[:, kc:kc + 1], ypk.ap, ALU.mult, ALU.add,
                                [xsb[kc], cvec, ypk], [yt])
                            tt('dve', yg.ap, yt.ap, sz[kc].ap[:, cs:cs + Q], ALU.mult, [yt, sz[kc]], [yg])
                            act(sq.ap, yg.ap, AF.Square, [yg], [sq])
                            mm(ss1.ap[:, cs:cs + Q], ones1k.ap, sq.ap, kc == 0, kc == 7, [ones1k, sq], [ss1])
                            act(mix_in[kc].ap[:, cs:cs + Q], yg.ap, AF.Copy, [yg, cvec], [mix_in[kc]], scale=gssd[:, kc:kc + 1])

                    ssd_a(0)
                    for g in range(4):
                        if g + 1 < 4:
                            ssd_a(g + 1)
                        ssd_b(g)
                    Sp = [pbank(0, [512]), pbank(2, [512])]
                    for g in range(4):
                        mm(Sp[g // 2].ap[:, (g % 2) * 256:(g % 2) * 256 + 256], Btm.ap[0:Q, g, :],
                           xdd_flat.ap[0:Q, g * 256:(g + 1) * 256], True, True, [Btm, xdd], [Sp[g // 2]])
                    tt('dve', Sst.ap.rearrange('p (h j) -> p h j', j=64), Sst.ap.rearrange('p (h j) -> p h j', j=64),
                       decay_bc.ap.unsqueeze(2).to_broadcast([128, 16, 64]), ALU.mult, [Sst, decay_bc], [Sst])
                    for j in range(2):
                        tt('dve', Sst.ap[:, j * 512:(j + 1) * 512], Sst.ap[:, j * 512:(j + 1) * 512], Sp[j].ap, ALU.add,
                           [Sst, Sp[j]], [Sst])
                    cp('act', Sb.ap, Sst.ap, [Sst], [Sb])
            finish_rstd(ss1, Tt, rstd1)

            ckpt('ssd')
            A.top = xsb.off
            nfr = Tt // Fr
            s5base = A.top
            sets = []
            for _ in range(4):
                d = {k: A.alloc([512], BF16) for k in ['brb', 'bib', 'p1', 'p2']}
                d['p3'], d['p4'] = d['bib'], d['brb']
                d['u1'], d['u2'], d['u3'], d['u4'] = d['p1'], d['p2'], d['brb'], d['bib']
                d['G'] = A.alloc([2, 512], F32)
                d['gre'] = d['G'][0]
                d['gim'] = d['G'][1]
                d['greb'] = d['brb']
                d['gimb'] = d['bib']
                d['tiny'] = A.alloc([8], F32)
                sets.append(d)
            glb = sz
            wsl = {}
            for nm in ['wbbre', 'wbbim', 'tcos', 'tsin', 'wcre', 'wcren', 'wcimn']:
                wsl[nm] = pool_load(nm, 0, Wb[nm], lambda b: b.ap.rearrange('p (s j) -> p s j', s=32))
            cb_, ctab_all = wsl['tcos']
            sb_, stab_all = wsl['tsin']
            cFc, sFc, ssc = cFs[('tcos', Fr)], cFs[('tsin', Fr)], cFs[('ss', Fr)]
            y5ps = {}

            def v3(bf):
                return bf.ap[:, :Tt].rearrange('p (f r) -> p f r', f=nfr)

            def tabs(sc):
                ctab = ctab_all[:, sc, 0:Fr].unsqueeze(1).to_broadcast([128, nfr, Fr])
                stab = stab_all[:, sc, 0:Fr].unsqueeze(1).to_broadcast([128, nfr, Fr])
                return ctab, stab

            def st_front(sc):
                kc = sc // 4
                d = sets[sc % 4]
                brp = pbank(0, [512])
                bip = pbank(1, [512])
                mm(brp.ap[:, :Tt], wsl['wbbre'][1][:, sc, :], u5b[kc].ap[:, :Tt], True, True, [wsl['wbbre'][0], u5b[kc]], [brp])
                mm(bip.ap[:, :Tt], wsl['wbbim'][1][:, sc, :], u5b[kc].ap[:, :Tt], True, True, [wsl['wbbim'][0], u5b[kc]], [bip])
                cp('act', d['brb'].ap[:, :Tt], brp.ap[:, :Tt], [brp], [d['brb']])
                cp('act', d['bib'].ap[:, :Tt], bip.ap[:, :Tt], [bip], [d['bib']])

            def st_mid(sc):
                d = sets[sc % 4]
                ctab, stab = tabs(sc)
                tt('dve', v3(d['p1']), v3(d['brb']), ctab, ALU.mult, [d['brb'], cb_], [d['p1']])
                tt('dve', v3(d['p2']), v3(d['bib']), stab, ALU.mult, [d['bib'], sb_], [d['p2']])
                tt('dve', v3(d['p3']), v3(d['bib']), ctab, ALU.mult, [d['bib'], cb_], [d['p3']])
                tt('dve', v3(d['p4']), v3(d['brb']), stab, ALU.mult, [d['brb'], sb_], [d['p4']])
                bprp = pbank(2 + 2 * (sc % 2), [512])
                bpip = pbank(3 + 2 * (sc % 2), [512])
                mm(bprp.ap[:, :Tt], ident_b.ap, d['p1'].ap[:, :Tt], True, False, [ident_b, d['p1']], [bprp])
                mm(bprp.ap[:, :Tt], ident_b.ap, d['p2'].ap[:, :Tt], False, True, [ident_b, d['p2']], [bprp])
                mm(bpip.ap[:, :Tt], ident_b.ap, d['p3'].ap[:, :Tt], True, False, [ident_b, d['p3']], [bpip])
                mm(bpip.ap[:, :Tt], identn_b.ap, d['p4'].ap[:, :Tt], False, True, [identn_b, d['p4']], [bpip])

            def st_scan_pair(scs):
                frames = [(sidx, c0 + f * Fr) for (sidx, c0, Ls) in segs for f in range(Ls // Fr)]
                for (sidx, a) in frames:
                    h5 = st_h5[sidx]
                    col = a + Fr - 1
                    for sc in scs:
                        d = sets[sc % 4]
                        bprp = pbank(2 + 2 * (sc % 2), [512])
                        bpip = pbank(3 + 2 * (sc % 2), [512])
                        magb = mag_c[:, sc:sc + 1].to_broadcast([128, Fr])
                        for comp, srcp, dst in ((0, bprp, 'gre'), (1, bpip, 'gim')):
                            scan(d[dst].ap[:, a:a + Fr], magb, srcp.ap[:, a:a + Fr], h5.ap[:, comp, sc:sc + 1],
                                 [s5c, srcp, h5], [d[dst]])
                    for sc in scs:
                        d = sets[sc % 4]
                        tt('dve', d['tiny'].ap[:, 0:2], d['G'].ap[:, ::-1, col], ssc[:, sc, :], ALU.mult, [d['G'], s5c], [d['tiny']])
                    for sc in scs:
                        d = sets[sc % 4]
                        stt('dve', h5.ap[:, :, sc], d['G'].ap[:, :, col], cFc[:, sc:sc + 1], d['tiny'].ap[:, 0:2], ALU.mult, ALU.add,
                            [d['G'], s5c, d['tiny']], [h5])
                for sc in scs:
                    d = sets[sc % 4]
                    cp('act', d['greb'].ap[:, :Tt], d['gre'].ap[:, :Tt], [d['gre']], [d['greb']])
                    cp('act', d['gimb'].ap[:, :Tt], d['gim'].ap[:, :Tt], [d['gim']], [d['gimb']])

            def st_out(sc):
                kc = sc // 4
                d = sets[sc % 4]
                ctab, stab = tabs(sc)
                tt('dve', v3(d['u1']), v3(d['greb']), ctab, ALU.mult, [d['greb'], cb_], [d['u1']])
                tt('dve', v3(d['u2']), v3(d['gimb']), stab, ALU.mult, [d['gimb'], sb_], [d['u2']])
                tt('dve', v3(d['u3']), v3(d['greb']), stab, ALU.mult, [d['greb'], sb_], [d['u3']])
                tt('dve', v3(d['u4']), v3(d['gimb']), ctab, ALU.mult, [d['gimb'], cb_], [d['u4']])
                if sc % 4 == 0:
                    y5ps[kc] = pbank(6 + (kc % 2), [512])
                y5p = y5ps[kc]
                for wi, (wn, un) in enumerate([('wcre', 'u1'), ('wcren', 'u2'), ('wcimn', 'u3'), ('wcimn', 'u4')]):
                    mm(y5p.ap[:, :Tt], wsl[wn][1][:, sc, :], d[un].ap[:, :Tt], sc % 4 == 0 and wi == 0, False,
                       [wsl[wn][0], d[un]], [y5p])
                if sc % 4 == 3:
                    mm(y5p.ap[:, :Tt], Ddiag5[kc].ap, u5b[kc].ap[:, :Tt], False, True, [Ddiag5[kc], u5b[kc]], [y5p])
                    act(glb[kc].ap[:, :Tt], y5p.ap[:, :Tt], AF.Gelu, [y5p], [glb[kc]])

            st_front(0)
            st_front(1)
            st_mid(0)
            st_mid(1)
            for j in range(16):
                a_, b_ = 2 * j, 2 * j + 1
                if j + 1 < 16:
                    st_front(a_ + 2)
                    st_front(b_ + 2)
                st_scan_pair((a_, b_))
                if j + 1 < 16:
                    st_mid(a_ + 2)
                    st_mid(b_ + 2)
                st_out(a_)
                st_out(b_)
            A.top = s5base
            rstd2 = A.alloc([512], F32)
            gtmp = [{'sig': A.alloc([512], F32), 'yg': A.alloc([512], F32), 'sq': A.alloc([512], BF16),
                     'tA': A.alloc([512], F32), 'tB': A.alloc([512], F32)} for _ in range(2)]
            ckpt('s5')
            wg = [pool_load('wglu', hh, Wb['wglu'][hh], lambda b: b.ap.rearrange('p (k j) -> p k j', k=4)) for hh in range(2)]
            ss2 = pbank(5, [512])
            for fc in range(8):
                gp_ = pbank(fc % 4, [512])
                for kc in range(8):
                    b, w = wg[kc // 4]
                    mm(gp_.ap[:, :Tt], w[:, kc % 4, fc * 128:(fc + 1) * 128], glb[kc].ap[:, :Tt], kc == 0, kc == 7, [b, glb[kc]], [gp_])
                sig = gtmp[fc % 2]['sig']
                yg = gtmp[fc % 2]['yg']
                sq = gtmp[fc % 2]['sq']
                act(sig.ap[:, :Tt], gp_.ap[:, :Tt], AF.Sigmoid, [gp_, cvec], [sig], bias=bglu[:, fc:fc + 1])
                tt('dve', yg.ap[:, :Tt], glb[fc].ap[:, :Tt], sig.ap[:, :Tt], ALU.mult, [glb[fc], sig], [yg])
                act(sq.ap[:, :Tt], yg.ap[:, :Tt], AF.Square, [yg], [sq])
                mm(ss2.ap[:, :Tt], ones1k.ap, sq.ap[:, :Tt], fc == 0, fc == 7, [ones1k, sq], [ss2])
                ts('pool', mix_in[8 + fc].ap[:, :Tt], yg.ap[:, :Tt], gs5[:, fc:fc + 1], None, ALU.mult, None, [yg, cvec], [mix_in[8 + fc]])
            finish_rstd(ss2, Tt, rstd2)
            for pr in range(8):
                b, w = pool_load('wout', pr, Wb['wout'][2 * pr:2 * pr + 2].rearrange('c p k j -> p c k j'),
                                 lambda b: b.ap.rearrange('p (c k j) -> p c k j', c=2, k=16))
                for cc in range(2):
                    fc = 2 * pr + cc
                    m1 = pbank(2 * (fc % 2), [512])
                    m2 = pbank(2 * (fc % 2) + 1, [512])
                    for ko in range(8):
                        mm(m1.ap[:, :Tt], w[:, cc, ko, :], mix_in[ko].ap[:, :Tt], ko == 0, ko == 7, [b, mix_in[ko]], [m1])
                    for ko in range(8, 16):
                        mm(m2.ap[:, :Tt], w[:, cc, ko, :], mix_in[ko].ap[:, :Tt], ko == 8, ko == 15, [b, mix_in[ko]], [m2])
                    tA = gtmp[fc % 2]['tA']
                    tB = gtmp[fc % 2]['tB']
                    tt('dve', tA.ap[:, :Tt], m1.ap[:, :Tt], rstd1.ap[:, :Tt], ALU.mult, [m1, rstd1], [tA])
                    tt('dve', tB.ap[:, :Tt], m2.ap[:, :Tt], rstd2.ap[:, :Tt], ALU.mult, [m2, rstd2], [tB])
                    tt('pool', tA.ap[:, :Tt], tA.ap[:, :Tt], tB.ap[:, :Tt], ALU.add, [tA, tB], [tA])
                    tt('pool', xf[fc].ap[:, :Tt], xf[fc].ap[:, :Tt], tA.ap[:, :Tt], ALU.add, [xf[fc], tA], [xf[fc]])

        def init_state_zero(sidx):
            memset('pool', st_halo[sidx].ap, 0.0, [st_halo[sidx]])
            memset('pool', st_S[sidx].ap, 0.0, [st_S[sidx]])
            memset('pool', st_Sb[sidx].ap, 0.0, [st_Sb[sidx]])
            memset('pool', st_h5[sidx].ap, 0.0, [st_h5[sidx]])

        def init_state_from(sidx, b, A):
            t0 = A.top
            cst = A.alloc([3, 128], F32)
            dma('sp', cst.ap[0:16], I['cconv'][b].rearrange('k (c p) -> c k p', p=128), (), [cst])
            pb_ = pbank(0, [3, 16])
            for k in range(3):
                tr(pb_.ap[:, k, :], cst.ap[0:16, k, :], ident_f.ap[0:16, 0:16], [cst, ident_f], [pb_])
            cp('dve', st_halo[sidx].ap.rearrange('p c k -> p k c'), pb_.ap, [pb_], [st_halo[sidx]])
            sst = A.alloc([8, 128], F32)
            dma('sp', sst.ap, I['sssd'][b].rearrange('(c p) n -> p c n', p=128), (), [sst])
            for half in range(2):
                pb2 = pbank(1 + half, [4, 128])
                for j in range(4):
                    tr(pb2.ap[:, j, :], sst.ap[:, half * 4 + j, :], ident_f.ap, [sst, ident_f], [pb2])
                cp('dve', st_S[sidx].ap[:, half * 512:(half + 1) * 512], pb2.ap.rearrange('p a b -> p (a b)'), [pb2], [st_S[sidx]])
            cp('act', st_Sb[sidx].ap, st_S[sidx].ap, [st_S[sidx]], [st_Sb[sidx]])
            h5s = A.alloc([2, 128], F32)
            dma('sp', h5s.ap[0:32, 0, :], I['s5re'][b], (), [h5s])
            dma('sp', h5s.ap[0:32, 1, :], I['s5im'][b], (), [h5s])
            pb3 = pbank(3, [2, 32])
            for k in range(2):
                tr(pb3.ap[:, k, :], h5s.ap[0:32, k, :], ident_f.ap[0:32, 0:32], [h5s, ident_f], [pb3])
            cp('dve', st_h5[sidx].ap, pb3.ap, [pb3], [st_h5[sidx]])
            A.top = t0

        def out_state(sidx, b, okeys, A):
            t0 = A.top
            oc, os_, ore, oim = okeys
            pb_ = pbank(0, [3, 128])
            for k in range(3):
                tr(pb_.ap[0:16, k, :], st_halo[sidx].ap[:, :, k], ident_f.ap, [st_halo[sidx], ident_f], [pb_])
            cst = A.alloc([3, 128], F32)
            cp('dve', cst.ap[0:16], pb_.ap[0:16], [pb_], [cst])
            dma('act', O[oc][b].rearrange('k (c p) -> c k p', p=128), cst.ap[0:16], [cst], ())
            sst = A.alloc([8, 128], F32)
            for half in range(2):
                pb2 = pbank(1 + half, [4, 128])
                for j in range(4):
                    kc = half * 4 + j
                    tr(pb2.ap[:, j, :], st_S[sidx].ap[:, kc * 128:(kc + 1) * 128], ident_f.ap, [st_S[sidx], ident_f], [pb2])
                cp('dve', sst.ap[:, half * 4:half * 4 + 4, :], pb2.ap, [pb2], [sst])
            dma('act', O[os_][b].rearrange('(c p) n -> p c n', p=128), sst.ap, [sst], ())
            pb3 = pbank(3, [2, 128])
            for k in range(2):
                tr(pb3.ap[0:32, k, :], st_h5[sidx].ap[:, k, :], ident_f.ap, [st_h5[sidx], ident_f], [pb3])
            h5s = A.alloc([2, 128], F32)
            cp('dve', h5s.ap[0:32], pb3.ap[0:32], [pb3], [h5s])
            dma('act', O[ore][b], h5s.ap[0:32, 0, :], [h5s], ())
            dma('act', O[oim][b], h5s.ap[0:32, 1, :], [h5s], ())
            A.top = t0

        A = Arena(sb_root, 'sb', SB_SIZE - STG)

        def run_tile(src_rows, dst_rows, Tt, segs):
            A.top = arena0
            load_tile(src_rows, Tt, A)
            ckpt('load')
            A.top = arena0
            ffn(Tt, '1', g1c, A)
            ckpt('ffn1')
            A.top = arena0
            mixer(Tt, segs, A)
            ckpt('mixer')
            A.top = arena0
            ffn(Tt, '2', g2c, A)
            if t0['active']:
                t0_finish()
            ckpt('ffn2')
            A.top = arena0
            store_tile(dst_rows, Tt, A)

        for b in range(NSEQ):
            init_state_zero(0)
            for ti in range(LP // 512):
                r0 = b * LP + ti * 512
                run_tile(I['xp'][r0:r0 + 512], O['yp'][r0:r0 + 512], 512, [(0, 0, 512)])
            A.top = arena0
            out_state(0, b, ('convp', 'ssdp', 's5rep', 's5imp'), A)
        if NS:
            A.top = arena0
            for s_ in range(NS):
                init_state_from(s_, s_, A)
            run_tile(I['xs'], O['ys'], NS * LS, [(s_, s_ * LS, LS) for s_ in range(NS)])
            A.top = arena0
            for s_ in range(NS):
                out_state(s_, s_, ('convs', 'ssds', 's5res', 's5ims'), A)

    import contextlib
    with contextlib.ExitStack() as es:
        sb_root = es.enter_context(nc.sbuf_tensor("arena", [128, SB_SIZE // 4], F32))
        ps_root = es.enter_context(nc.psum_tensor("psum", [128, 4096], F32))
        esem = {e: es.enter_context(nc.semaphore("e_" + e)) for e in ENGS}
        dsem = {(q, i): es.enter_context(nc.semaphore("d_%s%d" % (q, i))) for q in ('sp', 'pool', 'act') for i in range(NDS)}
        try:
            program(sb_root, ps_root)
        except _Stop:
            pass
        block = es.enter_context(nc.Block())
        S.emit(nc, block, esem, dsem)
    ctx['nops'] = {e: len(S.ops[e]) for e in ENGS}
    return nc, ctx


def prep_weights(p):
    f = np.float32
    W = {}

    def gu(wg, wu):
        g = wg.reshape(16, 128, 43, 128).transpose(2, 1, 0, 3)
        u = wu.reshape(16, 128, 43, 128).transpose(2, 1, 0, 3)
        return np.ascontiguousarray(np.stack([g, u], axis=2))

    def dn(wd):
        return np.ascontiguousarray(wd.reshape(43, 128, 4, 512).transpose(2, 0, 1, 3))

    def col(v, n):
        return np.ascontiguousarray(np.asarray(v, f).reshape(n, 128).T)
    W['wgu1'] = gu(p['w_ffn1_gate'][0], p['w_ffn1_up'][0])
    W['wd1'] = dn(p['w_ffn1_down'][0])
    W['wgu2'] = gu(p['w_ffn2_gate'][0], p['w_ffn2_up'][0])
    W['wd2'] = dn(p['w_ffn2_down'][0])
    win = p['w_in'][0]
    main = np.concatenate([win[:, 0:3072], win[:, 3088:4112]], axis=1)
    W['win'] = np.ascontiguousarray(main.reshape(16, 128, 32, 128).transpose(2, 1, 0, 3))
    W['wdt'] = np.ascontiguousarray(win[:, 3072:3088].reshape(16, 128, 16).transpose(1, 0, 2))
    W['wout'] = np.ascontiguousarray(p['w_out'][0].reshape(16, 128, 16, 128).transpose(2, 1, 0, 3))
    W['wglu'] = np.ascontiguousarray(p['w_glu'][0].reshape(2, 4, 128, 1024).transpose(0, 2, 1, 3))
    W['g1c'] = col(p['norm_ffn1'][0], 16)
    W['gmc'] = col(p['norm_mix'][0], 16)
    W['g2c'] = col(p['norm_ffn2'][0], 16)
    W['gfc'] = col(p['norm_final'], 16)
    W['gssd'] = col(p['norm_ssd'][0], 8)
    W['gs5'] = col(p['norm_s5'][0], 8)
    W['convw'] = np.ascontiguousarray(p['conv_w'][0].reshape(4, 16, 128).transpose(2, 1, 0))
    W['convb'] = col(p['conv_b'][0], 16)
    W['dssd'] = col(np.repeat(p['d_ssd'][0], 64), 8)
    W['bglu'] = col(p['b_glu'][0], 8)
    W['d5'] = col(p['s5_d'][0].reshape(-1), 8)
    W['dtb'] = np.ascontiguousarray(p['dt_bias'][0].reshape(1, 16))
    W['alog'] = np.ascontiguousarray(p['a_log'][0].reshape(1, 16))
    W['tau'] = np.arange(1, 129, dtype=f).reshape(1, 128)
    W['lrc'] = col(p['s5_lambda_re'][0].reshape(-1), 32)
    W['lic'] = col(p['s5_lambda_im'][0].reshape(-1), 32)
    W['lsc'] = col(np.repeat(p['s5_log_step'][0], 64), 32)
    bre, bim = p['s5_b_re'][0], p['s5_b_im'][0]
    cre, cim = p['s5_c_re'][0], p['s5_c_im'][0]
    wbre = np.zeros((128, 32, 128), f)
    wbim = np.zeros((128, 32, 128), f)
    wcre = np.zeros((128, 32, 128), f)
    wcim = np.zeros((128, 32, 128), f)
    for sc in range(32):
        for gl in range(2):
            g = 2 * sc + gl
            r0 = (sc % 4) * 32 + gl * 16
            wbre[r0:r0 + 16, sc, gl * 64:(gl + 1) * 64] = bre[g].T
            wbim[r0:r0 + 16, sc, gl * 64:(gl + 1) * 64] = bim[g].T
            wcre[gl * 64:(gl + 1) * 64, sc, r0:r0 + 16] = cre[g].T
            wcim[gl * 64:(gl + 1) * 64, sc, r0:r0 + 16] = cim[g].T
    W['wbre'], W['wbim'], W['wcre'], W['wcim'] = wbre, wbim, wcre, wcim
    for k in W:
        W[k] = np.ascontiguousarray(W[k], dtype=f)
        assert list(W[k].shape) == INPUT_SHAPES[k], (k, W[k].shape)
    return W


_CACHE = {}


def kernel(**inputs):
    p = {k: np.asarray(v) for k, v in inputs.items()}
    NCORES = 8
    cfg = {'nseq': 2, 'lp': 2048, 'ns': 2}
    W = prep_weights(p)
    nc, _ = build(cfg)
    in_maps = []
    for c in range(NCORES):
        m = dict(W)
        m['xp'] = np.ascontiguousarray(p['x_prompt'][2 * c:2 * c + 2].reshape(4096, 2048))
        m['xs'] = np.ascontiguousarray(p['x_sample'][2 * c:2 * c + 2].reshape(32, 2048))
        m['cconv'] = np.ascontiguousarray(p['cache_conv'][0, 2 * c:2 * c + 2])
        m['sssd'] = np.ascontiguousarray(p['state_ssd'][0, 2 * c:2 * c + 2].reshape(2, 1024, 128))
        m['s5re'] = np.ascontiguousarray(p['state_s5_re'][0, 2 * c:2 * c + 2].reshape(2, 32, 128))
        m['s5im'] = np.ascontiguousarray(p['state_s5_im'][0, 2 * c:2 * c + 2].reshape(2, 32, 128))
        in_maps.append(m)
    res = run_bass_kernel_spmd(nc, in_maps, core_ids=list(range(NCORES)))
    R = res.results

    def cat(key, shape):
        return np.concatenate([np.asarray(R[c][key], np.float32).reshape(shape) for c in range(NCORES)], axis=0)
    y_prompt = cat('yp', (2, 2048, 2048))
    y_sample = cat('ys', (2, 16, 2048))
    conv_p = cat('convp', (2, 3, 2048))[None]
    ssd_p = cat('ssdp', (2, 16, 64, 128))[None]
    s5re_p = cat('s5rep', (2, 64, 64))[None]
    s5im_p = cat('s5imp', (2, 64, 64))[None]
    conv_s = cat('convs', (2, 3, 2048))[None]
    ssd_s = cat('ssds', (2, 16, 64, 128))[None]
    s5re_s = cat('s5res', (2, 64, 64))[None]
    s5im_s = cat('s5ims', (2, 64, 64))[None]
    return (y_prompt, y_sample, conv_p, ssd_p, s5re_p, s5im_p, conv_s, ssd_s, s5re_s, s5im_s)
```

```python
import numpy as np
import concourse.bass as bass
import concourse.mybir as mybir
from concourse.bass_utils import run_bass_kernel_spmd

F32 = mybir.dt.float32
BF16 = mybir.dt.bfloat16
I32 = mybir.dt.int32
ALU = mybir.AluOpType
AF = mybir.ActivationFunctionType
ENGS = ['pe', 'act', 'dve', 'pool', 'sp']
NDS = 12
BLK = 512
EPS = 1e-6
TWO_PI = 6.283185307179586


def _dtsize(dt):
    return 4 if dt in (F32, I32) else 2


def _prod(s):
    r = 1
    for v in s:
        r *= v
    return r


class Buf:
    def __init__(s, root, space, off, nbytes, dt, shape):
        s.root, s.space, s.off, s.nbytes, s.dt, s.shape = root, space, off, nbytes, dt, tuple(shape)
        blk = 2048 if space == 'ps' else BLK
        s.blocks = [(space, b) for b in range(off // blk, (off + nbytes - 1) // blk + 1)]
        s._ap = None

    @property
    def ap(s):
        if s._ap is None:
            a = s.root[:, s.off // 4:(s.off + s.nbytes) // 4]
            if s.dt != F32:
                a = a.bitcast(s.dt)
            if len(s.shape) > 1:
                names = ' '.join('d%d' % i for i in range(len(s.shape)))
                a = a.rearrange('p (%s) -> p %s' % (names, names),
                                **{'d%d' % i: s.shape[i] for i in range(len(s.shape))})
            s._ap = a
        return s._ap

    def __getitem__(s, i):
        inner = _prod(s.shape[1:]) * _dtsize(s.dt)
        return Buf(s.root, s.space, s.off + i * inner, inner, s.dt, s.shape[1:] or (1,))

    def view(s, dt, shape):
        assert _prod(shape) * _dtsize(dt) <= s.nbytes
        return Buf(s.root, s.space, s.off, _prod(shape) * _dtsize(dt), dt, shape)


class Arena:
    def __init__(s, root, space, size):
        s.root, s.space, s.size, s.top = root, space, size, 0

    def alloc(s, shape, dt, align=BLK):
        off = (s.top + align - 1) // align * align
        nb = _prod(shape) * _dtsize(dt)
        nb4 = (nb + 3) // 4 * 4
        s.top = off + nb4
        assert s.top <= s.size, "arena %s overflow: %d > %d" % (s.space, s.top, s.size)
        return Buf(s.root, s.space, off, nb4, dt, shape)

    def at(s, off, shape, dt):
        nb = _prod(shape) * _dtsize(dt)
        assert off + nb <= s.size
        return Buf(s.root, s.space, off, nb, dt, shape)


class _Stop(Exception):
    pass


class DRes:
    def __init__(s, name, i=0):
        s.blocks = [('dr', name, i)]


class Op:
    __slots__ = ('fn', 'waits', 'signal', 'cnt', 'dma')

    def __init__(s, fn, dma):
        s.fn, s.waits, s.signal, s.cnt, s.dma = fn, [], False, 0, dma


class Sched:
    def __init__(s):
        s.ops = {e: [] for e in ENGS}
        s.blocks = {}
        s.seen = {e: {} for e in ENGS}
        s.ndma = {'sp': 0, 'pool': 0}
        s.dval = {}

    def op(s, eng, fn, r=(), w=(), dma=False):
        idx = len(s.ops[eng])
        o = Op(fn, None)
        deps = {}

        def add(tok):
            k = tok[:-1]
            if deps.get(k, -1) < tok[-1]:
                deps[k] = tok[-1]
        for x in r:
            for b in x.blocks:
                st = s.blocks.get(b)
                if st and st[0]:
                    add(st[0])
        for x in w:
            for b in x.blocks:
                st = s.blocks.get(b)
                if st:
                    if st[0]:
                        add(st[0])
                    for t in st[1].values():
                        add(t)
        seen = s.seen[eng]
        if dma:
            i = s.ndma[eng] % NDS
            s.ndma[eng] += 1
            val = s.dval.get((eng, i), 0) + 16
            s.dval[(eng, i)] = val
            if val > 16:
                add(('d', eng, i, val - 16))
            tok = ('d', eng, i, val)
            o.dma = (eng, i)
        else:
            tok = ('e', eng, idx)
        for k, v in deps.items():
            if k[0] == 'e' and k[1] == eng:
                if eng in ('pe', 'sp'):
                    continue
                if v < idx - 4:
                    continue
            if seen.get(k, -1) >= v:
                continue
            seen[k] = v
            if k[0] == 'e':
                s.ops[k[1]][v].signal = True
            o.waits.append((k, v))
        s.ops[eng].append(o)
        key = tok[:-1]
        for x in r:
            for b in x.blocks:
                st = s.blocks.setdefault(b, [None, {}])
                st[1][key] = tok
        for x in w:
            for b in x.blocks:
                s.blocks[b] = [tok, {}]
        return tok

    def emit(s, nc, block, esem, dsem):
        for e in ENGS:
            c = 0
            for o in s.ops[e]:
                if o.signal:
                    c += 1
                o.cnt = c

        def run(eng_name):
            def body(e):
                for o in s.ops[eng_name]:
                    for k, v in o.waits:
                        if k[0] == 'e':
                            e.wait_ge(esem[k[1]], s.ops[k[1]][v].cnt)
                        else:
                            e.wait_ge(dsem[(k[1], k[2])], v)
                    ins = o.fn(e)
                    if o.dma:
                        ins.then_inc(dsem[o.dma], 16)
                    elif o.signal:
                        ins.then_inc(esem[eng_name], 1)
                if eng_name == 'sp':
                    for (q, i), v in s.dval.items():
                        e.wait_ge(dsem[(q, i)], v)
            return body
        block.tensor(run('pe'))
        block.scalar(run('act'))
        block.vector(run('dve'))
        block.gpsimd(run('pool'))
        block.sync(run('sp'))


NFC = 43
WD_GROUPS = [(0, 8), (8, 8), (16, 8), (24, 8), (32, 8), (40, 3)]
NSLOT = 8

INPUT_SHAPES = {
    'wgu1': [43, 128, 2, 16, 128], 'wd1': [4, 43, 128, 512],
    'wgu2': [43, 128, 2, 16, 128], 'wd2': [4, 43, 128, 512],
    'win': [32, 128, 16, 128], 'wdt': [128, 16, 16], 'wout': [16, 128, 16, 128], 'wglu': [2, 128, 4, 1024],
    'g1c': [128, 16], 'gmc': [128, 16], 'g2c': [128, 16], 'gfc': [128, 16],
    'gssd': [128, 8], 'gs5': [128, 8], 'convw': [128, 16, 4], 'convb': [128, 16],
    'dssd': [128, 8], 'bglu': [128, 8], 'd5': [128, 8],
    'dtb': [1, 16], 'alog': [1, 16], 'tau': [1, 128],
    'lrc': [128, 32], 'lic': [128, 32], 'lsc': [128, 32],
    'wbre': [128, 32, 128], 'wbim': [128, 32, 128], 'wcre': [128, 32, 128], 'wcim': [128, 32, 128],
}


def build(cfg):
    NSEQ, LP, NS = cfg['nseq'], cfg['lp'], cfg['ns']
    LS = 16
    nc = bass.Bass("TRN2", target_bir_lowering=False)

    def din(name, shape):
        return nc.dram_tensor(name, list(shape), F32, kind="ExternalInput").ap()

    def dout(name, shape):
        return nc.dram_tensor(name, list(shape), F32, kind="ExternalOutput").ap()

    def dscr(name, shape, dt):
        return nc.dram_tensor(name, list(shape), dt, kind="Internal").ap()

    I = {k: din(k, v) for k, v in INPUT_SHAPES.items()}
    I['xp'] = din('xp', [NSEQ * LP, 2048])
    O = {'yp': dout('yp', [NSEQ * LP, 2048]), 'convp': dout('convp', [NSEQ, 3, 2048]),
         'ssdp': dout('ssdp', [NSEQ, 1024, 128]), 's5rep': dout('s5rep', [NSEQ, 32, 128]),
         's5imp': dout('s5imp', [NSEQ, 32, 128])}
    if NS:
        I['xs'] = din('xs', [NS * LS, 2048])
        I['cconv'] = din('cconv', [NS, 3, 2048])
        I['sssd'] = din('sssd', [NS, 1024, 128])
        I['s5re'] = din('s5re', [NS, 32, 128])
        I['s5im'] = din('s5im', [NS, 32, 128])
        O.update({'ys': dout('ys', [NS * LS, 2048]), 'convs': dout('convs', [NS, 3, 2048]),
                  'ssds': dout('ssds', [NS, 1024, 128]), 's5res': dout('s5res', [NS, 32, 128]),
                  's5ims': dout('s5ims', [NS, 32, 128])})
    Wb = {k: dscr(k + '_b', INPUT_SHAPES[k], BF16) for k in ['wgu1', 'wd1', 'wgu2', 'wd2', 'win', 'wout', 'wglu']}
    Wb['wbbre'] = dscr('wbbre_b', [128, 32, 128], BF16)
    Wb['wbbim'] = dscr('wbbim_b', [128, 32, 128], BF16)
    Wb['wcre'] = dscr('wcre_b', [128, 32, 128], BF16)
    Wb['wcimn'] = dscr('wcimn_b', [128, 32, 128], BF16)
    Wb['wcren'] = dscr('wcren_b', [128, 32, 128], BF16)
    Wb['tcos'] = dscr('tcos', [128, 32, 128], BF16)
    Wb['tsin'] = dscr('tsin', [128, 32, 128], BF16)

    S = Sched()
    SB_SIZE = 207 * 1024
    STG = 16 * 1024
    ctx = {}

    def ckpt(stage):
        if cfg.get('upto') == stage:
            raise _Stop()

    def program(sb_root, ps_root):
        sb = Arena(sb_root, 'sb', SB_SIZE)
        ps = Arena(ps_root, 'ps', 16384)

        def pbank(bank, shape, dt=F32, coloff=0):
            return ps.at(bank * 2048 + coloff, shape, dt)

        def tt(eng, out, in0, in1, op, r, w):
            S.op(eng, lambda e: e.tensor_tensor(out=out, in0=in0, in1=in1, op=op), r, w)

        def stt(eng, out, in0, scalar, in1, op0, op1, r, w):
            S.op(eng, lambda e: e.scalar_tensor_tensor(out=out, in0=in0, scalar=scalar, in1=in1, op0=op0, op1=op1), r, w)

        def ts(eng, out, in0, s1, s2, op0, op1, r, w):
            if s2 is None:
                S.op(eng, lambda e: e.tensor_scalar(out=out, in0=in0, scalar1=s1, scalar2=None, op0=op0), r, w)
            else:
                S.op(eng, lambda e: e.tensor_scalar(out=out, in0=in0, scalar1=s1, scalar2=s2, op0=op0, op1=op1), r, w)

        def act(out, in_, func, r, w, bias=None, scale=None):
            kw = {}
            if bias is not None:
                kw['bias'] = bias
            if scale is not None:
                kw['scale'] = scale
            S.op('act', lambda e: e.activation(out=out, in_=in_, func=func, **kw), r, w)

        def cp(eng, out, in_, r, w):
            if eng == 'act':
                S.op('act', lambda e: e.activation(out=out, in_=in_, func=AF.Copy), r, w)
            else:
                S.op(eng, lambda e: e.tensor_copy(out=out, in_=in_), r, w)

        def mm(out, lhsT, rhs, start, stop, r, w):
            S.op('pe', lambda e: e.matmul(out, lhsT=lhsT, rhs=rhs, start=start, stop=stop), r, w)

        def tr(out, in_, ident, r, w):
            S.op('pe', lambda e: e.transpose(out, in_, ident), r, w)

        def dma(q, out, in_, r, w):
            S.op(q, lambda e: e.dma_start(out=out, in_=in_), r, w, dma=True)

        def scan(out, d0, d1, init, r, w):
            S.op('dve', lambda e: e.tensor_tensor_scan(out=out, data0=d0, data1=d1, initial=init,
                                                       op0=ALU.mult, op1=ALU.add), r, w)

        def memset(eng, ap, val, w):
            S.op(eng, lambda e: e.memset(ap, val), (), w)

        xf = sb.alloc([16, 512], F32)
        xn = sb.alloc([16, 512], BF16)
        slots = [sb.alloc([4096], BF16) for _ in range(NSLOT)]
        slot_ctr = [0]

        def next_slot():
            b = slots[slot_ctr[0] % NSLOT]
            slot_ctr[0] += 1
            return b

        ident_f = sb.alloc([128], F32)
        ident_b = sb.alloc([128], BF16)
        UL = sb.alloc([128], F32)
        SL = sb.alloc([128], F32)
        ones_f = sb.alloc([128], F32)
        onesD = sb.alloc([128], BF16)
        ones1k = sb.alloc([128], BF16)
        cvec = sb.alloc([16 * 4 + 8 * 5 + 16 * 4 + 16], F32)
        co = [0]

        def cslice(n):
            a = cvec.ap[:, co[0]:co[0] + n]
            co[0] += n
            return a
        g1c, gmc, g2c, gfc = cslice(16), cslice(16), cslice(16), cslice(16)
        gssd, gs5, dssd, bglu, d5 = cslice(8), cslice(8), cslice(8), cslice(8), cslice(8)
        convw_ap = cslice(64)
        convb = cslice(16)
        convw = convw_ap.rearrange('p (c k) -> p c k', k=4)
        rowc = sb.alloc([16 + 16 + 128], F32)
        dtb_bc, A_bc, tau_bc = rowc.ap[:, 0:16], rowc.ap[:, 16:32], rowc.ap[:, 32:160]
        s5c = sb.alloc([32 * 9], F32)
        mag_c = s5c.ap[:, 0:32]
        cFs = {('tcos', 128): s5c.ap[:, 32:64], ('tsin', 128): s5c.ap[:, 64:96],
               ('tcos', 16): s5c.ap[:, 96:128], ('tsin', 16): s5c.ap[:, 128:160],
               ('ss', 128): s5c.ap[:, 160:224].rearrange('p (s c) -> p s c', c=2),
               ('ss', 16): s5c.ap[:, 224:288].rearrange('p (s c) -> p s c', c=2)}
        identn_b = sb.alloc([128], BF16)
        Ddiag5 = sb.alloc([8, 128], BF16)
        wdt_b = sb.alloc([16, 16], BF16)
        nstate = max(1, NS)
        st_halo = [sb.alloc([16, 3], F32) for _ in range(nstate)]
        st_S = [sb.alloc([1024], F32) for _ in range(nstate)]
        st_Sb = [sb.alloc([1024], BF16) for _ in range(nstate)]
        st_h5 = [sb.alloc([2, 32], F32) for _ in range(nstate)]
        arena0 = sb.top

        wres = {}
        units = []
        upos = {}

        def cast_unit(name, idx, dst, src, fshape):
            wres[(name, idx)] = DRes(name, idx)
            upos[(name, idx)] = len(units)
            units.append((name, idx, dst, src, list(fshape)))

        def cast_ffn(tag):
            for c in range(NFC):
                cast_unit('wgu' + tag, c, Wb['wgu' + tag][c], I['wgu' + tag][c], [2, 16, 128])
            for fg in range(4):
                for gi, (c0, n) in enumerate(WD_GROUPS):
                    cast_unit('wd' + tag, fg * 6 + gi,
                              Wb['wd' + tag][fg, c0:c0 + n].rearrange('c p j -> p c j'),
                              I['wd' + tag][fg, c0:c0 + n].rearrange('c p j -> p c j'), [n, 512])
        cast_ffn('1')
        for pr in range(16):
            cast_unit('win', pr, Wb['win'][2 * pr:2 * pr + 2].rearrange('c p k j -> p c k j'),
                      I['win'][2 * pr:2 * pr + 2].rearrange('c p k j -> p c k j'), [2, 16, 128])
        for hh in range(2):
            cast_unit('wglu', hh, Wb['wglu'][hh], I['wglu'][hh], [4, 1024])
        for pr in range(8):
            cast_unit('wout', pr, Wb['wout'][2 * pr:2 * pr + 2].rearrange('c p k j -> p c k j'),
                      I['wout'][2 * pr:2 * pr + 2].rearrange('c p k j -> p c k j'), [2, 16, 128])
        cast_ffn('2')
        ckpt('cast')

        cres = [cvec]
        o = 0
        for nm, n in [('g1c', 16), ('gmc', 16), ('g2c', 16), ('gfc', 16), ('gssd', 8), ('gs5', 8), ('dssd', 8),
                      ('bglu', 8), ('d5', 8)]:
            dma('sp', cvec.ap[:, o:o + n], I[nm], (), cres)
            o += n
        dma('sp', cvec.ap[:, o:o + 64], I['convw'].rearrange('p c k -> p (c k)'), (), cres)
        o += 64
        dma('sp', cvec.ap[:, o:o + 16], I['convb'], (), cres)
        dma('sp', rowc.ap[:, 0:16], I['dtb'].partition_broadcast(128), (), [rowc])
        dma('sp', rowc.ap[:, 16:32], I['alog'].partition_broadcast(128), (), [rowc])
        dma('sp', rowc.ap[:, 32:160], I['tau'].partition_broadcast(128), (), [rowc])
        act(A_bc, A_bc, AF.Exp, [rowc], [rowc])
        ts('dve', A_bc, A_bc, -1.0, None, ALU.mult, None, [rowc], [rowc])
        memset('pool', ones_f.ap, 1.0, [ones_f])
        memset('pool', ident_f.ap, 0.0, [ident_f])
        S.op('pool', lambda e: e.affine_select(out=ident_f.ap, in_=ident_f.ap, pattern=[[-1, 128]],
                                               compare_op=ALU.not_equal, fill=1.0, base=0, channel_multiplier=1),
             [ident_f], [ident_f])
        S.op('pool', lambda e: e.affine_select(out=UL.ap, in_=ones_f.ap, pattern=[[1, 128]],
                                               compare_op=ALU.is_ge, fill=0.0, base=0, channel_multiplier=-1),
             [ones_f], [UL])
        S.op('pool', lambda e: e.affine_select(out=SL.ap, in_=ones_f.ap, pattern=[[-1, 128]],
                                               compare_op=ALU.is_gt, fill=0.0, base=0, channel_multiplier=1),
             [ones_f], [SL])
        cp('dve', ident_b.ap, ident_f.ap, [ident_f], [ident_b])
        ts('dve', identn_b.ap, ident_f.ap, -1.0, None, ALU.mult, None, [ident_f], [identn_b])
        memset('dve', onesD.ap, 1.0 / 2048.0, [onesD])
        memset('dve', ones1k.ap, 1.0 / 1024.0, [ones1k])
        tmpA = Arena(sb_root, 'sb', SB_SIZE)
        tmpA.top = arena0
        wdt_f = tmpA.alloc([16, 16], F32)
        dma('sp', wdt_f.ap, I['wdt'], (), [wdt_f])
        cp('dve', wdt_b.ap, wdt_f.ap, [wdt_f], [wdt_b])
        for kc in range(8):
            ts('dve', Ddiag5[kc].ap, ident_f.ap, d5[:, kc:kc + 1], None, ALU.mult, None, [ident_f, cvec], [Ddiag5[kc]])

        ckpt('consts')
        c5 = tmpA.alloc([16, 32], F32)
        ci5 = tmpA.alloc([32], I32)

        def C5(i):
            return c5.ap[:, i, :]
        LR, LI, LS_, STEP, ANG, LBR, LBI, T0, T1, QRE, QIM, T2, T3 = range(13)
        dma('sp', C5(LR), I['lrc'], (), [c5])
        dma('sp', C5(LI), I['lic'], (), [c5])
        dma('sp', C5(LS_), I['lsc'], (), [c5])
        R5 = [c5, ci5]
        act(C5(STEP), C5(LS_), AF.Exp, R5, R5)
        tt('dve', C5(T0), C5(LR), C5(STEP), ALU.mult, R5, R5)
        act(mag_c, C5(T0), AF.Exp, R5, [s5c] + R5)
        tt('dve', C5(ANG), C5(LI), C5(STEP), ALU.mult, R5, R5)

        def sincos(dst, src, shift, ki, tmp, R):
            ts('dve', tmp, src, shift, 1.0 / TWO_PI, ALU.add, ALU.mult, R, R)
            cp('dve', ki, tmp, R, R)
            cp('dve', tmp, ki, R, R)
            stt('dve', tmp, tmp, -TWO_PI, src, ALU.mult, ALU.add, R, R)
            ts('dve', tmp, tmp, shift, None, ALU.add, None, R, R)
            ts('dve', tmp, tmp, -3.141592, 3.141592, ALU.max, ALU.min, R, R)
            act(dst, tmp, AF.Sin, R, R)
        sincos(C5(LBI), C5(ANG), 0.0, ci5.ap, C5(T1), R5)
        sincos(C5(LBR), C5(ANG), 1.5707963267948966, ci5.ap, C5(T1), R5)
        tt('dve', C5(LBR), C5(LBR), mag_c, ALU.mult, R5 + [s5c], R5)
        tt('dve', C5(LBI), C5(LBI), mag_c, ALU.mult, R5 + [s5c], R5)
        tt('dve', C5(T0), C5(LR), C5(LR), ALU.mult, R5, R5)
        tt('dve', C5(T1), C5(LI), C5(LI), ALU.mult, R5, R5)
        tt('dve', C5(T0), C5(T0), C5(T1), ALU.add, R5, R5)
        S.op('dve', lambda e: e.reciprocal(out=C5(T0), in_=C5(T0)), R5, R5)
        ts('dve', C5(T1), C5(LBR), -1.0, None, ALU.add, None, R5, R5)
        tt('dve', C5(T2), C5(T1), C5(LR), ALU.mult, R5, R5)
        tt('dve', C5(T3), C5(LBI), C5(LI), ALU.mult, R5, R5)
        tt('dve', C5(T2), C5(T2), C5(T3), ALU.add, R5, R5)
        tt('dve', C5(QRE), C5(T2), C5(T0), ALU.mult, R5, R5)
        tt('dve', C5(T2), C5(LBI), C5(LR), ALU.mult, R5, R5)
        tt('dve', C5(T3), C5(T1), C5(LI), ALU.mult, R5, R5)
        tt('dve', C5(T2), C5(T2), C5(T3), ALU.subtract, R5, R5)
        tt('dve', C5(QIM), C5(T2), C5(T0), ALU.mult, R5, R5)

        ckpt('s5a')
        wbre_f = tmpA.alloc([32, 128], F32)
        wbim_f = tmpA.alloc([32, 128], F32)
        wbbre_o = tmpA.alloc([32, 128], BF16)
        wbbim_o = tmpA.alloc([32, 128], BF16)
        dma('sp', wbre_f.ap, I['wbre'], (), [wbre_f])
        dma('sp', wbim_f.ap, I['wbim'], (), [wbim_f])
        dg = [tmpA.alloc([2, 128], F32) for _ in range(2)]
        tq = [tmpA.alloc([4, 128], F32) for _ in range(2)]
        for sc in range(32):
            d_ = dg[sc % 2]
            t_ = tq[sc % 2]
            qps = pbank(sc % 4, [2, 128])
            ts('dve', d_[0].ap, ident_f.ap, C5(QRE)[:, sc:sc + 1], None, ALU.mult, None, [ident_f, c5], [d_[0]])
            ts('dve', d_[1].ap, ident_f.ap, C5(QIM)[:, sc:sc + 1], None, ALU.mult, None, [ident_f, c5], [d_[1]])
            mm(qps[0].ap, ones_f.ap, d_[0].ap, True, True, [ones_f, d_[0]], [qps[0]])
            mm(qps[1].ap, ones_f.ap, d_[1].ap, True, True, [ones_f, d_[1]], [qps[1]])
            tt('dve', t_[0].ap, wbre_f[sc].ap, qps[0].ap, ALU.mult, [wbre_f[sc], qps[0]], [t_[0]])
            tt('dve', t_[1].ap, wbim_f[sc].ap, qps[1].ap, ALU.mult, [wbim_f[sc], qps[1]], [t_[1]])
            tt('dve', t_[2].ap, wbre_f[sc].ap, qps[1].ap, ALU.mult, [wbre_f[sc], qps[1]], [t_[2]])
            tt('dve', t_[3].ap, wbim_f[sc].ap, qps[0].ap, ALU.mult, [wbim_f[sc], qps[0]], [t_[3]])
            tt('pool', wbbre_o[sc].ap, t_[0].ap, t_[1].ap, ALU.subtract, [t_[0], t_[1]], [wbbre_o[sc]])
            tt('pool', wbbim_o[sc].ap, t_[2].ap, t_[3].ap, ALU.add, [t_[2], t_[3]], [wbbim_o[sc]])
        for nm, bf in [('wbbre', wbbre_o), ('wbbim', wbbim_o)]:
            rs = DRes(nm)
            wres[(nm, 0)] = rs
            dma('sp', Wb[nm], bf.ap, [bf], [rs])
        ckpt('s5b')
        tmpA.top = wbre_f.off
        for nm, src, scale in [('wcre', 'wcre', 1.0), ('wcren', 'wcre', -1.0), ('wcimn', 'wcim', -1.0)]:
            wf_ = tmpA.alloc([32, 128], F32)
            wo_ = tmpA.alloc([32, 128], BF16)
            dma('sp', wf_.ap, I[src], (), [wf_])
            act(wo_.ap, wf_.ap, AF.Copy, [wf_], [wo_], scale=scale)
            rs = DRes(nm)
            wres[(nm, 0)] = rs
            dma('sp', Wb[nm], wo_.ap, [wo_], [rs])
            tmpA.top = wbre_f.off
        ckpt('s5c')
        tmpA.top = wbre_f.off
        trs = {nm: DRes(nm) for nm in ('tsin', 'tcos')}
        for nm in trs:
            wres[(nm, 0)] = trs[nm]
        for half in range(2):
            av = tmpA.alloc([16, 128], F32)
            tv = tmpA.alloc([16, 128], F32)
            ov = tmpA.alloc([16, 128], F32)
            kv = tmpA.alloc([16, 128], I32)
            ob = [tmpA.alloc([16, 128], BF16) for _ in range(2)]
            Rt = [av, tv, ov, kv]
            hs = slice(half * 16, (half + 1) * 16)
            tt('dve', av.ap, C5(ANG)[:, hs].unsqueeze(2).to_broadcast([128, 16, 128]),
               tau_bc.unsqueeze(1).to_broadcast([128, 16, 128]), ALU.mult, [c5, rowc], Rt)
            for ti_, (nm, shift) in enumerate([('tsin', 0.0), ('tcos', 1.5707963267948966)]):
                sincos(ov.ap, av.ap, shift, kv.ap, tv.ap, Rt)
                for frl in (128, 16):
                    cp('dve', cFs[(nm, frl)][:, hs], ov.ap[:, :, frl - 1], [ov], [s5c])
                    if nm == 'tsin':
                        ts('dve', cFs[('ss', frl)][:, hs, 0], ov.ap[:, :, frl - 1], -1.0, None, ALU.mult, None, [ov], [s5c])
                        cp('dve', cFs[('ss', frl)][:, hs, 1], ov.ap[:, :, frl - 1], [ov], [s5c])
                cp('act', ob[ti_].ap, ov.ap, [ov], [ob[ti_]])
                dma('sp', Wb[nm][:, hs, :], ob[ti_].ap, [ob[ti_]], [trs[nm]])
            tmpA.top = wbre_f.off

        ckpt('setup')
        stgs = [Buf(sb_root, 'sb', SB_SIZE - STG + i * (STG // 2), STG // 2, F32, [STG // 8]) for i in range(2)]
        t0 = {'active': True, 'emitted': 0, 'slot': {}, 'pending': [], 'k': 0}
        LOOKAHEAD = 2

        def fpat(shape):
            names = ' '.join('d%d' % k for k in range(len(shape)))
            return 'p (%s) -> p %s' % (names, names), {'d%d' % k: shape[k] for k in range(len(shape))}

        def t0_emit(pos):
            name, idx, dst, src, fshape = units[pos]
            b = next_slot()
            parts = []
            if len(fshape) == 3:
                a_, k_, j_ = fshape
                kh = k_ if k_ * j_ <= STG // 8 else k_ // 2
                for ai in range(a_):
                    for k0 in range(0, k_, kh):
                        parts.append((src[:, ai, k0:k0 + kh, :], (ai * k_ + k0) * j_, [kh, j_]))
            else:
                n_, j_ = fshape
                rp = max(1, (STG // 8) // j_)
                for r0 in range(0, n_, rp):
                    r1 = min(n_, r0 + rp)
                    parts.append((src[:, r0:r1, :], r0 * j_, [r1 - r0, j_]))
            for (sap, off, shp) in parts:
                stg = stgs[t0['k'] % 2]
                eng = 'act' if t0['k'] % 2 == 0 else 'dve'
                t0['k'] += 1
                cnt = _prod(shp)
                pat, kw = fpat(shp)
                dma('sp', stg.ap[:, 0:cnt].rearrange(pat, **kw), sap, (), [stg])
                cp(eng, b.ap[:, off:off + cnt], stg.ap[:, 0:cnt], [stg], [b])
            t0['slot'][pos] = b
            t0['pending'].append(pos)

        def t0_flush(keep):
            while len(t0['pending']) > keep:
                pos = t0['pending'].pop(0)
                name, idx, dst, src, fshape = units[pos]
                b = t0['slot'][pos]
                pat, kw = fpat(fshape)
                dma('sp', dst, b.ap[:, 0:_prod(fshape)].rearrange(pat, **kw), [b], [wres[(name, idx)]])

        def t0_finish():
            assert t0['emitted'] == len(units), (t0['emitted'], len(units))
            t0_flush(0)
            t0['active'] = False

        def pool_load(name, idx, src_ap, view):
            if t0['active'] and (name, idx) in upos:
                pos = upos[(name, idx)]
                gend = upos[('wglu', 0)]
                lim = gend if pos < gend else len(units)
                while t0['emitted'] < min(pos + 1 + LOOKAHEAD, lim):
                    t0_emit(t0['emitted'])
                    t0['emitted'] += 1
                    t0_flush(LOOKAHEAD + 1)
                b = t0['slot'][pos]
                return b, view(b)
            if t0['active']:
                t0_flush(0)
            b = next_slot()
            dst = view(b)
            dma('sp', dst, src_ap, [wres[(name, idx)]], [b])
            return b, dst

        def norm_rstd(sqsrc, nchunks, ones_b, Tt, ssb, tmp_sq, rstd):
            for kc in range(nchunks):
                a, rr = sqsrc(kc)
                sq = tmp_sq[kc % 2]
                act(sq.ap[:, :Tt], a, AF.Square, rr, [sq])
                mm(ssb.ap[:, :Tt], ones_b.ap, sq.ap[:, :Tt], kc == 0, kc == nchunks - 1, [ones_b, sq], [ssb])
            finish_rstd(ssb, Tt, rstd)

        def finish_rstd(ssb, Tt, rstd):
            act(rstd.ap[:, :Tt], ssb.ap[:, :Tt], AF.Sqrt, [ssb], [rstd], bias=EPS)
            S.op('dve', lambda e: e.reciprocal(out=rstd.ap[:, :Tt], in_=rstd.ap[:, :Tt]), [rstd], [rstd])

        def ffn(Tt, tag, gcol, A):
            h = A.alloc([NFC, 512], BF16)
            sgt = [A.alloc([512], F32) for _ in range(2)]
            sqt = [A.alloc([512], BF16) for _ in range(2)]
            rstd = A.alloc([512], F32)
            norm_rstd(lambda kc: (xf[kc].ap[:, :Tt], [xf[kc]]), 16, onesD, Tt, pbank(7, [512]), sqt, rstd)
            for kc in range(16):
                stt('dve', xn[kc].ap[:, :Tt], xf[kc].ap[:, :Tt], gcol[:, kc:kc + 1], rstd.ap[:, :Tt],
                    ALU.mult, ALU.mult, [xf[kc], cvec, rstd], [xn[kc]])
            for c in range(NFC):
                b, w = pool_load('wgu' + tag, c, Wb['wgu' + tag][c],
                                 lambda b: b.ap.rearrange('p (g k j) -> p g k j', g=2, k=16))
                gp = pbank(2 * (c % 2), [512])
                up = pbank(2 * (c % 2) + 1, [512])
                for gi, pp in enumerate((gp, up)):
                    for ko in range(16):
                        mm(pp.ap[:, :Tt], w[:, gi, ko, :], xn[ko].ap[:, :Tt], ko == 0, ko == 15, [b, xn[ko]], [pp])
                sg = sgt[c % 2]
                act(sg.ap[:, :Tt], gp.ap[:, :Tt], AF.Silu, [gp], [sg])
                tt('dve', h[c].ap[:, :Tt], up.ap[:, :Tt], sg.ap[:, :Tt], ALU.mult, [up, sg], [h[c]])
            for fg in range(4):
                base = 4 if fg % 2 == 0 else 0
                ops_ = [pbank(base + j, [512]) for j in range(4)]
                for gi, (c0, n) in enumerate(WD_GROUPS):
                    b, w = pool_load('wd' + tag, fg * 6 + gi,
                                     Wb['wd' + tag][fg, c0:c0 + n].rearrange('c p j -> p c j'),
                                     lambda b: b.ap[:, 0:n * 512].rearrange('p (c j) -> p c j', c=n))
                    for ci in range(n):
                        c = c0 + ci
                        for j in range(4):
                            mm(ops_[j].ap[:, :Tt], w[:, ci, j * 128:(j + 1) * 128], h[c].ap[:, :Tt],
                               c == 0, c == NFC - 1, [b, h[c]], [ops_[j]])
                for j in range(4):
                    kc = 4 * fg + j
                    stt('dve', xf[kc].ap[:, :Tt], ops_[j].ap[:, :Tt], 0.5, xf[kc].ap[:, :Tt], ALU.mult, ALU.add,
                        [ops_[j], xf[kc]], [xf[kc]])

        pref = {'on': False}

        def prefetch_x(src_rows, Tt):
            nb = (Tt + 127) // 128
            for tb in range(min(2, nb)):
                n = min(128, Tt - tb * 128)
                dma('sp', stgs[tb].ap[0:n, :], src_rows[tb * 128:tb * 128 + n, :], (), [stgs[tb]])
            pref['on'] = True

        def load_tile(src_rows, Tt, A):
            nb = (Tt + 127) // 128
            pre = pref['on']
            pref['on'] = False
            stg = stgs if pre else [A.alloc([2048], F32) for _ in range(2)]
            for tb in range(nb):
                n = min(128, Tt - tb * 128)
                sg = stg[tb % 2]
                if not (pre and tb < 2):
                    dma('sp', sg.ap[0:n, :], src_rows[tb * 128:tb * 128 + n, :], (), [sg])
                for q4 in range(4):
                    pb_ = pbank((tb * 4 + q4) % 4, [4, 128])
                    for j in range(4):
                        kc = q4 * 4 + j
                        tr(pb_.ap[:, j, 0:n], sg.ap[0:n, kc * 128:(kc + 1) * 128], ident_f.ap[0:n, 0:n], [sg, ident_f], [pb_])
                    dst = xf.ap[:, q4 * 4:q4 * 4 + 4, tb * 128:tb * 128 + n]
                    cp('act' if q4 % 2 == 0 else 'dve', dst, pb_.ap[:, :, 0:n], [pb_], [xf[q4 * 4 + j] for j in range(4)])

        def store_tile(dst_rows, Tt, A):
            nb = (Tt + 127) // 128
            sqt = [A.alloc([512], BF16) for _ in range(2)]
            rstd = A.alloc([512], F32)
            stg = [A.alloc([2048], F32) for _ in range(2)]
            norm_rstd(lambda kc: (xf[kc].ap[:, :Tt], [xf[kc]]), 16, onesD, Tt, pbank(7, [512]), sqt, rstd)
            for kc in range(16):
                stt('dve', xf[kc].ap[:, :Tt], xf[kc].ap[:, :Tt], gfc[:, kc:kc + 1], rstd.ap[:, :Tt],
                    ALU.mult, ALU.mult, [xf[kc], cvec, rstd], [xf[kc]])
            for tb in range(nb):
                n = min(128, Tt - tb * 128)
                sg = stg[tb % 2]
                for q4 in range(4):
                    pb_ = pbank((tb * 4 + q4) % 4, [4, 128])
                    for j in range(4):
                        kc = q4 * 4 + j
                        tr(pb_.ap[0:n, j, :], xf[kc].ap[:, tb * 128:tb * 128 + n], ident_f.ap, [xf[kc], ident_f], [pb_])
                    cp('act' if q4 % 2 == 0 else 'dve', sg.ap[0:n, q4 * 512:(q4 + 1) * 512],
                       pb_.ap[0:n].rearrange('p a b -> p (a b)'), [pb_], [sg])
                dma('sp', dst_rows[tb * 128:tb * 128 + n, :], sg.ap[0:n, :], [sg], ())

        def mixer(Tt, segs, A):
            nseg = len(segs)
            L = segs[0][2]
            Q = min(128, L)
            Fr = min(128, L)
            sz = A.alloc([8, 512], BF16)
            u5b = A.alloc([8, 512], BF16)
            rstd1 = A.alloc([512], F32)
            xsb = A.alloc([8, 512], BF16)
            Bb = A.alloc([4, 512], BF16)
            Cb = A.alloc([4, 512], BF16)
            sub0 = A.top
            rawc = [A.alloc([nseg * (L + 3)], F32) for _ in range(2)]
            acc = [A.alloc([512], F32) for _ in range(2)]
            mix_in = xn
            sqt = [acc[0].view(BF16, [512]), acc[1].view(BF16, [512])]
            norm_rstd(lambda kc: (xf[kc].ap[:, :Tt], [xf[kc]]), 16, onesD, Tt, pbank(7, [512]), sqt, rstd1)
            for kc in range(16):
                stt('dve', xn[kc].ap[:, :Tt], xf[kc].ap[:, :Tt], gmc[:, kc:kc + 1], rstd1.ap[:, :Tt],
                    ALU.mult, ALU.mult, [xf[kc], cvec, rstd1], [xn[kc]])
            for pr in range(16):
                b, w = pool_load('win', pr, Wb['win'][2 * pr:2 * pr + 2].rearrange('c p k j -> p c k j'),
                                 lambda b: b.ap.rearrange('p (c k j) -> p c k j', c=2, k=16))
                for cc in range(2):
                    c = 2 * pr + cc
                    pp = pbank(c % 4, [512])
                    for ko in range(16):
                        mm(pp.ap[:, :Tt], w[:, cc, ko, :], xn[ko].ap[:, :Tt], ko == 0, ko == 15, [b, xn[ko]], [pp])
                    if c < 8:
                        act(sz[c].ap[:, :Tt], pp.ap[:, :Tt], AF.Silu, [pp], [sz[c]])
                    elif c < 24:
                        j = c - 8
                        rw = rawc[j % 2]
                        rw3 = rw.ap.rearrange('p (s l) -> p s l', s=nseg)
                        ac = acc[j % 2]
                        ac3 = ac.ap[:, :Tt].rearrange('p (s l) -> p s l', s=nseg)
                        for si, (sidx, c0, _) in enumerate(segs):
                            cp('pool', rw3[:, si, 0:3], st_halo[sidx].ap[:, j, :], [st_halo[sidx]], [rw])
                        cp('act', rw3[:, :, 3:3 + L], pp.ap[:, :Tt].rearrange('p (s l) -> p s l', s=nseg), [pp], [rw])
                        for si, (sidx, c0, _) in enumerate(segs):
                            cp('pool', st_halo[sidx].ap[:, j, :], rw3[:, si, L:L + 3], [rw], [st_halo[sidx]])
                        ts('dve', ac3, rw3[:, :, 3:3 + L], convw[:, j, 3:4], convb[:, j:j + 1], ALU.mult, ALU.add,
                           [rw, cvec], [ac])
                        for k in (2, 1, 0):
                            stt('dve', ac3, rw3[:, :, k:k + L], convw[:, j, k:k + 1], ac3, ALU.mult, ALU.add,
                                [rw, cvec, ac], [ac])
                        dst = xsb[j] if j < 8 else (Bb[j - 8] if j < 12 else Cb[j - 12])
                        act(dst.ap[:, :Tt], ac.ap[:, :Tt], AF.Silu, [ac], [dst])
                    else:
                        cp('dve', u5b[c - 24].ap[:, :Tt], pp.ap[:, :Tt], [pp], [u5b[c - 24]])
            ckpt('inproj')
            A.top = sub0
            dtt = A.alloc([6, 16], F32)
            V2 = [A.alloc([4, Q], F32)] * 2
            Lexp2 = [A.alloc([4, Q], F32)] * 2
            Ebc2 = [A.alloc([4, Q], F32)] * 2
            Mp2 = [A.alloc([4, Q], BF16) for _ in range(2)]
            Cexp2 = [A.alloc([4, Q], BF16) for _ in range(2)]
            CBm = A.alloc([4, Q], F32)
            xdt = A.alloc([16, 64], BF16)
            xdd = A.alloc([16, 64], BF16)
            xdd_flat = xdd.view(BF16, [1024])
            Btm = A.alloc([4, 128], BF16)
            ygf = [A.alloc([Q], F32) for _ in range(2)]
            ytmp = [A.alloc([Q], F32) for _ in range(2)]
            ysq = [A.alloc([Q], BF16) for _ in range(2)]
            ss1 = pbank(7, [512])
            for (sidx, c0, Ls) in segs:
                Sst, Sb = st_S[sidx], st_Sb[sidx]
                for ck in range(Ls // Q):
                    cs = c0 + ck * Q
                    dtp = pbank(3, [16])
                    remp = pbank(3, [16], coloff=512)
                    totp = pbank(3, [16], coloff=1024)
                    for ko in range(16):
                        mm(dtp.ap[0:Q, :], xn[ko].ap[:, cs:cs + Q], wdt_b.ap[:, ko, :], ko == 0, ko == 15, [xn[ko], wdt_b], [dtp])
                    t1, dt_, dtA, dec_end, decay_bc, e1 = [dtt[i] for i in range(6)]
                    tt('dve', t1.ap[0:Q], dtp.ap[0:Q, :], dtb_bc[0:Q], ALU.add, [dtp, rowc], [t1])
                    act(e1.ap[0:Q], t1.ap[0:Q], AF.Exp, [t1], [e1])
                    act(dt_.ap[0:Q], e1.ap[0:Q], AF.Ln, [e1], [dt_], bias=1.0)
                    tt('dve', dtA.ap[0:Q], dt_.ap[0:Q], A_bc[0:Q], ALU.mult, [dt_, rowc], [dtA])
                    mm(remp.ap[0:Q, :], SL.ap[0:Q, 0:Q], dtA.ap[0:Q], True, True, [SL, dtA], [remp])
                    mm(totp.ap[:, :], ones_f.ap[0:Q, :], dtA.ap[0:Q], True, True, [ones_f, dtA], [totp])
                    act(dec_end.ap[0:Q], remp.ap[0:Q, :], AF.Exp, [remp], [dec_end])
                    act(decay_bc.ap, totp.ap, AF.Exp, [totp], [decay_bc])
                    cbp = pbank(2, [4, 128])
                    for g in range(4):
                        mm(cbp.ap[0:Q, g, 0:Q], Bb[g].ap[:, cs:cs + Q], Cb[g].ap[:, cs:cs + Q], True, True, [Bb[g], Cb[g]], [cbp])
                    tt('dve', CBm.ap[0:Q], cbp.ap[0:Q, :, 0:Q], UL.ap[0:Q, 0:Q].unsqueeze(1).to_broadcast([Q, 4, Q]),
                       ALU.mult, [cbp, UL], [CBm])
                    xtp = ps.at(0, [8, 128], BF16)
                    xtp16 = ps.at(0, [16, 64], BF16)
                    for kc in range(8):
                        tr(xtp.ap[0:Q, kc, :], xsb[kc].ap[:, cs:cs + Q], ident_b.ap, [xsb[kc], ident_b], [xtp])
                    tt('dve', xdt.ap[0:Q], xtp16.ap[0:Q], dt_.ap[0:Q].unsqueeze(2).to_broadcast([Q, 16, 64]), ALU.mult,
                       [xtp, dt_], [xdt])
                    tt('pool', xdd.ap[0:Q], xdt.ap[0:Q], dec_end.ap[0:Q].unsqueeze(2).to_broadcast([Q, 16, 64]), ALU.mult,
                       [xdt, dec_end], [xdd])
                    btp = ps.at(2048, [4, 128], BF16)
                    for g in range(4):
                        tr(btp.ap[0:Q, g, :], Bb[g].ap[:, cs:cs + Q], ident_b.ap, [Bb[g], ident_b], [btp])
                    cp('act', Btm.ap[0:Q], btp.ap[0:Q], [btp], [Btm])
                    def ssd_a(g):
                        V, Lexp, Ebc, Mp, Cexp = V2[g % 2], Lexp2[g % 2], Ebc2[g % 2], Mp2[g % 2], Cexp2[g % 2]
                        segp = pbank(4 + (g % 2), [4, Q])
                        Rp = pbank(6 if g % 2 == 0 else 3, [4, Q])
                        tt('dve', V.ap[0:Q], UL.ap[0:Q, 0:Q].unsqueeze(1).to_broadcast([Q, 4, Q]),
                           dtA.ap[0:Q, 4 * g:4 * g + 4].unsqueeze(2).to_broadcast([Q, 4, Q]), ALU.mult, [UL, dtA], [V])
                        mm(segp.ap[0:Q], SL.ap[0:Q, 0:Q], V.ap[0:Q], True, True, [SL, V], [segp])
                        mm(Rp.ap, ones_f.ap[0:Q, :], V.ap[0:Q], True, True, [ones_f, V], [Rp])
                        act(Lexp.ap[0:Q], segp.ap[0:Q], AF.Exp, [segp], [Lexp])
                        act(Ebc.ap, Rp.ap, AF.Exp, [Rp], [Ebc])
                        tt('dve', Mp.ap[0:Q], Lexp.ap[0:Q], CBm.ap[0:Q, g:g + 1, :].to_broadcast([Q, 4, Q]), ALU.mult,
                           [Lexp, CBm], [Mp])
                        tt('pool', Cexp.ap, Ebc.ap, Cb.ap[:, g:g + 1, cs:cs + Q].to_broadcast([128, 4, Q]), ALU.mult,
                           [Ebc, Cb[g]], [Cexp])

                    def ssd_b(g):
                        Mp, Cexp = Mp2[g % 2], Cexp2[g % 2]
                        for kk in range(2):
                            kc = 2 * g + kk
                            ybase = (2048 + 1024) if g % 2 == 0 else 0
                            ypk = ps.at(ybase + kk * 512, [Q], F32)
                            for hh in range(2):
                                h_ = 2 * kc + hh
                                hq = h_ - 4 * g
                                o_ = ypk.ap[hh * 64:(hh + 1) * 64, :]
                                mm(o_, xdt.ap[0:Q, h_, :], Mp.ap[0:Q, hq, :], True, False, [xdt, Mp], [ypk])
                                mm(o_, Sb.ap[:, h_ * 64:(h_ + 1) * 64], Cexp.ap[:, hq, :], False, True, [Sb, Cexp], [ypk])
                            yt = ytmp[kc % 2]
                            yg = ygf[kc % 2]
                            sq = ysq[kc % 2]
                            stt('dve', yt.ap, xsb[kc].ap[:, cs:cs + Q], dssd[:, kc:kc + 1], ypk.ap, ALU.mult, ALU.add,
                                [xsb[kc], cvec, ypk], [yt])
                            tt('dve', yg.ap, yt.ap, sz[kc].ap[:, cs:cs + Q], ALU.mult, [yt, sz[kc]], [yg])
                            act(sq.ap, yg.ap, AF.Square, [yg], [sq])
                            mm(ss1.ap[:, cs:cs + Q], ones1k.ap, sq.ap, kc == 0, kc == 7, [ones1k, sq], [ss1])
                            act(mix_in[kc].ap[:, cs:cs + Q], yg.ap, AF.Copy, [yg, cvec], [mix_in[kc]], scale=gssd[:, kc:kc + 1])

                    ssd_a(0)
                    for g in range(4):
                        if g + 1 < 4:
                            ssd_a(g + 1)
                        ssd_b(g)
                    Sp = [pbank(0, [512]), pbank(2, [512])]
                    for g in range(4):
                        mm(Sp[g // 2].ap[:, (g % 2) * 256:(g % 2) * 256 + 256], Btm.ap[0:Q, g, :],
                           xdd_flat.ap[0:Q, g * 256:(g + 1) * 256], True, True, [Btm, xdd], [Sp[g // 2]])
                    tt('dve', Sst.ap.rearrange('p (h j) -> p h j', j=64), Sst.ap.rearrange('p (h j) -> p h j', j=64),
                       decay_bc.ap.unsqueeze(2).to_broadcast([128, 16, 64]), ALU.mult, [Sst, decay_bc], [Sst])
                    for j in range(2):
                        tt('dve', Sst.ap[:, j * 512:(j + 1) * 512], Sst.ap[:, j * 512:(j + 1) * 512], Sp[j].ap, ALU.add,
                           [Sst, Sp[j]], [Sst])
                    cp('act', Sb.ap, Sst.ap, [Sst], [Sb])
            finish_rstd(ss1, Tt, rstd1)

            ckpt('ssd')
            A.top = xsb.off
            nfr = Tt // Fr
            s5base = A.top
            sets = []
            for _ in range(4):
                d = {k: A.alloc([512], BF16) for k in ['brb', 'bib', 'p1', 'p2']}
                d['p3'], d['p4'] = d['bib'], d['brb']
                d['u1'], d['u2'], d['u3'], d['u4'] = d['p1'], d['p2'], d['brb'], d['bib']
                d['G'] = A.alloc([2, 512], F32)
                d['gre'] = d['G'][0]
                d['gim'] = d['G'][1]
                d['greb'] = d['brb']
                d['gimb'] = d['bib']
                d['tiny'] = A.alloc([8], F32)
                sets.append(d)
            glb = sz
            wsl = {}
            for nm in ['wbbre', 'wbbim', 'tcos', 'tsin', 'wcre', 'wcren', 'wcimn']:
                wsl[nm] = pool_load(nm, 0, Wb[nm], lambda b: b.ap.rearrange('p (s j) -> p s j', s=32))
            cb_, ctab_all = wsl['tcos']
            sb_, stab_all = wsl['tsin']
            cFc, sFc, ssc = cFs[('tcos', Fr)], cFs[('tsin', Fr)], cFs[('ss', Fr)]
            y5ps = {}

            def v3(bf):
                return bf.ap[:, :Tt].rearrange('p (f r) -> p f r', f=nfr)

            def tabs(sc):
                ctab = ctab_all[:, sc, 0:Fr].unsqueeze(1).to_broadcast([128, nfr, Fr])
                stab = stab_all[:, sc, 0:Fr].unsqueeze(1).to_broadcast([128, nfr, Fr])
                return ctab, stab

            def st_front(sc):
                kc = sc // 4
                d = sets[sc % 4]
                brp = pbank(0, [512])
                bip = pbank(1, [512])
                mm(brp.ap[:, :Tt], wsl['wbbre'][1][:, sc, :], u5b[kc].ap[:, :Tt], True, True, [wsl['wbbre'][0], u5b[kc]], [brp])
                mm(bip.ap[:, :Tt], wsl['wbbim'][1][:, sc, :], u5b[kc].ap[:, :Tt], True, True, [wsl['wbbim'][0], u5b[kc]], [bip])
                cp('act', d['brb'].ap[:, :Tt], brp.ap[:, :Tt], [brp], [d['brb']])
                cp('act', d['bib'].ap[:, :Tt], bip.ap[:, :Tt], [bip], [d['bib']])

            def st_mid(sc):
                d = sets[sc % 4]
                ctab, stab = tabs(sc)
                tt('dve', v3(d['p1']), v3(d['brb']), ctab, ALU.mult, [d['brb'], cb_], [d['p1']])
                tt('dve', v3(d['p2']), v3(d['bib']), stab, ALU.mult, [d['bib'], sb_], [d['p2']])
                tt('dve', v3(d['p3']), v3(d['bib']), ctab, ALU.mult, [d['bib'], cb_], [d['p3']])
                tt('dve', v3(d['p4']), v3(d['brb']), stab, ALU.mult, [d['brb'], sb_], [d['p4']])
                bprp = pbank(2 + 2 * (sc % 2), [512])
                bpip = pbank(3 + 2 * (sc % 2), [512])
                mm(bprp.ap[:, :Tt], ident_b.ap, d['p1'].ap[:, :Tt], True, False, [ident_b, d['p1']], [bprp])
                mm(bprp.ap[:, :Tt], ident_b.ap, d['p2'].ap[:, :Tt], False, True, [ident_b, d['p2']], [bprp])
                mm(bpip.ap[:, :Tt], ident_b.ap, d['p3'].ap[:, :Tt], True, False, [ident_b, d['p3']], [bpip])
                mm(bpip.ap[:, :Tt], identn_b.ap, d['p4'].ap[:, :Tt], False, True, [identn_b, d['p4']], [bpip])

            def st_scan_pair(scs):
                frames = [(sidx, c0 + f * Fr) for (sidx, c0, Ls) in segs for f in range(Ls // Fr)]
                for (sidx, a) in frames:
                    h5 = st_h5[sidx]
                    col = a + Fr - 1
                    for sc in scs:
                        d = sets[sc % 4]
                        bprp = pbank(2 + 2 * (sc % 2), [512])
                        bpip = pbank(3 + 2 * (sc % 2), [512])
                        magb = mag_c[:, sc:sc + 1].to_broadcast([128, Fr])
                        for comp, srcp, dst in ((0, bprp, 'gre'), (1, bpip, 'gim')):
                            scan(d[dst].ap[:, a:a + Fr], magb, srcp.ap[:, a:a + Fr], h5.ap[:, comp, sc:sc + 1],
                                 [s5c, srcp, h5], [d[dst]])
                    for sc in scs:
                        d = sets[sc % 4]
                        tt('dve', d['tiny'].ap[:, 0:2], d['G'].ap[:, ::-1, col], ssc[:, sc, :], ALU.mult, [d['G'], s5c], [d['tiny']])
                    for sc in scs:
                        d = sets[sc % 4]
                        stt('dve', h5.ap[:, :, sc], d['G'].ap[:, :, col], cFc[:, sc:sc + 1], d['tiny'].ap[:, 0:2], ALU.mult, ALU.add,
                            [d['G'], s5c, d['tiny']], [h5])
                for sc in scs:
                    d = sets[sc % 4]
                    cp('act', d['greb'].ap[:, :Tt], d['gre'].ap[:, :Tt], [d['gre']], [d['greb']])
                    cp('act', d['gimb'].ap[:, :Tt], d['gim'].ap[:, :Tt], [d['gim']], [d['gimb']])

            def st_out(sc):
                kc = sc // 4
                d = sets[sc % 4]
                ctab, stab = tabs(sc)
                tt('dve', v3(d['u1']), v3(d['greb']), ctab, ALU.mult, [d['greb'], cb_], [d['u1']])
                tt('dve', v3(d['u2']), v3(d['gimb']), stab, ALU.mult, [d['gimb'], sb_], [d['u2']])
                tt('dve', v3(d['u3']), v3(d['greb']), stab, ALU.mult, [d['greb'], sb_], [d['u3']])
                tt('dve', v3(d['u4']), v3(d['gimb']), ctab, ALU.mult, [d['gimb'], cb_], [d['u4']])
                if sc % 4 == 0:
                    y5ps[kc] = pbank(6 + (kc % 2), [512])
                y5p = y5ps[kc]
                for wi, (wn, un) in enumerate([('wcre', 'u1'), ('wcren', 'u2'), ('wcimn', 'u3'), ('wcimn', 'u4')]):
                    mm(y5p.ap[:, :Tt], wsl[wn][1][:, sc, :], d[un].ap[:, :Tt], sc % 4 == 0 and wi == 0, False,
                       [wsl[wn][0], d[un]], [y5p])
                if sc % 4 == 3:
                    mm(y5p.ap[:, :Tt], Ddiag5[kc].ap, u5b[kc].ap[:, :Tt], False, True, [Ddiag5[kc], u5b[kc]], [y5p])
                    act(glb[kc].ap[:, :Tt], y5p.ap[:, :Tt], AF.Gelu, [y5p], [glb[kc]])

            st_front(0)
            st_front(1)
            st_mid(0)
            st_mid(1)
            for j in range(16):
                a_, b_ = 2 * j, 2 * j + 1
                if j + 1 < 16:
                    st_front(a_ + 2)
                    st_front(b_ + 2)
                st_scan_pair((a_, b_))
                if j + 1 < 16:
                    st_mid(a_ + 2)
                    st_mid(b_ + 2)
                st_out(a_)
                st_out(b_)
            A.top = s5base
            rstd2 = A.alloc([512], F32)
            gtmp = [{'sig': A.alloc([512], F32), 'yg': A.alloc([512], F32), 'sq': A.alloc([512], BF16),
                     'tA': A.alloc([512], F32), 'tB': A.alloc([512], F32)} for _ in range(2)]
            ckpt('s5')
            wg = [pool_load('wglu', hh, Wb['wglu'][hh], lambda b: b.ap.rearrange('p (k j) -> p k j', k=4)) for hh in range(2)]
            ss2 = pbank(5, [512])
            for fc in range(8):
                gp_ = pbank(fc % 4, [512])
                for kc in range(8):
                    b, w = wg[kc // 4]
                    mm(gp_.ap[:, :Tt], w[:, kc % 4, fc * 128:(fc + 1) * 128], glb[kc].ap[:, :Tt], kc == 0, kc == 7, [b, glb[kc]], [gp_])
                sig = gtmp[fc % 2]['sig']
                yg = gtmp[fc % 2]['yg']
                sq = gtmp[fc % 2]['sq']
                act(sig.ap[:, :Tt], gp_.ap[:, :Tt], AF.Sigmoid, [gp_, cvec], [sig], bias=bglu[:, fc:fc + 1])
                tt('dve', yg.ap[:, :Tt], glb[fc].ap[:, :Tt], sig.ap[:, :Tt], ALU.mult, [glb[fc], sig], [yg])
                act(sq.ap[:, :Tt], yg.ap[:, :Tt], AF.Square, [yg], [sq])
                mm(ss2.ap[:, :Tt], ones1k.ap, sq.ap[:, :Tt], fc == 0, fc == 7, [ones1k, sq], [ss2])
                ts('pool', mix_in[8 + fc].ap[:, :Tt], yg.ap[:, :Tt], gs5[:, fc:fc + 1], None, ALU.mult, None, [yg, cvec], [mix_in[8 + fc]])
            finish_rstd(ss2, Tt, rstd2)
            for pr in range(8):
                b, w = pool_load('wout', pr, Wb['wout'][2 * pr:2 * pr + 2].rearrange('c p k j -> p c k j'),
                                 lambda b: b.ap.rearrange('p (c k j) -> p c k j', c=2, k=16))
                for cc in range(2):
                    fc = 2 * pr + cc
                    m1 = pbank(2 * (fc % 2), [512])
                    m2 = pbank(2 * (fc % 2) + 1, [512])
                    for ko in range(8):
                        mm(m1.ap[:, :Tt], w[:, cc, ko, :], mix_in[ko].ap[:, :Tt], ko == 0, ko == 7, [b, mix_in[ko]], [m1])
                    for ko in range(8, 16):
                        mm(m2.ap[:, :Tt], w[:, cc, ko, :], mix_in[ko].ap[:, :Tt], ko == 8, ko == 15, [b, mix_in[ko]], [m2])
                    tA = gtmp[fc % 2]['tA']
                    tB = gtmp[fc % 2]['tB']
                    tt('dve', tA.ap[:, :Tt], m1.ap[:, :Tt], rstd1.ap[:, :Tt], ALU.mult, [m1, rstd1], [tA])
                    tt('dve', tB.ap[:, :Tt], m2.ap[:, :Tt], rstd2.ap[:, :Tt], ALU.mult, [m2, rstd2], [tB])
                    tt('pool', tA.ap[:, :Tt], tA.ap[:, :Tt], tB.ap[:, :Tt], ALU.add, [tA, tB], [tA])
                    tt('pool', xf[fc].ap[:, :Tt], xf[fc].ap[:, :Tt], tA.ap[:, :Tt], ALU.add, [xf[fc], tA], [xf[fc]])

        def init_state_zero(sidx):
            memset('pool', st_halo[sidx].ap, 0.0, [st_halo[sidx]])
            memset('pool', st_S[sidx].ap, 0.0, [st_S[sidx]])
            memset('pool', st_Sb[sidx].ap, 0.0, [st_Sb[sidx]])
            memset('pool', st_h5[sidx].ap, 0.0, [st_h5[sidx]])

        def init_state_from(sidx, b, A):
            t0 = A.top
            cst = A.alloc([3, 128], F32)
            dma('sp', cst.ap[0:16], I['cconv'][b].rearrange('k (c p) -> c k p', p=128), (), [cst])
            pb_ = pbank(0, [3, 16])
            for k in range(3):
                tr(pb_.ap[:, k, :], cst.ap[0:16, k, :], ident_f.ap[0:16, 0:16], [cst, ident_f], [pb_])
            cp('dve', st_halo[sidx].ap.rearrange('p c k -> p k c'), pb_.ap, [pb_], [st_halo[sidx]])
            sst = A.alloc([8, 128], F32)
            dma('sp', sst.ap, I['sssd'][b].rearrange('(c p) n -> p c n', p=128), (), [sst])
            for half in range(2):
                pb2 = pbank(1 + half, [4, 128])
                for j in range(4):
                    tr(pb2.ap[:, j, :], sst.ap[:, half * 4 + j, :], ident_f.ap, [sst, ident_f], [pb2])
                cp('dve', st_S[sidx].ap[:, half * 512:(half + 1) * 512], pb2.ap.rearrange('p a b -> p (a b)'), [pb2], [st_S[sidx]])
            cp('act', st_Sb[sidx].ap, st_S[sidx].ap, [st_S[sidx]], [st_Sb[sidx]])
            h5s = A.alloc([2, 128], F32)
            dma('sp', h5s.ap[0:32, 0, :], I['s5re'][b], (), [h5s])
            dma('sp', h5s.ap[0:32, 1, :], I['s5im'][b], (), [h5s])
            pb3 = pbank(3, [2, 32])
            for k in range(2):
                tr(pb3.ap[:, k, :], h5s.ap[0:32, k, :], ident_f.ap[0:32, 0:32], [h5s, ident_f], [pb3])
            cp('dve', st_h5[sidx].ap, pb3.ap, [pb3], [st_h5[sidx]])
            A.top = t0

        def out_state(sidx, b, okeys, A):
            t0 = A.top
            oc, os_, ore, oim = okeys
            pb_ = pbank(0, [3, 128])
            for k in range(3):
                tr(pb_.ap[0:16, k, :], st_halo[sidx].ap[:, :, k], ident_f.ap, [st_halo[sidx], ident_f], [pb_])
            cst = A.alloc([3, 128], F32)
            cp('dve', cst.ap[0:16], pb_.ap[0:16], [pb_], [cst])
            dma('sp', O[oc][b].rearrange('k (c p) -> c k p', p=128), cst.ap[0:16], [cst], ())
            sst = A.alloc([8, 128], F32)
            for half in range(2):
                pb2 = pbank(1 + half, [4, 128])
                for j in range(4):
                    kc = half * 4 + j
                    tr(pb2.ap[:, j, :], st_S[sidx].ap[:, kc * 128:(kc + 1) * 128], ident_f.ap, [st_S[sidx], ident_f], [pb2])
                cp('dve', sst.ap[:, half * 4:half * 4 + 4, :], pb2.ap, [pb2], [sst])
            dma('sp', O[os_][b].rearrange('(c p) n -> p c n', p=128), sst.ap, [sst], ())
            pb3 = pbank(3, [2, 128])
            for k in range(2):
                tr(pb3.ap[0:32, k, :], st_h5[sidx].ap[:, k, :], ident_f.ap, [st_h5[sidx], ident_f], [pb3])
            h5s = A.alloc([2, 128], F32)
            cp('dve', h5s.ap[0:32], pb3.ap[0:32], [pb3], [h5s])
            dma('sp', O[ore][b], h5s.ap[0:32, 0, :], [h5s], ())
            dma('sp', O[oim][b], h5s.ap[0:32, 1, :], [h5s], ())
            A.top = t0

        A = Arena(sb_root, 'sb', SB_SIZE - STG)

        def run_tile(src_rows, dst_rows, Tt, segs, nxt=None):
            A.top = arena0
            load_tile(src_rows, Tt, A)
            ckpt('load')
            A.top = arena0
            ffn(Tt, '1', g1c, A)
            ckpt('ffn1')
            A.top = arena0
            mixer(Tt, segs, A)
            ckpt('mixer')
            A.top = arena0
            if nxt is not None and not t0['active']:
                prefetch_x(*nxt)
            ffn(Tt, '2', g2c, A)
            if t0['active']:
                t0_finish()
            ckpt('ffn2')
            A.top = arena0
            store_tile(dst_rows, Tt, A)

        for b in range(NSEQ):
            init_state_zero(0)
            for ti in range(LP // 512):
                r0 = b * LP + ti * 512
                r1 = r0 + 512
                if r1 < NSEQ * LP:
                    nxt = (I['xp'][r1:r1 + 512], 512)
                elif NS:
                    nxt = (I['xs'], NS * LS)
                else:
                    nxt = None
                run_tile(I['xp'][r0:r0 + 512], O['yp'][r0:r0 + 512], 512, [(0, 0, 512)], nxt)
            A.top = arena0
            out_state(0, b, ('convp', 'ssdp', 's5rep', 's5imp'), A)
        if NS:
            A.top = arena0
            for s_ in range(NS):
                init_state_from(s_, s_, A)
            run_tile(I['xs'], O['ys'], NS * LS, [(s_, s_ * LS, LS) for s_ in range(NS)])
            A.top = arena0
            for s_ in range(NS):
                out_state(s_, s_, ('convs', 'ssds', 's5res', 's5ims'), A)

    import contextlib
    with contextlib.ExitStack() as es:
        sb_root = es.enter_context(nc.sbuf_tensor("arena", [128, SB_SIZE // 4], F32))
        ps_root = es.enter_context(nc.psum_tensor("psum", [128, 4096], F32))
        esem = {e: es.enter_context(nc.semaphore("e_" + e)) for e in ENGS}
        dsem = {(q, i): es.enter_context(nc.semaphore("d_%s%d" % (q, i))) for q in ('sp', 'pool') for i in range(NDS)}
        try:
            program(sb_root, ps_root)
        except _Stop:
            pass
        block = es.enter_context(nc.Block())
        S.emit(nc, block, esem, dsem)
    ctx['nops'] = {e: len(S.ops[e]) for e in ENGS}
    return nc, ctx


def prep_weights(p):
    f = np.float32
    W = {}

    def gu(wg, wu):
        g = wg.reshape(16, 128, 43, 128).transpose(2, 1, 0, 3)
        u = wu.reshape(16, 128, 43, 128).transpose(2, 1, 0, 3)
        return np.ascontiguousarray(np.stack([g, u], axis=2))

    def dn(wd):
        return np.ascontiguousarray(wd.reshape(43, 128, 4, 512).transpose(2, 0, 1, 3))

    def col(v, n):
        return np.ascontiguousarray(np.asarray(v, f).reshape(n, 128).T)
    W['wgu1'] = gu(p['w_ffn1_gate'][0], p['w_ffn1_up'][0])
    W['wd1'] = dn(p['w_ffn1_down'][0])
    W['wgu2'] = gu(p['w_ffn2_gate'][0], p['w_ffn2_up'][0])
    W['wd2'] = dn(p['w_ffn2_down'][0])
    win = p['w_in'][0]
    main = np.concatenate([win[:, 0:3072], win[:, 3088:4112]], axis=1)
    W['win'] = np.ascontiguousarray(main.reshape(16, 128, 32, 128).transpose(2, 1, 0, 3))
    W['wdt'] = np.ascontiguousarray(win[:, 3072:3088].reshape(16, 128, 16).transpose(1, 0, 2))
    W['wout'] = np.ascontiguousarray(p['w_out'][0].reshape(16, 128, 16, 128).transpose(2, 1, 0, 3))
    W['wglu'] = np.ascontiguousarray(p['w_glu'][0].reshape(2, 4, 128, 1024).transpose(0, 2, 1, 3))
    W['g1c'] = col(p['norm_ffn1'][0], 16)
    W['gmc'] = col(p['norm_mix'][0], 16)
    W['g2c'] = col(p['norm_ffn2'][0], 16)
    W['gfc'] = col(p['norm_final'], 16)
    W['gssd'] = col(p['norm_ssd'][0], 8)
    W['gs5'] = col(p['norm_s5'][0], 8)
    W['convw'] = np.ascontiguousarray(p['conv_w'][0].reshape(4, 16, 128).transpose(2, 1, 0))
    W['convb'] = col(p['conv_b'][0], 16)
    W['dssd'] = col(np.repeat(p['d_ssd'][0], 64), 8)
    W['bglu'] = col(p['b_glu'][0], 8)
    W['d5'] = col(p['s5_d'][0].reshape(-1), 8)
    W['dtb'] = np.ascontiguousarray(p['dt_bias'][0].reshape(1, 16))
    W['alog'] = np.ascontiguousarray(p['a_log'][0].reshape(1, 16))
    W['tau'] = np.arange(1, 129, dtype=f).reshape(1, 128)
    W['lrc'] = col(p['s5_lambda_re'][0].reshape(-1), 32)
    W['lic'] = col(p['s5_lambda_im'][0].reshape(-1), 32)
    W['lsc'] = col(np.repeat(p['s5_log_step'][0], 64), 32)
    bre, bim = p['s5_b_re'][0], p['s5_b_im'][0]
    cre, cim = p['s5_c_re'][0], p['s5_c_im'][0]
    wbre = np.zeros((128, 32, 128), f)
    wbim = np.zeros((128, 32, 128), f)
    wcre = np.zeros((128, 32, 128), f)
    wcim = np.zeros((128, 32, 128), f)
    for sc in range(32):
        for gl in range(2):
            g = 2 * sc + gl
            r0 = (sc % 4) * 32 + gl * 16
            wbre[r0:r0 + 16, sc, gl * 64:(gl + 1) * 64] = bre[g].T
            wbim[r0:r0 + 16, sc, gl * 64:(gl + 1) * 64] = bim[g].T
            wcre[gl * 64:(gl + 1) * 64, sc, r0:r0 + 16] = cre[g].T
            wcim[gl * 64:(gl + 1) * 64, sc, r0:r0 + 16] = cim[g].T
    W['wbre'], W['wbim'], W['wcre'], W['wcim'] = wbre, wbim, wcre, wcim
    for k in W:
        W[k] = np.ascontiguousarray(W[k], dtype=f)
        assert list(W[k].shape) == INPUT_SHAPES[k], (k, W[k].shape)
    return W


_CACHE = {}


def kernel(**inputs):
    p = {k: np.asarray(v) for k, v in inputs.items()}
    NCORES = 8
    cfg = {'nseq': 2, 'lp': 2048, 'ns': 2}
    W = prep_weights(p)
    nc, _ = build(cfg)
    in_maps = []
    for c in range(NCORES):
        m = dict(W)
        m['xp'] = np.ascontiguousarray(p['x_prompt'][2 * c:2 * c + 2].reshape(4096, 2048))
        m['xs'] = np.ascontiguousarray(p['x_sample'][2 * c:2 * c + 2].reshape(32, 2048))
        m['cconv'] = np.ascontiguousarray(p['cache_conv'][0, 2 * c:2 * c + 2])
        m['sssd'] = np.ascontiguousarray(p['state_ssd'][0, 2 * c:2 * c + 2].reshape(2, 1024, 128))
        m['s5re'] = np.ascontiguousarray(p['state_s5_re'][0, 2 * c:2 * c + 2].reshape(2, 32, 128))
        m['s5im'] = np.ascontiguousarray(p['state_s5_im'][0, 2 * c:2 * c + 2].reshape(2, 32, 128))
        in_maps.append(m)
    res = run_bass_kernel_spmd(nc, in_maps, core_ids=list(range(NCORES)))
    R = res.results

    def cat(key, shape):
        return np.concatenate([np.asarray(R[c][key], np.float32).reshape(shape) for c in range(NCORES)], axis=0)
    y_prompt = cat('yp', (2, 2048, 2048))
    y_sample = cat('ys', (2, 16, 2048))
    conv_p = cat('convp', (2, 3, 2048))[None]
    ssd_p = cat('ssdp', (2, 16, 64, 128))[None]
    s5re_p = cat('s5rep', (2, 64, 64))[None]
    s5im_p = cat('s5imp', (2, 64, 64))[None]
    conv_s = cat('convs', (2, 3, 2048))[None]
    ssd_s = cat('ssds', (2, 16, 64, 128))[None]
    s5re_s = cat('s5res', (2, 64, 64))[None]
    s5im_s = cat('s5ims', (2, 64, 64))[None]
    return (y_prompt, y_sample, conv_p, ssd_p, s5re_p, s5im_p, conv_s, ssd_s, s5re_s, s5im_s)
```
